# Optimizing a Trainium2 kernel written in Bass

```python
import math
import jax, jax.numpy as jnp
from jax import lax
import numpy as np

D_MODEL = 1024
BATCH = 8
SEQ = 2048
DEPTH = 2
DEC_BATCH = 128
DEC_SEQ = 4
PAST_LEN = 16384
PAGE_SIZE = 128

N_META = 16
MIX_WIDTH = D_MODEL // 2
GLA_HEADS = 4
GLA_DV = MIX_WIDTH // GLA_HEADS
GLA_DK = GLA_DV // 2
GLA_GATE_RANK = 16
GLA_GATE_NORM = 16.0
GLA_CHUNK = 16
SSM_HEAD_DIM = 64
SSM_HEADS = MIX_WIDTH // SSM_HEAD_DIM
SSM_GROUPS = 2
SSM_STATE = 64
SSM_CONV = 4
SSM_CONV_DIM = MIX_WIDTH + 2 * SSM_GROUPS * SSM_STATE
SSM_CHUNK = 128
RET_HEADS = 4
RET_DV = MIX_WIDTH // RET_HEADS
RET_DK = RET_DV // 2
RET_CHUNK = 128
ROPE_BASE = 10000.0
D_FF = 4 * D_MODEL
ALPHA = (2 * DEPTH) ** 0.25
BETA = (8 * DEPTH) ** -0.25
N_BRANCH = 3
SPLIT_SIZES = (GLA_HEADS * GLA_DK, GLA_HEADS * GLA_DK, MIX_WIDTH, MIX_WIDTH, GLA_GATE_RANK,
               MIX_WIDTH, SSM_CONV_DIM, SSM_HEADS,
               RET_HEADS * RET_DK, RET_HEADS * RET_DK, MIX_WIDTH, MIX_WIDTH,
               N_BRANCH * D_MODEL)
D_IN_PROJ = sum(SPLIT_SIZES)

kernel_name = 'hybrid_gla_ssd_retention_decoder_step'


def layer_norm(x, w, b, eps=1e-5):
    xf = x.astype(jnp.float32)
    mu = jnp.mean(xf, axis=-1, keepdims=True)
    var = jnp.mean(jnp.square(xf - mu), axis=-1, keepdims=True)
    return ((xf - mu) * lax.rsqrt(var + eps) * w + b).astype(x.dtype)


def rms_norm(x, w=None, eps=1e-6):
    xf = x.astype(jnp.float32)
    y = xf * lax.rsqrt(jnp.mean(xf * xf, axis=-1, keepdims=True) + eps)
    if w is not None:
        y = y * w
    return y.astype(x.dtype)


def rotary(x, pos):
    half = x.shape[-1] // 2
    inv_freq = ROPE_BASE ** (-jnp.arange(half, dtype=jnp.float32) / half)
    ang = pos.astype(jnp.float32)[:, None] * inv_freq[None, :]
    cos = jnp.cos(ang)[None, :, None, :]
    sin = jnp.sin(ang)[None, :, None, :]
    xf = x.astype(jnp.float32)
    x1, x2 = xf[..., :half], xf[..., half:]
    return jnp.concatenate([x1 * cos - x2 * sin, x1 * sin + x2 * cos], axis=-1).astype(x.dtype)


def gated_linear_recurrence(q, k, v, g, s0, chunk):
    f32 = jnp.float32
    b, t, h, _ = q.shape
    dv = v.shape[-1]
    n = t // chunk
    scalar_decay = g.shape[-1] == 1
    mask = jnp.tril(jnp.ones((chunk, chunk), dtype=bool))[None, :, :, None, None]

    def blocks(a):
        return a.astype(f32).reshape(b, n, chunk, h, a.shape[-1]).swapaxes(0, 1)

    def step(s, inp):
        qc, kc, vc, gc = inp
        G = jnp.cumsum(gc, axis=1)
        G_end = G[:, -1]
        diff = G[:, :, None] - G[:, None, :]
        decay = jnp.exp(jnp.where(mask, diff, -jnp.inf))
        if scalar_decay:
            att = jnp.einsum('bthk,bshk->btsh', qc, kc) * decay[..., 0]
        else:
            att = jnp.einsum('bthk,bshk,btshk->btsh', qc, kc, decay)
        o = (jnp.einsum('btsh,bshv->bthv', att, vc)
             + jnp.einsum('bthk,bhkv->bthv', qc * jnp.exp(G), s))
        s_new = (jnp.exp(G_end)[..., None] * s
                 + jnp.einsum('bshk,bshv->bhkv', kc * jnp.exp(G_end[:, None] - G), vc))
        return s_new, o

    s_fin, o = lax.scan(step, s0.astype(f32), (blocks(q), blocks(k), blocks(v), blocks(g)))
    o = o.swapaxes(0, 1).reshape(b, t, h, dv)
    return o.astype(v.dtype), s_fin.astype(s0.dtype)


def run_recurrence(q, k, v, g, s0, chunk, prompt):
    if not prompt:
        return gated_linear_recurrence(q, k, v, g, s0, q.shape[1])
    o_meta, s = gated_linear_recurrence(q[:, :N_META], k[:, :N_META], v[:, :N_META],
                                        g[:, :N_META], s0, N_META)
    o_body, s = gated_linear_recurrence(q[:, N_META:], k[:, N_META:], v[:, N_META:],
                                        g[:, N_META:], s, chunk)
    return jnp.concatenate([o_meta, o_body], axis=1), s


def token_mixers(u, pos, state, wl, prompt):
    b, t, _ = u.shape
    f32 = jnp.float32
    dtype = u.dtype
    proj = jnp.einsum('btd,de->bte', u, wl['w_in'])
    offs = np.cumsum(SPLIT_SIZES)[:-1].tolist()
    (gla_q, gla_k, gla_v, gla_r, gla_a, ssm_z, ssm_xbc, ssm_dt,
     ret_q, ret_k, ret_v, ret_g, gate_logits) = jnp.split(proj, offs, axis=-1)
    if prompt:
        s_gla = jnp.zeros((b, GLA_HEADS, GLA_DK, GLA_DV), dtype)
        s_ssm = jnp.zeros((b, SSM_HEADS, SSM_STATE, SSM_HEAD_DIM), dtype)
        conv_buf = jnp.zeros((b, SSM_CONV - 1, SSM_CONV_DIM), dtype)
        s_ret = jnp.zeros((b, RET_HEADS, RET_DK, RET_DV), dtype)
    else:
        s_gla, s_ssm, conv_buf, s_ret = state

    q = gla_q.reshape(b, t, GLA_HEADS, GLA_DK) * (GLA_DK ** -0.5)
    k = gla_k.reshape(b, t, GLA_HEADS, GLA_DK)
    v = gla_v.reshape(b, t, GLA_HEADS, GLA_DV)
    a = jnp.einsum('btr,re->bte', gla_a, wl['w_gla_a2']) + wl['b_gla_a']
    g = (jax.nn.log_sigmoid(a.astype(f32)) / GLA_GATE_NORM).reshape(b, t, GLA_HEADS, GLA_DK)
    o, s_gla_new = run_recurrence(q, k, v, g, s_gla, GLA_CHUNK, prompt)
    y_gla = jax.nn.silu(gla_r) * rms_norm(o, wl['w_gla_norm']).reshape(b, t, MIX_WIDTH)

    xbc_in = jnp.concatenate([conv_buf.astype(dtype), ssm_xbc], axis=1)
    conv_new = xbc_in[:, -(SSM_CONV - 1):]
    cw = wl['conv_w']
    xbc = sum(cw[i] * xbc_in[:, i:i + t] for i in range(SSM_CONV)) + wl['conv_b']
    xbc = jax.nn.silu(xbc)
    xs, bm, cm = jnp.split(xbc, [MIX_WIDTH, MIX_WIDTH + SSM_GROUPS * SSM_STATE], axis=-1)
    dt = jax.nn.softplus(ssm_dt.astype(f32) + wl['dt_bias'])
    a_neg = -jnp.exp(wl['a_log'].astype(f32))
    xh = xs.reshape(b, t, SSM_HEADS, SSM_HEAD_DIM)
    rep = SSM_HEADS // SSM_GROUPS
    bh = jnp.repeat(bm.reshape(b, t, SSM_GROUPS, SSM_STATE), rep, axis=2)
    ch = jnp.repeat(cm.reshape(b, t, SSM_GROUPS, SSM_STATE), rep, axis=2)
    v_ssm = (xh.astype(f32) * dt[..., None]).astype(dtype)
    o, s_ssm_new = run_recurrence(ch, bh, v_ssm, (dt * a_neg)[..., None], s_ssm, SSM_CHUNK, prompt)
    y = o + wl['d_skip'][:, None] * xh
    y = (y.reshape(b, t, MIX_WIDTH) * jax.nn.silu(ssm_z)).reshape(b, t, SSM_GROUPS, MIX_WIDTH // SSM_GROUPS)
    y_ssm = rms_norm(y).reshape(b, t, MIX_WIDTH) * wl['w_ssm_norm']

    q = rotary(ret_q.reshape(b, t, RET_HEADS, RET_DK), pos)
    k = rotary(ret_k.reshape(b, t, RET_HEADS, RET_DK), pos) * (RET_DK ** -0.5)
    v = ret_v.reshape(b, t, RET_HEADS, RET_DV)
    log_gamma = jnp.log1p(-jnp.exp2(-5.0 - jnp.arange(RET_HEADS, dtype=f32)))
    g = jnp.broadcast_to(log_gamma[:, None], (b, t, RET_HEADS, 1))
    o, s_ret_new = run_recurrence(q, k, v, g, s_ret, RET_CHUNK, prompt)
    y_ret = jax.nn.silu(ret_g) * rms_norm(o).reshape(b, t, MIX_WIDTH)

    gates = jax.nn.sigmoid(gate_logits.reshape(b, t, N_BRANCH, D_MODEL))
    merged = (gates[..., 0, :] * jnp.einsum('btm,md->btd', y_gla, wl['w_gla_out'])
              + gates[..., 1, :] * jnp.einsum('btm,md->btd', y_ssm, wl['w_ssm_out'])
              + gates[..., 2, :] * jnp.einsum('btm,md->btd', y_ret, wl['w_ret_out']))
    out = jnp.einsum('btd,de->bte', merged, wl['w_o'])
    return out, (s_gla_new, s_ssm_new, conv_new, s_ret_new)


def run_trunk(x, pos, states, ln_in_w, ln_in_b, layer_w, prompt):
    x = layer_norm(x, ln_in_w, ln_in_b)
    collected = ([], [], [], [])
    for l in range(DEPTH):
        wl = {name: arr[l] for name, arr in layer_w.items()}
        st = None if prompt else tuple(s[l] for s in states)
        mix, new_st = token_mixers(x, pos, st, wl, prompt)
        h = layer_norm(ALPHA * x + mix, wl['ln1_w'], wl['ln1_b'])
        hid = jnp.square(jax.nn.relu(jnp.einsum('btd,df->btf', h, wl['w_ff1']) + wl['b_ff1']))
        ff = jnp.einsum('btf,fd->btd', hid, wl['w_ff2']) + wl['b_ff2']
        x = layer_norm(ALPHA * h + ff, wl['ln2_w'], wl['ln2_b'])
        for store, s in zip(collected, new_st):
            store.append(s)
    return x, tuple(jnp.stack(store) for store in collected)


def setup_inputs(seed: int = 0) -> dict:
    key = jax.random.key(seed)
    ks = jax.random.split(key, 32)
    nrm = jax.random.normal
    f32 = jnp.float32
    dt0 = jnp.exp(jax.random.uniform(ks[15], (DEPTH, SSM_HEADS), minval=math.log(1e-3), maxval=math.log(1e-1)))
    return {
        'x_prompt': nrm(ks[0], (BATCH, SEQ, D_MODEL), f32),
        'x_sample': nrm(ks[1], (DEC_BATCH, DEC_SEQ, D_MODEL), f32),
        'state_gla': nrm(ks[2], (DEPTH, DEC_BATCH, GLA_HEADS, GLA_DK, GLA_DV), f32),
        'state_ssm': 0.1 * nrm(ks[3], (DEPTH, DEC_BATCH, SSM_HEADS, SSM_STATE, SSM_HEAD_DIM), f32),
        'state_conv': nrm(ks[4], (DEPTH, DEC_BATCH, SSM_CONV - 1, SSM_CONV_DIM), f32),
        'state_ret': nrm(ks[5], (DEPTH, DEC_BATCH, RET_HEADS, RET_DK, RET_DV), f32),
        'meta_tokens': nrm(ks[6], (N_META, D_MODEL), f32),
        'ln_in_w': 1.0 + 0.02 * nrm(ks[7], (D_MODEL,), f32),
        'ln_in_b': 0.02 * nrm(ks[8], (D_MODEL,), f32),
        'w_in': nrm(ks[9], (DEPTH, D_MODEL, D_IN_PROJ), f32) * D_MODEL ** -0.5,
        'w_gla_a2': nrm(ks[10], (DEPTH, GLA_GATE_RANK, GLA_HEADS * GLA_DK), f32) * GLA_GATE_RANK ** -0.5,
        'b_gla_a': 0.02 * nrm(ks[11], (DEPTH, GLA_HEADS * GLA_DK), f32),
        'w_gla_norm': 1.0 + 0.02 * nrm(ks[12], (DEPTH, GLA_DV), f32),
        'conv_w': nrm(ks[13], (DEPTH, SSM_CONV, SSM_CONV_DIM), f32) * SSM_CONV ** -0.5,
        'conv_b': 0.02 * nrm(ks[14], (DEPTH, SSM_CONV_DIM), f32),
        'dt_bias': dt0 + jnp.log(-jnp.expm1(-dt0)),
        'a_log': jnp.log(jax.random.uniform(ks[16], (DEPTH, SSM_HEADS), minval=1.0, maxval=16.0)),
        'd_skip': 1.0 + 0.02 * nrm(ks[17], (DEPTH, SSM_HEADS), f32),
        'w_ssm_norm': 1.0 + 0.02 * nrm(ks[18], (DEPTH, MIX_WIDTH), f32),
        'w_gla_out': nrm(ks[19], (DEPTH, MIX_WIDTH, D_MODEL), f32) * (BETA * MIX_WIDTH ** -0.5),
        'w_ssm_out': nrm(ks[20], (DEPTH, MIX_WIDTH, D_MODEL), f32) * (BETA * MIX_WIDTH ** -0.5),
        'w_ret_out': nrm(ks[21], (DEPTH, MIX_WIDTH, D_MODEL), f32) * (BETA * MIX_WIDTH ** -0.5),
        'w_o': nrm(ks[22], (DEPTH, D_MODEL, D_MODEL), f32) * (BETA * D_MODEL ** -0.5),
        'ln1_w': 1.0 + 0.02 * nrm(ks[23], (DEPTH, D_MODEL), f32),
        'ln1_b': 0.02 * nrm(ks[24], (DEPTH, D_MODEL), f32),
        'w_ff1': nrm(ks[25], (DEPTH, D_MODEL, D_FF), f32) * (BETA * D_MODEL ** -0.5),
        'b_ff1': 0.02 * nrm(ks[26], (DEPTH, D_FF), f32),
        'w_ff2': nrm(ks[27], (DEPTH, D_FF, D_MODEL), f32) * (BETA * D_FF ** -0.5),
        'b_ff2': 0.02 * nrm(ks[28], (DEPTH, D_MODEL), f32),
        'ln2_w': 1.0 + 0.02 * nrm(ks[29], (DEPTH, D_MODEL), f32),
        'ln2_b': 0.02 * nrm(ks[30], (DEPTH, D_MODEL), f32),
    }


def reference(x_prompt, x_sample, state_gla, state_ssm, state_conv, state_ret, meta_tokens,
              ln_in_w, ln_in_b, w_in, w_gla_a2, b_gla_a, w_gla_norm, conv_w, conv_b, dt_bias,
              a_log, d_skip, w_ssm_norm, w_gla_out, w_ssm_out, w_ret_out, w_o, ln1_w, ln1_b,
              w_ff1, b_ff1, w_ff2, b_ff2, ln2_w, ln2_b):
    layer_w = {
        'w_in': w_in, 'w_gla_a2': w_gla_a2, 'b_gla_a': b_gla_a, 'w_gla_norm': w_gla_norm,
        'conv_w': conv_w, 'conv_b': conv_b, 'dt_bias': dt_bias, 'a_log': a_log, 'd_skip': d_skip,
        'w_ssm_norm': w_ssm_norm, 'w_gla_out': w_gla_out, 'w_ssm_out': w_ssm_out,
        'w_ret_out': w_ret_out, 'w_o': w_o, 'ln1_w': ln1_w, 'ln1_b': ln1_b,
        'w_ff1': w_ff1, 'b_ff1': b_ff1, 'w_ff2': w_ff2, 'b_ff2': b_ff2,
        'ln2_w': ln2_w, 'ln2_b': ln2_b,
    }
    bp = x_prompt.shape[0]
    meta = jnp.broadcast_to(meta_tokens[None].astype(x_prompt.dtype), (bp, N_META, D_MODEL))
    xp = jnp.concatenate([meta, x_prompt], axis=1)
    pos_p = jnp.arange(xp.shape[1], dtype=jnp.int32)
    yp, (state_gla_prompt, state_ssm_prompt, state_conv_prompt, state_ret_prompt) = run_trunk(
        xp, pos_p, None, ln_in_w, ln_in_b, layer_w, True)
    y_prompt = yp[:, N_META:]
    pos_s = PAST_LEN + jnp.arange(x_sample.shape[1], dtype=jnp.int32)
    y_sample, (state_gla_sample, state_ssm_sample, state_conv_sample, state_ret_sample) = run_trunk(
        x_sample, pos_s, (state_gla, state_ssm, state_conv, state_ret), ln_in_w, ln_in_b, layer_w, False)
    return (y_prompt, y_sample, state_gla_prompt, state_gla_sample, state_ssm_prompt, state_ssm_sample,
            state_conv_prompt, state_conv_sample, state_ret_prompt, state_ret_sample)
```

```python
import numpy as np
import concourse.bass as bass
import concourse.mybir as mybir
from concourse.bass_utils import run_bass_kernel_spmd

F32 = mybir.dt.float32
BF16 = mybir.dt.bfloat16
AF = mybir.ActivationFunctionType
ALU = mybir.AluOpType
AX = mybir.AxisListType

D = 1024
SEQ = 2048
NMETA = 16
NSEQ = 16
DSEQ = 4
PAST = 16384
L = 2
ALPHA = float((2 * L) ** 0.25)
NCORES = 8
STOP = None
DBG_CORES = None

GROUPS = []
_g0 = [("M", 16, 0, 0)]
for i in range(2):
    _g0.append(("B", 128, 16 + 128 * i, 16 + 128 * i))
_g0.append(("S", 64, 272, PAST))
GROUPS.append(_g0)
_b = 2
for n in (3, 3, 4, 4):
    g = []
    for i in range(n):
        g.append(("B", 128, 128 * i, 16 + 128 * (_b + i)))
    _b += n
    GROUPS.append(g)
NG = len(GROUPS)


def gcols(g):
    return sum(b[1] for b in g)


WT = [("gqk", 8, 512), ("ga", 8, 16), ("gv", 8, 512), ("gr", 8, 512),
      ("sx0", 8, 512), ("sx1", 8, 256), ("sz", 8, 512), ("sdt", 8, 8),
      ("rqk", 8, 512), ("rsw", 8, 512), ("rv", 8, 512), ("rg", 8, 512)]
for q in range(2):
    for br in range(3):
        WT.append(("mg%d%d" % (q, br), 8, 512))
        WT.append(("mo%d%d" % (q, br), 4, 512))
WT += [("wo0", 8, 512), ("wo1", 8, 512)]
for i in range(8):
    WT.append(("f1%d" % i, 8, 512))
for half in range(2):
    for fg in range(4):
        WT.append(("f2%d%d" % (half, fg), 8, 512))
WOFF = {}
_o = 0
for nm, kc, ncl in WT:
    WOFF[nm] = (_o, kc, ncl)
    _o += kc * ncl
WTOT = _o
NSLOT = 3


def _perm_half():
    p = np.arange(256).reshape(4, 2, 32)[:, ::-1, :].reshape(256)
    return p


def build_wstream(w_in, w_gla_out, w_ssm_out, w_ret_out, w_o, w_ff1, w_ff2):
    out = np.empty((L, 128, WTOT), np.float32)
    offs = np.cumsum([0, 256, 256, 512, 512, 16, 512, 768, 8, 256, 256, 512, 512, 3072])
    o = {n: offs[i] for i, n in enumerate(["gq", "gk", "gv", "gr", "ga", "sz", "sx", "sdt", "rq", "rk", "rv", "rg", "gate"])}
    ph = _perm_half()
    for l in range(L):
        wi = w_in[l]
        def put(nm, mat):
            off, kc, ncl = WOFF[nm]
            assert mat.shape == (kc * 128, ncl), (nm, mat.shape)
            out[l, :, off:off + kc * ncl] = mat.reshape(kc, 128, ncl).transpose(1, 0, 2).reshape(128, kc * ncl)
        put("gqk", wi[:, o["gq"]:o["gq"] + 512])
        put("gv", wi[:, o["gv"]:o["gv"] + 512])
        put("gr", wi[:, o["gr"]:o["gr"] + 512])
        put("ga", wi[:, o["ga"]:o["ga"] + 16])
        put("sz", wi[:, o["sz"]:o["sz"] + 512])
        put("sdt", wi[:, o["sdt"]:o["sdt"] + 8])
        put("sx0", wi[:, o["sx"]:o["sx"] + 512])
        put("sx1", wi[:, o["sx"] + 512:o["sx"] + 768])
        put("rqk", wi[:, o["rq"]:o["rq"] + 512])
        rq = wi[:, o["rq"]:o["rq"] + 256][:, ph]
        rk = wi[:, o["rk"]:o["rk"] + 256][:, ph]
        put("rsw", np.concatenate([rq, rk], axis=1))
        put("rv", wi[:, o["rv"]:o["rv"] + 512])
        put("rg", wi[:, o["rg"]:o["rg"] + 512])
        outs = [w_gla_out[l], w_ssm_out[l], w_ret_out[l]]
        for q in range(2):
            for br in range(3):
                put("mg%d%d" % (q, br), wi[:, o["gate"] + br * 1024 + q * 512:o["gate"] + br * 1024 + q * 512 + 512])
                put("mo%d%d" % (q, br), outs[br][:, q * 512:(q + 1) * 512])
        put("wo0", w_o[l][:, 0:512])
        put("wo1", w_o[l][:, 512:1024])
        for i in range(8):
            put("f1%d" % i, w_ff1[l][:, i * 512:(i + 1) * 512])
        for half in range(2):
            for fg in range(4):
                put("f2%d%d" % (half, fg), w_ff2[l][fg * 1024:(fg + 1) * 1024, half * 512:(half + 1) * 512])
    return out


def build_consts():
    c = {}
    i = np.arange(128)
    ca = (i[:, None] <= i[None, :]).astype(np.float32)
    sl = (i[:, None] > i[None, :]).astype(np.float32)
    same = ((i[:, None] // 4) == (i[None, :] // 4)).astype(np.float32)
    c["ca_b"] = ca
    c["sl_b"] = sl
    c["ca_s"] = ca * same
    c["sl_s"] = sl * same
    c["ident"] = np.eye(128, dtype=np.float32)
    c["ng_b"] = (-30000.0 * (1.0 - ca)).astype(np.float32)
    c["ng_s"] = (-30000.0 * (1.0 - ca * same)).astype(np.float32)
    sm = np.zeros((128, 16), np.float32)
    for s in range(64):
        sm[s, s // 4] = 1.0
    c["smtok"] = sm
    smT = np.zeros((128, 16, 64), np.float32)
    for t in range(64):
        smT[:, t // 4, t] = 1.0
    c["smT"] = smT
    lg = np.log1p(-np.exp2(-5.0 - np.arange(4, dtype=np.float64)))
    invf = (10000.0 ** (-(np.arange(32, dtype=np.float32) / np.float32(32.0)))).astype(np.float32)
    tabs = []
    for g in GROUPS:
        tab = np.zeros((2, 64, 2, 4, 512), np.float32)
        for (kind, nt, c0, pos0) in g:
            for t in range(nt):
                if kind == "S":
                    pos = PAST + (t % 4)
                    idx = (t % 4) + 1
                else:
                    pos = pos0 + t
                    idx = t + 1
                ang = (np.float32(pos) * invf).astype(np.float32).astype(np.float64)
                cfull = np.concatenate([np.cos(ang), np.cos(ang)])
                sfull = np.concatenate([-np.sin(ang), np.sin(ang)])
                for h in range(4):
                    eg = np.exp(lg[h] * idx)
                    tab[0, :, 0, h, c0 + t] = cfull * eg
                    tab[0, :, 1, h, c0 + t] = sfull * eg
                    tab[1, :, 0, h, c0 + t] = cfull / eg * 0.125
                    tab[1, :, 1, h, c0 + t] = sfull / eg * 0.125
        tabs.append(tab)
    c["rtab"] = np.stack(tabs)
    eg = np.zeros((64, 3, 4), np.float32)
    for ki, n in enumerate((128, 16, 4)):
        for h in range(4):
            eg[:, ki, h] = np.exp(lg[h] * n)
    c["regend"] = eg
    return c


CONST_NAMES = ["ca_b", "sl_b", "ca_s", "sl_s", "ident", "smtok", "smT", "rtab", "regend", "ng_b", "ng_s"]


class Tok:
    __slots__ = ("w", "r")

    def __init__(self):
        self.w = None
        self.r = []


class Prog:
    ENG = ("pe", "act", "dve", "pool", "sp")

    def __init__(self, nc):
        self.nc = nc
        self.q = {e: [] for e in self.ENG}
        self.cnt = {e: 0 for e in self.ENG}
        self.sems = {}
        self.semvals = {}
        self.waited = {e: {} for e in self.ENG}
        self.pending_out = []
        self._stack = []
        self.phase = "init"
        self.phases = {e: [] for e in self.ENG}

    def sem(self, key):
        if key not in self.sems:
            cm = self.nc.semaphore("s%d" % len(self.sems))
            s = cm.__enter__()
            self._stack.append(cm)
            self.sems[key] = s
            self.semvals[key] = 0
        return self.sems[key]

    def _deps(self, eng, reads, writes):
        ev = {}

        def add(e):
            if e is None:
                return
            if ev.get(e[0], 0) < e[1]:
                ev[e[0]] = e[1]
        for t in reads:
            add(t.w)
        for t in writes:
            add(t.w)
            for r in t.r:
                if r[0] == eng:
                    continue
                add(r)
        out = []
        wd = self.waited[eng]
        for k, v in ev.items():
            if k == "pe" and eng == "pe":
                continue
            if wd.get(k, 0) >= v:
                continue
            wd[k] = v
            out.append((k, v))
        return out

    def op(self, eng, fn, r=(), w=()):
        waits = self._deps(eng, r, w)
        self.cnt[eng] += 1
        seq = self.cnt[eng]
        s_self = self.sem(eng)
        wl = [(self.sem(k), v) for k, v in waits]

        def emit(e, wl=wl, fn=fn, s_self=s_self):
            for s, v in wl:
                e.wait_ge(s, v)
            fn(e).then_inc(s_self, 1)
        self.q[eng].append(emit)
        self.phases[eng].append(self.phase)
        evt = (eng, seq)
        for t in r:
            t.r.append(evt)
        for t in w:
            t.w = evt
            t.r = []
        return evt

    def dma(self, eng, out, in_, r=(), w=(), key=None, final=False, **kw):
        waits = self._deps(eng, r, w)
        semkey = ("dma", key)
        s = self.sem(semkey)
        self.semvals[semkey] += 16
        val = self.semvals[semkey]
        wl = [(self.sem(k), v) for k, v in waits]

        def emit(e, wl=wl, s=s):
            for ss, v in wl:
                e.wait_ge(ss, v)
            e.dma_start(out=out, in_=in_, **kw).then_inc(s, 16)
        self.q[eng].append(emit)
        evt = (semkey, val)
        for t in r:
            t.r.append(evt)
        for t in w:
            t.w = evt
            t.r = []
        if final:
            self.pending_out.append(evt)
        return evt

    def finish(self):
        ev = {}
        for k, v in self.pending_out:
            ev[k] = max(ev.get(k, 0), v)
        wl = [(self.sem(k), v) for k, v in ev.items()]

        def emit(e):
            for s, v in wl:
                e.wait_ge(s, v)
        self.q["sp"].append(emit)
        nc = self.nc
        with nc.Block() as block:
            @block.tensor
            def _(e):
                for f in self.q["pe"]:
                    f(e)

            @block.scalar
            def _(e):
                for f in self.q["act"]:
                    f(e)

            @block.vector
            def _(e):
                for f in self.q["dve"]:
                    f(e)

            @block.gpsimd
            def _(e):
                for f in self.q["pool"]:
                    f(e)

            @block.sync
            def _(e):
                for f in self.q["sp"]:
                    f(e)
        for cm in reversed(self._stack):
            cm.__exit__(None, None, None)


def build_program():
    nc = bass.Bass("TRN2", target_bir_lowering=False)
    P = Prog(nc)
    cms = []

    def din(name, shape, dt=F32):
        return nc.dram_tensor(name, list(shape), dt, kind="ExternalInput").ap()

    def dout(name, shape):
        return nc.dram_tensor(name, list(shape), F32, kind="ExternalOutput").ap()

    def sb(name, shape, dt=F32):
        cm = nc.sbuf_tensor(name, list(shape), dt)
        t = cm.__enter__()
        cms.append(cm)
        return t

    xp = din("xp", [SEQ, D])
    xs = din("xs", [NSEQ * DSEQ, D])
    meta = din("meta", [NMETA, D])
    wst = din("wst", [L, 128, WTOT])
    lnbc = din("lnbc", [5, 2, 128, D])
    lncol = din("lncol", [128, 5, 2, 8])
    small = din("small", [128, L, 80])
    rowp = din("rowp", [1, L * 1280])
    wa2 = din("wa2", [16, L, 256])
    b1col = din("b1col", [128, L, 32])
    sg_in = din("sg_in", [L, 2, 64, NSEQ, 2, 128])
    sr_in = din("sr_in", [L, 2, 64, NSEQ, 2, 128])
    ss_in = din("ss_in", [L, 2, 64, NSEQ, 4, 64])
    sc_in = din("sc_in", [L, 128, 8, NSEQ, 3])
    cst = {}
    cshape = {"ng_b": [128, 128], "ng_s": [128, 128], "ca_b": [128, 128], "sl_b": [128, 128], "ca_s": [128, 128], "sl_s": [128, 128],
              "ident": [128, 128], "smtok": [128, 16], "smT": [128, 16, 64],
              "rtab": [NG, 2, 64, 2, 4, 512], "regend": [64, 3, 4]}
    for n in CONST_NAMES:
        cst[n] = din("c_" + n, cshape[n])
    yp = dout("yp", [SEQ, D])
    ys = dout("ys", [NSEQ * DSEQ, D])
    sgp = dout("sgp", [L, 64, 4, 128])
    srp = dout("srp", [L, 64, 4, 128])
    ssp = dout("ssp", [L, 64, 8, 64])
    scp = dout("scp", [L, 3, 768])
    sgs = dout("sgs", [L, 2, 64, NSEQ, 2, 128])
    srs = dout("srs", [L, 2, 64, NSEQ, 2, 128])
    sss = dout("sss", [L, 2, 64, NSEQ, 4, 64])
    scs = dout("scs", [L, 3, NSEQ, 768])

    pf = []
    for i in range(6):
        cm = nc.psum_tensor("pf%d" % i, [128, 512], F32)
        pf.append((cm.__enter__(), Tok()))
        cms.append(cm)
    pb = []
    for i in range(2):
        cm = nc.psum_tensor("pb%d" % i, [128, 1024], BF16)
        pb.append((cm.__enter__(), Tok()))
        cms.append(cm)
    rr = {"f": 0, "b": 0}

    def PF():
        rr["f"] = (rr["f"] + 1) % 4
        return pf[rr["f"]]

    def PL(i):
        return pf[4 + i]

    def PB():
        rr["b"] = (rr["b"] + 1) % 2
        return pb[rr["b"]]

    ca_b = sb("ca_b", [128, 128]); sl_b = sb("sl_b", [128, 128])
    ca_s = sb("ca_s", [128, 128]); sl_s = sb("sl_s", [128, 128])
    ng_b = sb("ng_b", [128, 128]); ng_s = sb("ng_s", [128, 128])
    identb = sb("identb", [128, 128], BF16)
    smtok = sb("smtok", [128, 16]); smT = sb("smT", [128, 16, 64], BF16)
    regend = sb("regend", [64, 3, 4])
    ones_f = sb("ones_f", [128, 128]); ones_b = sb("ones_b", [128, 128], BF16)
    lncol_t = sb("lncol_t", [128, 5, 2, 8])
    small_t = sb("small_t", [128, L, 80])
    rowhl = sb("rowhl", [33, L, 1280], BF16)
    wa2_t = sb("wa2_t", [16, L, 256], BF16)
    b1col_t = sb("b1col_t", [128, L, 32])
    aneg = sb("aneg", [128, L, 8])
    dmat = sb("dmat", [128, 8, 128], BF16); t_dmat = Tok()
    tc_ = Tok()
    wslot = [(sb("wslot%d" % i, [128, 4096], BF16), Tok()) for i in range(NSLOT)]
    xT = sb("xT", [128, 8, 512], BF16); t_xT = Tok()
    xtok = sb("xtok", [128, 4, D]); t_xtok = [Tok() for _ in range(4)]
    yT = sb("yT", [128, 12, 512], BF16); t_yT = [Tok() for _ in range(3)]
    mT = sb("mT", [128, 8, 512], BF16); t_mT = Tok()
    big = sb("big", [128, 16384], BF16)
    hid = big[:].rearrange("p (f t) -> p f t", f=32); t_hid = [Tok() for _ in range(8)]
    xraw = big[:, 0:8240].bitcast(F32).rearrange("p (c t) -> p c t", c=8); t_xraw = Tok(); t_xraw8 = [Tok() for _ in range(8)]
    cacc = big[:, 8240:10288].bitcast(F32).rearrange("p (c t) -> p c t", c=2); t_cacc2 = [Tok(), Tok()]
    qTm = big[0:64, 10288:12336].rearrange("p (a j t) -> p a j t", a=2, j=16); t_qTm = Tok()
    cTm = big[0:64, 12336:13360].rearrange("p (j t) -> p j t", j=16); t_cTm = Tok()
    xsin = big[:, 13360:15152].bitcast(F32).rearrange("p (c j t) -> p c j t", c=8, j=16); t_xsin = Tok()
    lnrt = sb("lnrt", [128, 4096])
    lnw_bc = lnrt[:, 0:1024]; lnb_bc = lnrt[:, 1024:2048]; t_lnbc = Tok()
    rtab = lnrt[:].rearrange("p (a b c) -> p a b c", a=4, b=2); t_rtab = t_lnbc
    scrA = sb("scrA", [128, 2048]); t_qk = Tok()
    _sab = scrA[0:64, :].bitcast(BF16)
    qTf = _sab[:, 0:2048].rearrange("p (a t) -> p a t", a=4); kTf = _sab[:, 2048:4096].rearrange("p (a t) -> p a t", a=4)
    lhd = scrA[:, 0:1024].rearrange("p (h t) -> p h t", h=8); dec = scrA[:, 1024:2048].rearrange("p (h t) -> p h t", h=8)
    t_lhd = Tok(); t_dec = Tok()
    qTb = sb("qTb", [64, 4, 512], BF16); kTb = sb("kTb", [64, 4, 512], BF16); t_qkb = Tok()
    aTb = sb("aTb", [16, 512], BF16); t_aT = Tok()
    eG_a = sb("eG", [64, 4, 128]); eGn_a = sb("eGn", [64, 4, 128]); t_eG_a = Tok(); t_eG = t_eG_a
    ktok = sb("ktok", [128, 256], BF16); t_ktok = Tok()
    vtok2 = sb("vtok2", [128, 2, 512], BF16); t_vtok2 = [Tok(), Tok()]
    srt2 = sb("srt2", [128, 2, 512]); t_srt2 = [Tok(), Tok()]
    srt = srt2[:, 0, :]; t_srt = t_srt2[0]
    egp_all = sb("egp_all", [64, 4, 4]); egs_all = sb("egs_all", [64, 16, 4]); t_egp = Tok()
    attb = sb("attb", [128, 8, 128], BF16); t_att = Tok()
    ysb = sb("ysb", [128, 512]); t_ysb = Tok()
    ybf = sb("ybf", [128, 512], BF16); t_ybf = Tok()
    ssq = sb("ssq", [128, 8]); t_ssq = Tok()
    junkA = sb("junkA", [128, 512]); junkB = sb("junkB", [128, 512]); t_junkA = Tok(); t_junkB = Tok()
    junk = junkA; t_junk = t_junkA
    xcT = sb("xcT", [128, 8, 512], BF16); t_xcT = Tok()
    xstok = sb("xstok", [128, 512], BF16); btok = sb("btok", [128, 128], BF16); t_xstok = Tok()
    dtt_all = sb("dtt_all", [128, 4, 8]); gss_all = sb("gss_all", [128, 4, 8]); t_dt4 = [Tok() for _ in range(4)]
    eGt_all = sb("eGt_all", [128, 4, 8]); wsd_all = sb("wsd_all", [128, 4, 8]); eGe_all = sb("eGe_all", [128, 4, 8]); lndt_all = sb("lndt_all", [128, 4, 8])
    gm = sb("gm", [128, 16, 8]); t_gm = Tok()
    t_eGt4 = [Tok() for _ in range(4)]
    eGe = sb("eGe", [128, 16, 8]); t_eGe = Tok()
    m2 = sb("m2", [128, 8, 128]); t_m2 = Tok()
    _m2f = m2[0:64].rearrange("p h t -> p (h t)")
    eG_b = _m2f[:, 0:512].rearrange("p (h t) -> p h t", h=4); eGn_b = _m2f[:, 512:1024].rearrange("p (h t) -> p h t", h=4)
    xw = sb("xw", [128, 512], BF16); t_xw = Tok()
    hist = sb("hist", [128, L, 8, 3]); t_hist = [Tok() for _ in range(L)]
    sc3 = sb("sc3", [16, 768]); t_sc3 = Tok()
    Sg = sb("Sg", [64, L, 4, 128]); Sgb = sb("Sgb", [64, L, 4, 128], BF16)
    Sr = sb("Sr", [64, L, 4, 128]); Srb = sb("Srb", [64, L, 4, 128], BF16)
    Ss = sb("Ss", [64, L, 8, 64]); Ssb = sb("Ssb", [64, L, 8, 64], BF16)
    t_Sg = [Tok() for _ in range(L)]; t_Sr = [Tok() for _ in range(L)]; t_Ss = [Tok() for _ in range(L)]
    S0 = lnrt; t_S0 = t_lnbc
    S0b = mT[:].rearrange("p a b -> p (a b)"); t_S0b = t_mT
    ktm = S0b[0:64, 0:2048].rearrange("p (j n) -> p j n", j=16); t_ktm = t_S0b
    btm = S0b[0:64, 0:1024].rearrange("p (j n) -> p j n", j=16); t_btm = t_S0b

    def mm(out, lhsT, rhs, start, stop, r, w):
        P.op("pe", lambda e: e.matmul(out, lhsT, rhs, start=start, stop=stop), r=r, w=w)

    def tr(out, in_, r, w):
        n = in_.shape[0]
        P.op("pe", lambda e: e.transpose(out, in_, identb[:n, :n]), r=list(r) + [tc_], w=w)

    def act(out, in_, func, r, w, bias=None, scale=1.0, accum=None):
        kw = {}
        if bias is not None:
            kw["bias"] = bias
        if accum is not None:
            kw["accum_out"] = accum
        P.op("act", lambda e: e.activation(out=out, in_=in_, func=func, scale=scale, **kw), r=r, w=w)

    def tt(eng, out, in0, in1, op, r, w):
        P.op(eng, lambda e: e.tensor_tensor(out=out, in0=in0, in1=in1, op=op), r=r, w=w)

    def ts(eng, out, in0, s1, s2, op0, op1, r, w):
        if op1 is None:
            P.op(eng, lambda e: e.tensor_scalar(out=out, in0=in0, scalar1=s1, scalar2=None, op0=op0), r=r, w=w)
        else:
            P.op(eng, lambda e: e.tensor_scalar(out=out, in0=in0, scalar1=s1, scalar2=s2, op0=op0, op1=op1), r=r, w=w)

    def stt(out, in0, sc, in1, op0, op1, r, w):
        P.op("dve", lambda e: e.scalar_tensor_tensor(out=out, in0=in0, scalar=sc, in1=in1, op0=op0, op1=op1), r=r, w=w)

    def cp(eng, out, in_, r, w):
        if eng == "act":
            P.op("act", lambda e: e.copy(out=out, in_=in_), r=r, w=w)
        else:
            P.op(eng, lambda e: e.tensor_copy(out=out, in_=in_), r=r, w=w)

    def bc(ap, shape):
        return ap.to_broadcast(list(shape))

    for nm, t in (("ca_b", ca_b), ("sl_b", sl_b), ("ca_s", ca_s), ("sl_s", sl_s), ("smtok", smtok),
                  ("ng_b", ng_b), ("ng_s", ng_s),
                  ("regend", regend)):
        P.dma("sp", t[:], cst[nm], w=[tc_], key="const")
    P.dma("pool", identb[:], cst["ident"], w=[tc_], key="constp")
    P.dma("pool", smT[:], cst["smT"], w=[tc_], key="constp")
    P.dma("sp", lncol_t[:], lncol, w=[tc_], key="const")
    P.dma("sp", small_t[:], small, w=[tc_], key="const")
    P.dma("pool", wa2_t[:], wa2, w=[tc_], key="constp")
    P.dma("sp", b1col_t[:], b1col, w=[tc_], key="const")
    P.op("dve", lambda e: e.memset(ones_f[:], 1.0), w=[tc_])
    P.op("dve", lambda e: e.memset(big[:], 0.0), w=t_hid + [t_xraw, t_xsin, t_qTm, t_cTm] + t_cacc2)
    P.op("dve", lambda e: e.memset(ones_b[:], 1.0), w=[tc_])
    P.op("dve", lambda e: e.memset(hist[:], 0.0), w=t_hist)
    for l in range(L):
        P.op("dve", lambda e, l=l: e.memset(Sg[:, l], 0.0), w=[t_Sg[l]])
        P.op("dve", lambda e, l=l: e.memset(Sgb[:, l], 0.0), w=[t_Sg[l]])
        P.op("dve", lambda e, l=l: e.memset(Sr[:, l], 0.0), w=[t_Sr[l]])
        P.op("dve", lambda e, l=l: e.memset(Srb[:, l], 0.0), w=[t_Sr[l]])
        P.op("dve", lambda e, l=l: e.memset(Ss[:, l], 0.0), w=[t_Ss[l]])
        P.op("dve", lambda e, l=l: e.memset(Ssb[:, l], 0.0), w=[t_Ss[l]])
    NR = L * 1280
    rhl = rowhl[:].rearrange("p l n -> p (l n)")
    P.dma("sp", S0[0:1, 0:NR], rowp, w=[t_S0], key="S0")
    P.dma("sp", S0[32:33, 0:NR], rowp, w=[t_S0], key="S0")
    P.op("dve", lambda e: e.memset(rowhl[:], 0.0), w=[tc_])
    cp("dve", rhl[0:1, :], S0[0:1, 0:NR], [t_S0], [tc_])
    cp("dve", rhl[32:33, :], S0[32:33, 0:NR], [t_S0], [tc_])
    tt("dve", S0[32:33, 0:NR], S0[32:33, 0:NR], rhl[32:33, :], ALU.subtract, [t_S0, tc_], [t_S0])
    cp("dve", rhl[32:33, :], S0[32:33, 0:NR], [t_S0], [tc_])
    for l in range(L):
        act(aneg[:, l, :], small_t[:, l, 0:8], AF.Exp, [tc_], [tc_])
        ts("dve", aneg[:, l, :], aneg[:, l, :], -1.0, None, ALU.mult, None, [tc_], [tc_])
    identf = sb("identf", [128, 128])
    P.dma("sp", identf[:], cst["ident"], w=[tc_], key="const")

    wseq = [(l_, nm) for _gi in range(NG) for l_ in range(L) for (nm, _k, _n) in WT]
    wstate = {"issued": 0, "cur": 0}

    wbf = nc.dram_tensor("wbf", [L, 128, WTOT], BF16, kind="Internal").ap()
    t_wbf = {}
    NT_PER = len(WT)

    def _issue(idx):
        l_, nm = wseq[idx]
        off, kc, ncl = WOFF[nm]
        t, tk = wslot[idx % NSLOT]
        n = kc * ncl
        gi_ = idx // (L * NT_PER)
        if gi_ == 0:
            P.dma("pool", t[:, 0:n], wst[l_, :, off:off + n], w=[tk], key=("w", idx % NSLOT), max_dma_last_dim=4096)
            tkd = Tok()
            t_wbf[(l_, nm)] = tkd
            P.dma("sp", wbf[l_, :, off:off + n], t[:, 0:n], r=[tk], w=[tkd], key=("wbw", idx % NSLOT))
        else:
            P.dma("sp", t[:, 0:n], wbf[l_, :, off:off + n], r=[t_wbf[(l_, nm)]], w=[tk], key=("wr", idx % NSLOT))

    def wload(l, nm, live=0):
        idx = wstate["cur"]
        assert wseq[idx] == (l, nm), (wseq[idx], l, nm)
        while wstate["issued"] < min(len(wseq), idx - live + NSLOT):
            _issue(wstate["issued"])
            wstate["issued"] += 1
        assert wstate["issued"] > idx
        wstate["cur"] += 1
        off, kc, ncl = WOFF[nm]
        t, tk = wslot[idx % NSLOT]
        return t[:, 0:kc * ncl].rearrange("p (k n) -> p k n", k=kc), tk

    xhb4 = big[:, 0:4096].rearrange("p (b d) -> p b d", b=4)
    t_xhb4 = [t_hid[0], t_hid[1], t_xraw]
    lnst4 = sb("lnst4", [128, 4, 2, 6]); lnmv4 = sb("lnmv4", [128, 4, 4]); t_ln4 = [Tok() for _ in range(4)]

    def layernorm(g, lni):
        P.dma("sp", lnw_bc, lnbc[lni, 0], w=[t_lnbc], key="lnbc")
        P.dma("sp", lnb_bc, lnbc[lni, 1], w=[t_lnbc], key="lnbc")
        blks = list(enumerate(g))
        for bi, (kind, nt, c0, pos0) in blks:
            z = xtok[:nt, bi, :]
            for hh in range(2):
                P.op("dve", lambda e, hh=hh, z=z, nt=nt, bi=bi: e.bn_stats(out=lnst4[:nt, bi, hh, :], in_=z[:, hh * 512:(hh + 1) * 512]),
                     r=[t_xtok[bi]], w=[t_ln4[bi]])
            P.op("dve", lambda e, nt=nt, bi=bi: e.bn_aggr(out=lnmv4[:nt, bi, 0:2], in_=lnst4[:nt, bi].rearrange("p a b -> p (a b)")),
                 r=[t_ln4[bi]], w=[t_ln4[bi]])
        for bi, (kind, nt, c0, pos0) in blks:
            act(lnmv4[:nt, bi, 2:3], lnmv4[:nt, bi, 1:2], AF.Ln, [t_ln4[bi]], [t_ln4[bi]], bias=1e-5)
            act(lnmv4[:nt, bi, 3:4], lnmv4[:nt, bi, 2:3], AF.Exp, [t_ln4[bi]], [t_ln4[bi]], scale=-0.5)
        for bi, (kind, nt, c0, pos0) in blks:
            ts("dve", lnmv4[:nt, bi, 2:3], lnmv4[:nt, bi, 0:1], lnmv4[:nt, bi, 3:4], -1.0, ALU.mult, ALU.mult,
               [t_ln4[bi]], [t_ln4[bi]])
        for bi, (kind, nt, c0, pos0) in blks:
            z = xtok[:nt, bi, :]
            act(xhb4[:nt, bi, :], z, AF.Identity, [t_xtok[bi], t_ln4[bi]], t_xhb4,
                bias=lnmv4[:nt, bi, 2:3], scale=lnmv4[:nt, bi, 3:4])
            act(z, z, AF.Identity, [t_xtok[bi], t_ln4[bi]], [t_xtok[bi]],
                bias=lnmv4[:nt, bi, 2:3], scale=lnmv4[:nt, bi, 3:4])
        for bi, (kind, nt, c0, pos0) in blks:
            z = xtok[:nt, bi, :]
            tt("dve", z, z, lnw_bc[:nt, :], ALU.mult, [t_xtok[bi], t_lnbc], [t_xtok[bi]])
            tt("dve", z, z, lnb_bc[:nt, :], ALU.add, [t_xtok[bi], t_lnbc], [t_xtok[bi]])
        for bi, (kind, nt, c0, pos0) in blks:
            pt, tp = PB()
            for k in range(8):
                tr(pt[:, k * 128:k * 128 + nt], xhb4[:nt, bi, k * 128:(k + 1) * 128], t_xhb4, [tp])
            for k in range(8):
                act(xT[:, k, c0:c0 + nt], pt[:, k * 128:k * 128 + nt], AF.Identity, [tp, tc_], [t_xT],
                    bias=lncol_t[:, lni, 1, k:k + 1], scale=lncol_t[:, lni, 0, k:k + 1])

    def proj_feat(W, tw, wc0, M, src, tsrc, nk, ncols, koff=0):
        pt, tp = PF()
        for k in range(nk):
            mm(pt[:M, :ncols], W[:, k, wc0:wc0 + M], src[:, koff + k, 0:ncols], k == 0, k == nk - 1, [tw, tsrc], [tp])
        return pt, tp

    def proj_tok(W, tw, wc0, N, c0, nt):
        pt, tp = PF()
        for k in range(8):
            mm(pt[:nt, :N], xT[:, k, c0:c0 + nt], W[:, k, wc0:wc0 + N], k == 0, k == 7, [tw, t_xT], [tp])
        return pt, tp

    KIND = {"B": 0, "M": 1, "S": 2}

    def lin_block(l, kind, nt, c0, Sm, Smb, tS, egp, egs, ybase, ycol_scale, st_in, st_out_prompt, st_out_sample,
                  last_prompt_block, vtok, t_vtok, srt, t_srt, hook=None):
        ca = ca_s if kind == "S" else ca_b
        pt, tp = PB()
        for h in range(4):
            tr(pt[:nt, h * 64:(h + 1) * 64], kTb[:, h, c0:c0 + nt], [t_qkb], [tp])
        cp("act", ktok[:nt, :], pt[:nt, 0:256], [tp], [t_ktok])
        chk("lb_ktok")
        pa, tpa = PF()
        for h in range(4):
            mm(pa[:nt, h * 128:h * 128 + nt], kTb[:, h, c0:c0 + nt], qTb[:, h, c0:c0 + nt], True, True, [t_qkb], [tpa])
        chk("lb_attmm")
        pav = pa[:].rearrange("p (h t) -> p h t", h=4)[:nt, :, :nt]
        tt("dve", attb[:nt, 0:4, :nt], pav, bc(ca[:nt, :nt].unsqueeze(1), [nt, 4, nt]), ALU.mult, [tpa, tc_], [t_att])
        chk("lb_att")
        po, tpo = PL(0)
        S0v = S0[0:64, :].rearrange("p (j a v) -> p j a v", j=16, a=2)
        S0bv = S0b[0:64, :].rearrange("p (j a v) -> p j a v", j=16, a=2)
        if kind != "S":
            for h in range(4):
                mm(po[:nt, h * 128:(h + 1) * 128], attb[:nt, h, :nt], vtok[:nt, h * 128:(h + 1) * 128], True, False,
                   [t_att, t_vtok], [tpo])
                mm(po[:nt, h * 128:(h + 1) * 128], qTb[:, h, c0:c0 + nt], Smb[:, l, h, :], False, True, [t_qkb, tS], [tpo])
        else:
            for p in range(2):
                P.dma("sp", S0[0:64, :], st_in[l, p].rearrange("p j a v -> p (j a v)"), w=[t_S0], key="S0")
                P.dma("pool", S0b[0:64, :], st_in[l, p].rearrange("p j a v -> p (j a v)"), w=[t_S0b], key="S0b",
                      max_dma_last_dim=4096)
                tt("dve", qTm[:], bc(qTb[:, 2 * p:2 * p + 2, c0:c0 + 64].unsqueeze(2), [64, 2, 16, 64]),
                   bc(smT[0:64].unsqueeze(1), [64, 2, 16, 64]), ALU.mult, [t_qkb, tc_], [t_qTm])
                for h2 in range(2):
                    h = 2 * p + h2
                    mm(po[:nt, h * 128:(h + 1) * 128], attb[:nt, h, :nt], vtok[:nt, h * 128:(h + 1) * 128], True, False,
                       [t_att, t_vtok], [tpo])
                    for j in range(16):
                        mm(po[:nt, h * 128:(h + 1) * 128], qTm[:, h2, j, :], S0bv[:, j, h2, :], False, j == 15,
                           [t_qTm, t_S0b], [tpo])
                tt("dve", ktm[:], bc(ktok[:64, p * 128:(p + 1) * 128].unsqueeze(1), [64, 16, 128]),
                   bc(smtok[:64, :].unsqueeze(2), [64, 16, 128]), ALU.mult, [t_ktok, tc_], [t_ktm])
                for rnd in range(2):
                    banks = []
                    for q in range(4):
                        ps_, tps = PF()
                        banks.append((ps_, tps))
                        for jj in range(2):
                            j = rnd * 8 + q * 2 + jj
                            for h2 in range(2):
                                h = 2 * p + h2
                                o_ = (jj * 2 + h2) * 128
                                mm(ps_[:64, o_:o_ + 128], ktm[:, j, h2 * 64:(h2 + 1) * 64], vtok[:64, h * 128:(h + 1) * 128],
                                   True, True, [t_ktm, t_vtok], [tps])
                    for q in range(4):
                        ps_, tps = banks[q]
                        j0 = rnd * 8 + q * 2
                        tt("dve", S0v[:, j0:j0 + 2], S0v[:, j0:j0 + 2],
                           ps_[:64, :].rearrange("p (j a v) -> p j a v", j=2, a=2), ALU.add, [t_S0, tps], [t_S0])
                        tt("dve", S0v[:, j0:j0 + 2], S0v[:, j0:j0 + 2],
                           bc(egs[:, j0:j0 + 2, 2 * p:2 * p + 2].unsqueeze(3), [64, 2, 2, 128]), ALU.mult,
                           [t_S0, t_eG, tc_], [t_S0])
                P.dma("sp", st_out_sample[l, p].rearrange("p j a v -> p (j a v)"), S0[0:64, :], r=[t_S0], key="S0out",
                      final=True)
        chk("lb_o")
        if kind != "S":
            ps_, tps = PF()
            for h in range(4):
                mm(ps_[:64, h * 128:(h + 1) * 128], ktok[:nt, h * 64:(h + 1) * 64], vtok[:nt, h * 128:(h + 1) * 128],
                   True, True, [t_ktok, t_vtok], [tps])
            tt("dve", Sm[:, l], Sm[:, l], ps_[:64, :].rearrange("p (h v) -> p h v", h=4), ALU.add, [tS, tps], [tS])
            tt("dve", Sm[:, l], Sm[:, l], bc(egp.unsqueeze(2), [64, 4, 128]), ALU.mult, [tS, t_eG, t_egp, tc_], [tS])
            cp("dve", Smb[:, l], Sm[:, l], [tS], [tS])
            if last_prompt_block:
                P.dma("sp", st_out_prompt[l], Sm[:, l], r=[tS], key="pfinal", final=True)
        hk = hook() if hook is not None else None
        if hk is not None:
            next(hk)
        act(junk[:nt, :], po[:nt, :], AF.Square, [tpo], [t_junk])
        P.op("dve", lambda e: e.reduce_sum(out=ssq[:nt, 0:4], in_=junk[:nt, :].rearrange("p (h v) -> p h v", h=4), axis=AX.X),
             r=[t_junk], w=[t_ssq])
        act(ssq[:nt, 0:4], ssq[:nt, 0:4], AF.Ln, [t_ssq], [t_ssq], bias=1e-6, scale=1.0 / 128.0)
        act(ssq[:nt, 4:8], ssq[:nt, 0:4], AF.Exp, [t_ssq], [t_ssq], scale=-0.5)
        tt("dve", ysb[:nt, :].rearrange("p (h v) -> p h v", h=4), po[:nt, :].rearrange("p (h v) -> p h v", h=4),
           bc(ssq[:nt, 4:8].unsqueeze(2), [nt, 4, 128]), ALU.mult, [tpo, t_ssq], [t_ysb])
        tt("dve", ybf[:nt, :], ysb[:nt, :], srt[:nt, :], ALU.mult, [t_ysb, t_srt], [t_ybf])
        if hk is not None:
            for _ in hk:
                pass
        pt, tp = PB()
        for c in range(4):
            tr(pt[:, c * 128:c * 128 + nt], ybf[:nt, c * 128:(c + 1) * 128], [t_ybf], [tp])
        pv = pt[:].rearrange("p (c t) -> p c t", c=8)[:, 0:4, 0:nt]
        if ycol_scale is not None:
            ts("dve", yT[:, ybase:ybase + 4, c0:c0 + nt], pv, ycol_scale, None, ALU.mult, None, [tp, tc_],
               [t_yT[ybase // 4]])
        else:
            cp("act", yT[:, ybase:ybase + 4, c0:c0 + nt], pv, [tp], [t_yT[ybase // 4]])
        chk("lb_yT")

    class _Stop(Exception):
        pass

    def chk(name):
        if STOP is not None and name == STOP:
            raise _Stop()

    def ph(name):
        P.phase = name

    def main():
        for gi, g in enumerate(GROUPS):
            ncols = gcols(g)
            nblk = len(g)
            last_group = gi == NG - 1
            for bi, (kind, nt, c0, pos0) in enumerate(g):
                if kind == "M":
                    src = meta
                elif kind == "S":
                    src = xs
                else:
                    src = xp[pos0 - 16:pos0 - 16 + nt, :]
                P.dma("sp", xtok[:nt, bi, :], src, w=[t_xtok[bi]], key=("xin", bi))
            ph("ln0")
            layernorm(g, 0)
            chk("ln0")
            for l in range(L):
                ph("gla_proj")
                W, tw = wload(l, "gqk")
                for h in range(4):
                    pt, tp = proj_feat(W, tw, h * 64, 64, xT, t_xT, 8, ncols)
                    cp("act", qTf[:, h, :ncols], pt[:64, :ncols], [tp], [t_qk, t_lhd])
                    pt, tp = proj_feat(W, tw, 256 + h * 64, 64, xT, t_xT, 8, ncols)
                    cp("act", kTf[:, h, :ncols], pt[:64, :ncols], [tp], [t_qk, t_dec])
                chk("gla_qk")
                Wa, twa = wload(l, "ga", live=1)
                pt, tp = proj_feat(Wa, twa, 0, 16, xT, t_xT, 8, ncols)
                cp("act", aTb[:, :ncols], pt[:16, :ncols], [tp], [t_aT])
                Wv, twv = wload(l, "gv")
                Wr_, twr = wload(l, "gr", live=1)
                ph("gla_blk")
                for bi, (kind, nt, c0, pos0) in enumerate(g):
                    U = ca_s if kind == "S" else ca_b
                    if bi % 2 == 0:
                        gpr, t_gpr, eG, eGn, t_eG = junkA[:, 0:256], t_junkA, eG_a, eGn_a, t_eG_a
                    else:
                        gpr, t_gpr, eG, eGn, t_eG = junkB[:, 0:256], t_junkB, eG_b, eGn_b, t_m2
                    pa_, tpa_ = PF()
                    mm(pa_[:nt, 0:256], aTb[0:16, c0:c0 + nt], wa2_t[0:16, l, :], True, False, [t_aT, tc_], [tpa_])
                    mm(pa_[:nt, 0:256], ones_b[0:33, :nt], rowhl[0:33, l, 0:256], False, True, [tc_], [tpa_])
                    act(gpr[:nt, :], pa_[:nt, 0:256], AF.Exp, [tpa_], [t_gpr], scale=-1.0)
                    act(gpr[:nt, :], gpr[:nt, :], AF.Ln, [t_gpr], [t_gpr], bias=1.0)
                    pg, tpg = PF()
                    for h in range(4):
                        mm(pg[:64, h * 128:h * 128 + nt], gpr[:nt, h * 64:(h + 1) * 64], U[:nt, :nt], True, True,
                           [t_gpr, tc_], [tpg])
                    pgv = pg[:].rearrange("p (a t) -> p a t", a=4)[:64, :, 0:nt]
                    act(eG[:, :, :nt], pgv, AF.Exp, [tpg], [t_eG], scale=-1.0 / 16.0)
                    act(eGn[:, :, :nt], pgv, AF.Exp, [tpg], [t_eG], scale=1.0 / 16.0)
                    stt(qTb[:, :, c0:c0 + nt], qTf[:, :, c0:c0 + nt], 0.125, eG[:, :, :nt], ALU.mult, ALU.mult,
                        [t_qk, t_eG], [t_qkb])
                    tt("dve", kTb[:, :, c0:c0 + nt], kTf[:, :, c0:c0 + nt], eGn[:, :, :nt], ALU.mult, [t_qk, t_eG], [t_qkb])
                    if kind == "S":
                        cp("dve", egs_all[:], eG[:, :, 3:64:4].rearrange("p h j -> p j h"), [t_eG], [t_egp])
                    else:
                        cp("dve", egp_all[:, bi, :], eG[:, :, nt - 1], [t_eG], [t_egp])

                def gla_pre(bi):
                    kind, nt, c0, pos0 = g[bi]
                    i2 = bi % 2
                    pt, tp = proj_tok(Wr_, twr, 0, 512, c0, nt)
                    pv_, tpv_ = proj_tok(Wv, twv, 0, 512, c0, nt)
                    yield
                    act(junkB[:nt, :], pt[:nt, :], AF.Exp, [tp], [t_junkB], scale=-1.0)
                    act(junkB[:nt, :], junkB[:nt, :], AF.Ln, [t_junkB], [t_junkB], bias=1.0)
                    act(junkB[:nt, :], junkB[:nt, :], AF.Exp, [t_junkB], [t_junkB], scale=-1.0)
                    cp("act", vtok2[:nt, i2, :], pv_[:nt, :], [tpv_], [t_vtok2[i2]])
                    tt("dve", srt2[:nt, i2, :], pt[:nt, :], junkB[:nt, :], ALU.mult, [tp, t_junkB], [t_srt2[i2]])
                for _ in gla_pre(0):
                    pass
                for bi, (kind, nt, c0, pos0) in enumerate(g):
                    i2 = bi % 2
                    lastp = last_group and bi == nblk - 1
                    hook = (lambda bi=bi: gla_pre(bi + 1)) if bi + 1 < nblk else None
                    lin_block(l, kind, nt, c0, Sg, Sgb, t_Sg[l], egp_all[:, bi, :], egs_all[:], 0, small_t[:, l, 68:69],
                              sg_in, sgp, sgs, lastp, vtok2[:, i2, :], t_vtok2[i2], srt2[:, i2, :], t_srt2[i2], hook)
                    chk("gla_b%d" % bi)
                chk("gla")
                ph("ssd_proj")
                tt("dve", dmat[:], bc(identf[:].unsqueeze(1), [128, 8, 128]),
                   bc(small_t[:, l, 16:24].unsqueeze(2), [128, 8, 128]), ALU.mult, [tc_], [t_dmat])
                Wx0, twx0 = wload(l, "sx0")
                Wx1, twx1 = wload(l, "sx1", live=1)
                has_s = any(b[0] == "S" for b in g)
                npr = ncols - (64 if has_s else 0)
                CP_ = [128, 128, 128, 128, 64, 64, 64, 64]
                cp("dve", xraw[:, :, 0:3], hist[:, l], [t_hist[l]], t_xraw8 + [t_xraw])
                if has_s:
                    P.dma("sp", junkB[:, 0:384], sc_in[l].rearrange("p c j i -> p (c j i)"), w=[t_junkB], key="xsin")
                    cp("dve", xsin[:, :, :, 0:3], junkB[:, 0:384].rearrange("p (c j i) -> p c j i", c=8, j=16),
                       [t_junkB], [t_xsin])

                def silu_chunk(c):
                    pc = CP_[c]
                    act(xcT[:pc, c, :ncols], cacc[:pc, c % 2, :ncols], AF.Silu, [t_cacc2[c % 2]], [t_xcT])
                for c in range(8):
                    pc = CP_[c]
                    txr = t_xraw8[c]
                    if c < 4:
                        pt, tp = proj_feat(Wx0, twx0, c * 128, 128, xT, t_xT, 8, ncols)
                    else:
                        pt, tp = proj_feat(Wx1, twx1, (c - 4) * 64, 64, xT, t_xT, 8, ncols)
                    cp("act", xraw[:pc, c, 3:3 + ncols], pt[:pc, :ncols], [tp], [txr])
                    cw = lambda i, c=c: small_t[:CP_[c], l, 32 + c * 4 + i:33 + c * 4 + i]
                    ca_ = cacc[:pc, c % 2, :]
                    tca = t_cacc2[c % 2]
                    act(ca_[:, 0:npr], xraw[:pc, c, 0:npr], AF.Identity, [txr, tc_], [tca],
                        bias=small_t[:pc, l, 24 + c:25 + c], scale=cw(0))
                    for i in range(1, 4):
                        stt(ca_[:, 0:npr], xraw[:pc, c, i:i + npr], cw(i), ca_[:, 0:npr], ALU.mult, ALU.add,
                            [txr, tc_, tca], [tca])
                    if has_s:
                        cp("dve", xsin[:pc, c, :, 3:7], xraw[:pc, c, 3 + npr:3 + ncols].rearrange("p (j t) -> p j t", j=16),
                           [txr], [t_xsin])
                        cv = ca_[:, npr:ncols].rearrange("p (j t) -> p j t", j=16)
                        act(cv, xsin[:pc, c, :, 0:4], AF.Identity, [t_xsin, tc_], [tca],
                            bias=small_t[:pc, l, 24 + c:25 + c], scale=cw(0))
                        for i in range(1, 4):
                            stt(cv, xsin[:pc, c, :, i:i + 4], cw(i), cv, ALU.mult, ALU.add, [t_xsin, tc_, tca], [tca])
                    if c > 0:
                        silu_chunk(c - 1)
                silu_chunk(7)
                cp("dve", hist[:, l], xraw[:, :, npr:npr + 3], t_xraw8, [t_hist[l]])
                cs_jobs = []
                if last_group:
                    cs_jobs.append((3, lambda k: xT[:, k, npr - 3:npr], scp[l]))
                if has_s:
                    for i in range(3):
                        cs_jobs.append((16, lambda k, i=i: xT[:, k, npr + i + 1:npr + 64:4], scs[l, i]))
                for (mrows, lvf, dst) in cs_jobs:
                    pc_, tpc = PF()
                    pc2, tpc2 = PF()
                    for k in range(8):
                        mm(pc_[:mrows, 0:512], lvf(k), Wx0[:, k, :], k == 0, k == 7, [t_xT, twx0], [tpc])
                    for k in range(8):
                        mm(pc2[:mrows, 0:256], lvf(k), Wx1[:, k, :], k == 0, k == 7, [t_xT, twx1], [tpc2])
                    cp("act", sc3[:mrows, 0:512], pc_[:mrows, 0:512], [tpc], [t_sc3])
                    cp("act", sc3[:mrows, 512:768], pc2[:mrows, 0:256], [tpc2], [t_sc3])
                    P.dma("sp", dst, sc3[:mrows, :], r=[t_sc3], key="sc3out", final=True)
                Wz, twz = wload(l, "sz")
                Wdt, twdt = wload(l, "sdt", live=1)
                ph("ssd_blk")
                blks = list(enumerate(g))
                V = lambda bi: (dtt_all[:, bi, :], gss_all[:, bi, :], eGt_all[:, bi, :], wsd_all[:, bi, :])
                pds = {}
                for bi, (kind, nt, c0, pos0) in blks:
                    pds[bi] = proj_tok(Wdt, twdt, 0, 8, c0, nt)
                for bi, (kind, nt, c0, pos0) in blks:
                    dtt, gss, eGt, wsd = V(bi)
                    pd, tpd = pds[bi]
                    tt("dve", dtt[:nt, :], pd[:nt, 0:8], small_t[:nt, l, 8:16], ALU.add, [tpd, tc_], [t_dt4[bi]])
                for bi, (kind, nt, c0, pos0) in blks:
                    dtt, gss, eGt, wsd = V(bi)
                    act(dtt[:nt, :], dtt[:nt, :], AF.Exp, [t_dt4[bi]], [t_dt4[bi]])
                    act(dtt[:nt, :], dtt[:nt, :], AF.Ln, [t_dt4[bi]], [t_dt4[bi]], bias=1.0)
                    act(lndt_all[:nt, bi, :], dtt[:nt, :], AF.Ln, [t_dt4[bi]], [t_dt4[bi]])
                for bi, (kind, nt, c0, pos0) in blks:
                    dtt, gss, eGt, wsd = V(bi)
                    tt("dve", gss[:nt, :], dtt[:nt, :], aneg[:nt, l, :], ALU.mult, [t_dt4[bi], tc_], [t_dt4[bi]])
                    if kind == "S":
                        tt("dve", gm[:nt, :, :], bc(gss[:nt, :].unsqueeze(1), [nt, 16, 8]),
                           bc(smtok[:nt, :].unsqueeze(2), [nt, 16, 8]), ALU.mult, [t_dt4[bi], tc_], [t_gm])
                pqs = {}
                for bi, (kind, nt, c0, pos0) in blks:
                    dtt, gss, eGt, wsd = V(bi)
                    U = ca_s if kind == "S" else ca_b
                    SLm = sl_s if kind == "S" else sl_b
                    nsq = 16 if kind == "S" else 1
                    gmv = gm[:nt, :, :].rearrange("p j h -> p (j h)") if kind == "S" else gss[:nt, :]
                    pq, tpq = PF()
                    pqs[bi] = (pq, tpq)
                    mm(pq[:nt, 0:8], U[:nt, :nt], gss[:nt, :], True, True, [tc_, t_dt4[bi]], [tpq])
                    mm(pq[:nt, 8:16], SLm[:nt, :nt], gss[:nt, :], True, True, [tc_, t_dt4[bi]], [tpq])
                    mm(pq[:, 128:128 + nsq * 8], ones_f[:nt, :], gmv, True, True, [tc_, t_gm, t_dt4[bi]], [tpq])
                for bi, (kind, nt, c0, pos0) in blks:
                    dtt, gss, eGt, wsd = V(bi)
                    pq, tpq = pqs[bi]
                    act(eGt[:nt, :], pq[:nt, 0:8], AF.Exp, [tpq], [t_eGt4[bi]])
                    act(wsd[:nt, :], pq[:nt, 8:16], AF.Exp, [tpq], [t_eGt4[bi]])
                    if kind == "S":
                        act(eGe[:, 0:16, :], pq[:, 128:256].rearrange("p (j h) -> p j h", h=8), AF.Exp, [tpq], [t_eGe])
                    else:
                        act(eGe_all[:, bi, :], pq[:, 128:136], AF.Exp, [tpq], [t_eGe])
                for bi, (kind, nt, c0, pos0) in blks:
                    dtt, gss, eGt, wsd = V(bi)
                    tt("dve", wsd[:nt, :], wsd[:nt, :], dtt[:nt, :], ALU.mult, [t_eGt4[bi], t_dt4[bi]], [t_eGt4[bi]])
                for bi, (kind, nt, c0, pos0) in enumerate(g):
                    U = ca_s if kind == "S" else ca_b
                    SLm = sl_s if kind == "S" else sl_b
                    nsq = 16 if kind == "S" else 1
                    pt, tp = PB()
                    for c in range(4):
                        tr(pt[:nt, c * 128:(c + 1) * 128], xcT[:, c, c0:c0 + nt], [t_xcT], [tp])
                    for gg in range(2):
                        tr(pt[:nt, 512 + gg * 64:512 + (gg + 1) * 64], xcT[0:64, 4 + gg, c0:c0 + nt], [t_xcT], [tp])
                    cp("act", xstok[:nt, :], pt[:nt, 0:512], [tp], [t_xstok])
                    cp("act", btok[:nt, :], pt[:nt, 512:640], [tp], [t_xstok])
                    dtt = dtt_all[:, bi, :]; gss = gss_all[:, bi, :]; eGt = eGt_all[:, bi, :]; wsd = wsd_all[:, bi, :]
                    NGm = ng_s if kind == "S" else ng_b
                    tt("dve", lhd[:nt, :, :nt], bc(SLm[:nt, :nt].unsqueeze(1), [nt, 8, nt]),
                       bc(gss[:nt, :].unsqueeze(2), [nt, 8, nt]), ALU.mult, [tc_, t_dt4[bi]], [t_lhd, t_qk])
                    for half in range(2):
                        pdf, tpdf = PF()
                        for hh in range(4):
                            h = half * 4 + hh
                            mm(pdf[:nt, hh * 128:hh * 128 + nt], lhd[:nt, h, :nt], U[:nt, :nt], True, False,
                               [t_lhd, tc_], [tpdf])
                            mm(pdf[:nt, hh * 128:hh * 128 + nt], identf[:nt, :nt], NGm[:nt, :nt], False, True,
                               [tc_], [tpdf])
                        for hh in range(4):
                            h = half * 4 + hh
                            act(dec[:nt, h, :nt], pdf[:nt, hh * 128:hh * 128 + nt], AF.Exp, [tpdf, t_dt4[bi]], [t_dec],
                                bias=lndt_all[:nt, bi, h:h + 1])
                    pz, tpz = proj_tok(Wz, twz, 0, 512, c0, nt)
                    pbc, tpbc = PF()
                    for gg in range(2):
                        mm(pbc[:nt, gg * 128:gg * 128 + nt], xcT[0:64, 4 + gg, c0:c0 + nt], xcT[0:64, 6 + gg, c0:c0 + nt],
                           True, True, [t_xcT], [tpbc])
                    for gg in range(2):
                        tt("dve", attb[:nt, gg * 4:gg * 4 + 4, :nt],
                           bc(pbc[:nt, gg * 128:gg * 128 + nt].unsqueeze(1), [nt, 4, nt]),
                           dec[:nt, gg * 4:gg * 4 + 4, :nt], ALU.mult, [tpbc, t_dec], [t_att])
                    act(junk[:nt, :], pz[:nt, :], AF.Exp, [tpz], [t_junk], scale=-1.0)
                    act(junk[:nt, :], junk[:nt, :], AF.Ln, [t_junk], [t_junk], bias=1.0)
                    act(junk[:nt, :], junk[:nt, :], AF.Exp, [t_junk], [t_junk], scale=-1.0)
                    tt("dve", srt[:nt, :], pz[:nt, :], junk[:nt, :], ALU.mult, [tpz, t_junk], [t_srt])
                    py, tpy = PL(0)
                    for h in range(8):
                        mm(py[:nt, h * 64:(h + 1) * 64], attb[:nt, h, :nt], xstok[:nt, h * 64:(h + 1) * 64], True, False,
                           [t_att, t_xstok], [tpy])
                        mm(py[:nt, h * 64:(h + 1) * 64], dmat[:nt, h, :nt], xstok[:nt, h * 64:(h + 1) * 64], False, True,
                           [t_dmat, t_xstok], [tpy])
                    tt("dve", xw[:nt, :].rearrange("p (h v) -> p h v", h=8), xstok[:nt, :].rearrange("p (h v) -> p h v", h=8),
                       bc(wsd[:nt, :].unsqueeze(2), [nt, 8, 64]), ALU.mult, [t_xstok, t_eGt4[bi]], [t_xw])
                    pi_, tpi = PL(1)
                    S0v = S0[0:64, :].rearrange("p (j a v) -> p j a v", j=16, a=4)
                    S0bv = S0b[0:64, :].rearrange("p (j a v) -> p j a v", j=16, a=4)
                    if kind != "S":
                        for h in range(8):
                            mm(pi_[:nt, h * 64:(h + 1) * 64], xcT[0:64, 6 + h // 4, c0:c0 + nt], Ssb[:, l, h, :], True, True,
                               [t_xcT, t_Ss[l]], [tpi])
                    else:
                        for gg in range(2):
                            P.dma("sp", S0[0:64, :], ss_in[l, gg].rearrange("p j a v -> p (j a v)"), w=[t_S0], key="S0")
                            P.dma("pool", S0b[0:64, :], ss_in[l, gg].rearrange("p j a v -> p (j a v)"), w=[t_S0b], key="S0b",
                                  max_dma_last_dim=4096)
                            tt("dve", cTm[:], bc(xcT[0:64, 6 + gg, c0:c0 + 64].unsqueeze(1), [64, 16, 64]), smT[0:64],
                               ALU.mult, [t_xcT, tc_], [t_cTm])
                            for hh in range(4):
                                h = gg * 4 + hh
                                for j in range(16):
                                    mm(pi_[:nt, h * 64:(h + 1) * 64], cTm[:, j, :], S0bv[:, j, hh, :], j == 0, j == 15,
                                       [t_cTm, t_S0b], [tpi])
                            tt("dve", btm[:], bc(btok[:64, gg * 64:(gg + 1) * 64].unsqueeze(1), [64, 16, 64]),
                               bc(smtok[:64, :].unsqueeze(2), [64, 16, 64]), ALU.mult, [t_xstok, tc_], [t_btm])
                            for rnd in range(2):
                                banks = []
                                for q in range(4):
                                    ps_, tps = PF()
                                    banks.append((ps_, tps))
                                    for jj in range(2):
                                        j = rnd * 8 + q * 2 + jj
                                        mm(ps_[:64, jj * 256:(jj + 1) * 256], btm[:, j, :], xw[:64, gg * 256:(gg + 1) * 256],
                                           True, True, [t_btm, t_xw], [tps])
                                for q in range(4):
                                    ps_, tps = banks[q]
                                    j0 = rnd * 8 + q * 2
                                    tt("dve", S0v[:, j0:j0 + 2], S0v[:, j0:j0 + 2],
                                       bc(eGe[0:64, j0:j0 + 2, gg * 4:gg * 4 + 4].unsqueeze(3), [64, 2, 4, 64]), ALU.mult,
                                       [t_S0, t_eGe], [t_S0])
                                    tt("dve", S0v[:, j0:j0 + 2], S0v[:, j0:j0 + 2],
                                       ps_[:64, :].rearrange("p (j a v) -> p j a v", j=2, a=4), ALU.add, [t_S0, tps], [t_S0])
                            P.dma("sp", sss[l, gg].rearrange("p j a v -> p (j a v)"), S0[0:64, :], r=[t_S0], key="S0out",
                                  final=True)
                    tt("dve", ysb[:nt, :].rearrange("p (h v) -> p h v", h=8), pi_[:nt, :].rearrange("p (h v) -> p h v", h=8),
                       bc(eGt[:nt, :].unsqueeze(2), [nt, 8, 64]), ALU.mult, [tpi, t_eGt4[bi]], [t_ysb])
                    tt("dve", ysb[:nt, :], ysb[:nt, :], py[:nt, :], ALU.add, [t_ysb, tpy], [t_ysb])
                    tt("dve", ysb[:nt, :], ysb[:nt, :], srt[:nt, :], ALU.mult, [t_ysb, t_srt], [t_ysb])
                    act(junk[:nt, :], ysb[:nt, :], AF.Square, [t_ysb], [t_junk])
                    P.op("dve", lambda e, nt=nt: e.reduce_sum(out=ssq[:nt, 0:2], in_=junk[:nt, :].rearrange("p (g v) -> p g v", g=2), axis=AX.X),
                         r=[t_junk], w=[t_ssq])
                    act(ssq[:nt, 0:2], ssq[:nt, 0:2], AF.Ln, [t_ssq], [t_ssq], bias=1e-6, scale=1.0 / 256.0)
                    act(ssq[:nt, 4:6], ssq[:nt, 0:2], AF.Exp, [t_ssq], [t_ssq], scale=-0.5)
                    tt("dve", ybf[:nt, :].rearrange("p (g v) -> p g v", g=2), ysb[:nt, :].rearrange("p (g v) -> p g v", g=2),
                       bc(ssq[:nt, 4:6].unsqueeze(2), [nt, 2, 256]), ALU.mult, [t_ysb, t_ssq], [t_ybf])
                    pt, tp = PB()
                    for c in range(4):
                        tr(pt[:, c * 128:c * 128 + nt], ybf[:nt, c * 128:(c + 1) * 128], [t_ybf], [tp])
                    pv = pt[:].rearrange("p (c t) -> p c t", c=8)[:, 0:4, 0:nt]
                    tt("dve", yT[:, 4:8, c0:c0 + nt], pv, bc(small_t[:, l, 64:68].unsqueeze(2), [128, 4, nt]), ALU.mult,
                       [tp, tc_], [t_yT[1]])
                    if kind != "S":
                        ps_, tps = PF()
                        for gg in range(2):
                            mm(ps_[:64, gg * 256:(gg + 1) * 256], btok[:nt, gg * 64:(gg + 1) * 64], xw[:nt, gg * 256:(gg + 1) * 256],
                               True, True, [t_xstok, t_xw], [tps])
                        tt("dve", Ss[:, l], Ss[:, l], bc(eGe_all[0:64, bi, :].unsqueeze(2), [64, 8, 64]), ALU.mult,
                           [t_Ss[l], t_eGe], [t_Ss[l]])
                        tt("dve", Ss[:, l], Ss[:, l], ps_[:64, :].rearrange("p (h v) -> p h v", h=8), ALU.add,
                           [t_Ss[l], tps], [t_Ss[l]])
                        cp("act", Ssb[:, l], Ss[:, l], [t_Ss[l]], [t_Ss[l]])
                        if last_group and bi == nblk - 1:
                            P.dma("sp", ssp[l], Ss[:, l], r=[t_Ss[l]], key="pfinal", final=True)
                    else:
                        pass
                chk("ssd")
                ph("ret_proj")
                W, tw = wload(l, "rqk")
                W2_, tw2 = wload(l, "rsw", live=1)
                rt = lnrt[0:64, :].rearrange("p (a h c) -> p a h c", a=2, h=4)
                for which, dstb in ((0, qTb), (1, kTb)):
                    P.dma("sp", lnrt[0:64, :], cst["rtab"][gi, which].rearrange("p a h c -> p (a h c)"), w=[t_rtab], key="lnbc")
                    for h in range(4):
                        pt, tp = proj_feat(W, tw, which * 256 + h * 64, 64, xT, t_xT, 8, ncols)
                        tt("dve", qTf[:, h, :ncols], pt[:64, :ncols], rt[:, 0, h, :ncols], ALU.mult, [tp, t_rtab], [t_qk, t_lhd])
                        pt, tp = proj_feat(W2_, tw2, which * 256 + h * 64, 64, xT, t_xT, 8, ncols)
                        tt("dve", kTf[:, h, :ncols], pt[:64, :ncols], rt[:, 1, h, :ncols], ALU.mult, [tp, t_rtab], [t_qk, t_dec])
                        tt("dve", dstb[:, h, :ncols], qTf[:, h, :ncols], kTf[:, h, :ncols], ALU.add, [t_qk], [t_qkb])
                ph("ret_blk")
                Wv, twv = wload(l, "rv")
                Wg_, twg = wload(l, "rg", live=1)
                def ret_pre(bi):
                    kind, nt, c0, pos0 = g[bi]
                    i2 = bi % 2
                    pt, tp = proj_tok(Wg_, twg, 0, 512, c0, nt)
                    pv_, tpv_ = proj_tok(Wv, twv, 0, 512, c0, nt)
                    yield
                    act(junkB[:nt, :], pt[:nt, :], AF.Exp, [tp], [t_junkB], scale=-1.0)
                    act(junkB[:nt, :], junkB[:nt, :], AF.Ln, [t_junkB], [t_junkB], bias=1.0)
                    act(junkB[:nt, :], junkB[:nt, :], AF.Exp, [t_junkB], [t_junkB], scale=-1.0)
                    cp("act", vtok2[:nt, i2, :], pv_[:nt, :], [tpv_], [t_vtok2[i2]])
                    tt("dve", srt2[:nt, i2, :], pt[:nt, :], junkB[:nt, :], ALU.mult, [tp, t_junkB], [t_srt2[i2]])
                for _ in ret_pre(0):
                    pass
                for bi, (kind, nt, c0, pos0) in enumerate(g):
                    i2 = bi % 2
                    egp = regend[:, KIND[kind], :]
                    egs = bc(regend[:, 2, :].unsqueeze(1), [64, 16, 4])
                    lastp = last_group and bi == nblk - 1
                    hook = (lambda bi=bi: ret_pre(bi + 1)) if bi + 1 < nblk else None
                    lin_block(l, kind, nt, c0, Sr, Srb, t_Sr[l], egp, egs, 8, None, sr_in, srp, srs, lastp,
                              vtok2[:, i2, :], t_vtok2[i2], srt2[:, i2, :], t_srt2[i2], hook)
                chk("ret")
                ph("merge")
                macc = big[:, 0:8192].bitcast(F32).rearrange("p (c t) -> p c t", c=8)
                for q in range(2):
                    for br in range(3):
                        Wg2, twg2 = wload(l, "mg%d%d" % (q, br))
                        Wo2, two2 = wload(l, "mo%d%d" % (q, br), live=1)
                        for cc in range(4):
                            dc = q * 4 + cc
                            pg_, tpg_ = proj_feat(Wg2, twg2, cc * 128, 128, xT, t_xT, 8, ncols)
                            jk, tjk = (junkA, t_junkA) if cc % 2 == 0 else (junkB, t_junkB)
                            act(jk[:, :ncols], pg_[:, :ncols], AF.Sigmoid, [tpg_], [tjk])
                            po_, tpo_ = proj_feat(Wo2, two2, cc * 128, 128, yT, t_yT[br], 4, ncols, koff=br * 4)
                            if br == 0:
                                tt("dve", macc[:, dc, :ncols], jk[:, :ncols], po_[:, :ncols], ALU.mult, [tjk, tpo_], [t_xraw])
                            else:
                                tca = t_cacc2[cc % 2]
                                tt("dve", cacc[:, cc % 2, :ncols], jk[:, :ncols], po_[:, :ncols], ALU.mult, [tjk, tpo_], [tca])
                                if br == 1:
                                    tt("dve", macc[:, dc, :ncols], macc[:, dc, :ncols], cacc[:, cc % 2, :ncols], ALU.add,
                                       [t_xraw, tca], [t_xraw])
                                else:
                                    tt("dve", mT[:, dc, :ncols], macc[:, dc, :ncols], cacc[:, cc % 2, :ncols], ALU.add,
                                       [t_xraw, tca], [t_mT])
                chk("merge")
                ph("wo_ln1")
                Wo0 = wload(l, "wo0")
                Wo1 = wload(l, "wo1", live=1)
                for bi, (kind, nt, c0, pos0) in enumerate(g):
                    for hf, (Wq, twq) in enumerate((Wo0, Wo1)):
                        pt, tp = PF()
                        for k in range(8):
                            mm(pt[:nt, :], mT[:, k, c0:c0 + nt], Wq[:, k, :], k == 0, k == 7, [t_mT, twq], [tp])
                        stt(xtok[:nt, bi, hf * 512:(hf + 1) * 512], xtok[:nt, bi, hf * 512:(hf + 1) * 512], ALPHA, pt[:nt, :],
                            ALU.mult, ALU.add, [t_xtok[bi], tp], [t_xtok[bi]])
                layernorm(g, 1 + 2 * l)
                chk("ln1")
                ph("ff1")
                for i in range(8):
                    W1, tw1 = wload(l, "f1%d" % i)
                    for cc in range(4):
                        f = i * 4 + cc
                        pt, tp = proj_feat(W1, tw1, cc * 128, 128, xT, t_xT, 8, ncols)
                        jk, tjk = (junkA, t_junkA) if f % 2 == 0 else (junkB, t_junkB)
                        ts("dve", jk[:, :ncols], pt[:, :ncols], b1col_t[:, l, f:f + 1], 0.0, ALU.add, ALU.max, [tp, tc_], [tjk])
                        act(hid[:, f, :ncols], jk[:, :ncols], AF.Square, [tjk], [t_hid[i]])
                ph("ff2_ln2")
                for half in range(2):
                    accs = [PF() for _ in range(nblk)]
                    for fg in range(4):
                        W2f, tw2f = wload(l, "f2%d%d" % (half, fg))
                        for bi, (kind, nt, c0, pos0) in enumerate(g):
                            pt, tp = accs[bi]
                            for k in range(8):
                                f = fg * 8 + k
                                mm(pt[:nt, :], hid[:, f, c0:c0 + nt], W2f[:, k, :], fg == 0 and k == 0, False,
                                   [t_hid[f // 4], tw2f], [tp])
                            if fg == 3:
                                o_ = 256 + half * 512
                                mm(pt[:nt, :], ones_b[0:33, :nt], rowhl[0:33, l, o_:o_ + 512], False, True, [tc_], [tp])
                    for bi, (kind, nt, c0, pos0) in enumerate(g):
                        pt, tp = accs[bi]
                        stt(xtok[:nt, bi, half * 512:(half + 1) * 512], xtok[:nt, bi, half * 512:(half + 1) * 512], ALPHA,
                            pt[:nt, :], ALU.mult, ALU.add, [t_xtok[bi], tp], [t_xtok[bi]])
                layernorm(g, 2 + 2 * l)
            for bi, (kind, nt, c0, pos0) in enumerate(g):
                if kind == "B":
                    P.dma("sp", yp[pos0 - 16:pos0 - 16 + nt, :], xtok[:nt, bi, :], r=[t_xtok[bi]], key=("yout", bi), final=True)
                elif kind == "S":
                    P.dma("sp", ys, xtok[:nt, bi, :], r=[t_xtok[bi]], key=("yout", bi), final=True)


    try:
        main()
    except _Stop:
        pass
    P.finish()
    for cm in reversed(cms):
        cm.__exit__(None, None, None)
    global LAST_PHASES
    LAST_PHASES = P.phases
    return nc


LAST_PHASES = None
_CACHE = {}


def kernel(x_prompt, x_sample, state_gla, state_ssm, state_conv, state_ret, meta_tokens,
           ln_in_w, ln_in_b, w_in, w_gla_a2, b_gla_a, w_gla_norm, conv_w, conv_b, dt_bias,
           a_log, d_skip, w_ssm_norm, w_gla_out, w_ssm_out, w_ret_out, w_o, ln1_w, ln1_b,
           w_ff1, b_ff1, w_ff2, b_ff2, ln2_w, ln2_b):
    f = lambda a: np.ascontiguousarray(np.asarray(a, dtype=np.float32))
    (x_prompt, x_sample, state_gla, state_ssm, state_conv, state_ret, meta_tokens, ln_in_w, ln_in_b, w_in,
     w_gla_a2, b_gla_a, w_gla_norm, conv_w, conv_b, dt_bias, a_log, d_skip, w_ssm_norm, w_gla_out, w_ssm_out,
     w_ret_out, w_o, ln1_w, ln1_b, w_ff1, b_ff1, w_ff2, b_ff2, ln2_w, ln2_b) = [f(a) for a in (
        x_prompt, x_sample, state_gla, state_ssm, state_conv, state_ret, meta_tokens, ln_in_w, ln_in_b, w_in,
        w_gla_a2, b_gla_a, w_gla_norm, conv_w, conv_b, dt_bias, a_log, d_skip, w_ssm_norm, w_gla_out, w_ssm_out,
        w_ret_out, w_o, ln1_w, ln1_b, w_ff1, b_ff1, w_ff2, b_ff2, ln2_w, ln2_b)]
    if "nc" not in _CACHE:
        _CACHE["nc"] = build_program()
        _CACHE["consts"] = build_consts()
    nc = _CACHE["nc"]
    consts = _CACHE["consts"]
    wstream = build_wstream(w_in, w_gla_out, w_ssm_out, w_ret_out, w_o, w_ff1, w_ff2)
    lnw = [ln_in_w, ln1_w[0], ln2_w[0], ln1_w[1], ln2_w[1]]
    lnb = [ln_in_b, ln1_b[0], ln2_b[0], ln1_b[1], ln2_b[1]]
    lnbc = np.empty((5, 2, 128, D), np.float32)
    lncol = np.empty((128, 5, 2, 8), np.float32)
    for i in range(5):
        lnbc[i, 0] = lnw[i][None, :]
        lnbc[i, 1] = lnb[i][None, :]
        lncol[:, i, 0, :] = lnw[i].reshape(8, 128).T
        lncol[:, i, 1, :] = lnb[i].reshape(8, 128).T
    small = np.zeros((128, L, 80), np.float32)
    rowp = np.zeros((1, L, 1280), np.float32)
    for l in range(L):
        small[:, l, 0:8] = a_log[l][None, :]
        small[:, l, 8:16] = dt_bias[l][None, :]
        small[:, l, 16:24] = d_skip[l][None, :]
        for c in range(8):
            if c < 4:
                ch = np.arange(c * 128, (c + 1) * 128)
            else:
                ch = np.arange(512 + (c - 4) * 64, 512 + (c - 3) * 64)
            small[:len(ch), l, 24 + c] = conv_b[l][ch]
            for i in range(4):
                small[:len(ch), l, 32 + c * 4 + i] = conv_w[l][i, ch]
        small[:, l, 64:68] = w_ssm_norm[l].reshape(4, 128).T
        small[:, l, 68] = w_gla_norm[l]
        rowp[0, l, 0:256] = b_gla_a[l]
        rowp[0, l, 256:1280] = b_ff2[l]
    wa2 = np.ascontiguousarray(w_gla_a2.transpose(1, 0, 2))
    b1col = np.ascontiguousarray(b_ff1.reshape(L, 32, 128).transpose(2, 0, 1))
    in_maps = []
    for c in range(NCORES):
        bs = slice(c * NSEQ, (c + 1) * NSEQ)
        def lin_state(s):
            s = s[:, bs].reshape(L, NSEQ, 2, 2, 64, 128)
            return np.ascontiguousarray(s.transpose(0, 2, 4, 1, 3, 5))
        s = state_ssm[:, bs].reshape(L, NSEQ, 2, 4, 64, 64)
        ss_in = np.ascontiguousarray(s.transpose(0, 2, 4, 1, 3, 5))
        sc_in = np.zeros((L, 128, 8, NSEQ, 3), np.float32)
        sc = state_conv[:, bs]
        for cc in range(8):
            if cc < 4:
                ch = np.arange(cc * 128, (cc + 1) * 128)
            else:
                ch = np.arange(512 + (cc - 4) * 64, 512 + (cc - 3) * 64)
            sc_in[:, :len(ch), cc] = sc[:, :, :, ch].transpose(0, 3, 1, 2)
        m = {"xp": x_prompt[c], "xs": x_sample[bs].reshape(NSEQ * DSEQ, D), "meta": meta_tokens,
             "wst": wstream, "lnbc": lnbc, "lncol": lncol, "small": small, "rowp": rowp.reshape(1, L * 1280), "wa2": wa2, "b1col": b1col,
             "sg_in": lin_state(state_gla), "sr_in": lin_state(state_ret), "ss_in": ss_in, "sc_in": sc_in}
        for n in CONST_NAMES:
            m["c_" + n] = consts[n]
        in_maps.append(m)
    if DBG_CORES:
        res = run_bass_kernel_spmd(nc, in_maps[:DBG_CORES], core_ids=list(range(DBG_CORES)))
        R = [res.results[c % DBG_CORES] for c in range(NCORES)]
    else:
        res = run_bass_kernel_spmd(nc, in_maps, core_ids=list(range(NCORES)))
        R = res.results
    y_prompt = np.stack([R[c]["yp"] for c in range(NCORES)])
    y_sample = np.concatenate([R[c]["ys"].reshape(NSEQ, DSEQ, D) for c in range(NCORES)], axis=0)

    def lin_p(name):
        o = np.stack([R[c][name] for c in range(NCORES)], axis=1)
        return np.ascontiguousarray(o.transpose(0, 1, 3, 2, 4))

    def lin_s(name):
        o = np.concatenate([R[c][name].transpose(0, 3, 1, 4, 2, 5).reshape(L, NSEQ, 4, 64, 128)
                            for c in range(NCORES)], axis=1)
        return np.ascontiguousarray(o)
    ssm_p = np.stack([R[c]["ssp"] for c in range(NCORES)], axis=1)
    ssm_p = np.ascontiguousarray(ssm_p.transpose(0, 1, 3, 2, 4))
    ssm_s = np.concatenate([R[c]["sss"].transpose(0, 3, 1, 4, 2, 5).reshape(L, NSEQ, 8, 64, 64)
                            for c in range(NCORES)], axis=1)
    conv_p = np.stack([R[c]["scp"] for c in range(NCORES)], axis=1)
    conv_s = np.concatenate([R[c]["scs"].transpose(0, 2, 1, 3) for c in range(NCORES)], axis=1)
    return (y_prompt.astype(np.float32), y_sample.astype(np.float32),
            lin_p("sgp"), lin_s("sgs"), ssm_p, np.ascontiguousarray(ssm_s),
            np.ascontiguousarray(conv_p), np.ascontiguousarray(conv_s), lin_p("srp"), lin_s("srs"))
```

```python
import numpy as np
import concourse.bass as bass
import concourse.mybir as mybir
from concourse.bass_utils import run_bass_kernel_spmd

F32 = mybir.dt.float32
BF16 = mybir.dt.bfloat16
AF = mybir.ActivationFunctionType
ALU = mybir.AluOpType
AX = mybir.AxisListType

D = 1024
SEQ = 2048
NMETA = 16
NSEQ = 16
DSEQ = 4
PAST = 16384
L = 2
ALPHA = float((2 * L) ** 0.25)
NCORES = 8
STOP = None
DBG_CORES = None

GROUPS = []
_g0 = [("M", 16, 0, 0)]
for i in range(2):
    _g0.append(("B", 128, 16 + 128 * i, 16 + 128 * i))
_g0.append(("S", 64, 272, PAST))
GROUPS.append(_g0)
_b = 2
for n in (3, 3, 4, 4):
    g = []
    for i in range(n):
        g.append(("B", 128, 128 * i, 16 + 128 * (_b + i)))
    _b += n
    GROUPS.append(g)
NG = len(GROUPS)


def gcols(g):
    return sum(b[1] for b in g)


WT = [("gqk", 8, 512), ("ga", 8, 16), ("gv", 8, 512), ("gr", 8, 512),
      ("sx0", 8, 512), ("sx1", 8, 256), ("sz", 8, 512), ("sdt", 8, 8),
      ("rqk", 8, 512), ("rsw", 8, 512), ("rv", 8, 512), ("rg", 8, 512)]
for q in range(2):
    for br in range(3):
        WT.append(("mg%d%d" % (q, br), 8, 512))
        WT.append(("mo%d%d" % (q, br), 4, 512))
WT += [("wo0", 8, 512), ("wo1", 8, 512)]
for i in range(8):
    WT.append(("f1%d" % i, 8, 512))
for half in range(2):
    for fg in range(4):
        WT.append(("f2%d%d" % (half, fg), 8, 512))
WOFF = {}
_o = 0
for nm, kc, ncl in WT:
    WOFF[nm] = (_o, kc, ncl)
    _o += kc * ncl
WTOT = _o
NSLOT = 3


def _perm_half():
    p = np.arange(256).reshape(4, 2, 32)[:, ::-1, :].reshape(256)
    return p


def build_wstream(w_in, w_gla_out, w_ssm_out, w_ret_out, w_o, w_ff1, w_ff2):
    out = np.empty((L, 128, WTOT), np.float32)
    offs = np.cumsum([0, 256, 256, 512, 512, 16, 512, 768, 8, 256, 256, 512, 512, 3072])
    o = {n: offs[i] for i, n in enumerate(["gq", "gk", "gv", "gr", "ga", "sz", "sx", "sdt", "rq", "rk", "rv", "rg", "gate"])}
    ph = _perm_half()
    for l in range(L):
        wi = w_in[l]
        def put(nm, mat):
            off, kc, ncl = WOFF[nm]
            assert mat.shape == (kc * 128, ncl), (nm, mat.shape)
            out[l, :, off:off + kc * ncl] = mat.reshape(kc, 128, ncl).transpose(1, 0, 2).reshape(128, kc * ncl)
        put("gqk", wi[:, o["gq"]:o["gq"] + 512])
        put("gv", wi[:, o["gv"]:o["gv"] + 512])
        put("gr", wi[:, o["gr"]:o["gr"] + 512])
        put("ga", wi[:, o["ga"]:o["ga"] + 16])
        put("sz", wi[:, o["sz"]:o["sz"] + 512])
        put("sdt", wi[:, o["sdt"]:o["sdt"] + 8])
        put("sx0", wi[:, o["sx"]:o["sx"] + 512])
        put("sx1", wi[:, o["sx"] + 512:o["sx"] + 768])
        put("rqk", wi[:, o["rq"]:o["rq"] + 512])
        rq = wi[:, o["rq"]:o["rq"] + 256][:, ph]
        rk = wi[:, o["rk"]:o["rk"] + 256][:, ph]
        put("rsw", np.concatenate([rq, rk], axis=1))
        put("rv", wi[:, o["rv"]:o["rv"] + 512])
        put("rg", wi[:, o["rg"]:o["rg"] + 512])
        outs = [w_gla_out[l], w_ssm_out[l], w_ret_out[l]]
        for q in range(2):
            for br in range(3):
                put("mg%d%d" % (q, br), wi[:, o["gate"] + br * 1024 + q * 512:o["gate"] + br * 1024 + q * 512 + 512])
                put("mo%d%d" % (q, br), outs[br][:, q * 512:(q + 1) * 512])
        put("wo0", w_o[l][:, 0:512])
        put("wo1", w_o[l][:, 512:1024])
        for i in range(8):
            put("f1%d" % i, w_ff1[l][:, i * 512:(i + 1) * 512])
        for half in range(2):
            for fg in range(4):
                put("f2%d%d" % (half, fg), w_ff2[l][fg * 1024:(fg + 1) * 1024, half * 512:(half + 1) * 512])
    return out


def build_consts():
    c = {}
    i = np.arange(128)
    ca = (i[:, None] <= i[None, :]).astype(np.float32)
    sl = (i[:, None] > i[None, :]).astype(np.float32)
    same = ((i[:, None] // 4) == (i[None, :] // 4)).astype(np.float32)
    c["ca_b"] = ca
    c["sl_b"] = sl
    c["ca_s"] = ca * same
    c["sl_s"] = sl * same
    c["ident"] = np.eye(128, dtype=np.float32)
    sm = np.zeros((128, 16), np.float32)
    for s in range(64):
        sm[s, s // 4] = 1.0
    c["smtok"] = sm
    smT = np.zeros((128, 16, 64), np.float32)
    for t in range(64):
        smT[:, t // 4, t] = 1.0
    c["smT"] = smT
    lg = np.log1p(-np.exp2(-5.0 - np.arange(4, dtype=np.float64)))
    invf = (10000.0 ** (-(np.arange(32, dtype=np.float32) / np.float32(32.0)))).astype(np.float32)
    tabs = []
    for g in GROUPS:
        tab = np.zeros((2, 64, 2, 4, 512), np.float32)
        for (kind, nt, c0, pos0) in g:
            for t in range(nt):
                if kind == "S":
                    pos = PAST + (t % 4)
                    idx = (t % 4) + 1
                else:
                    pos = pos0 + t
                    idx = t + 1
                ang = (np.float32(pos) * invf).astype(np.float32).astype(np.float64)
                cfull = np.concatenate([np.cos(ang), np.cos(ang)])
                sfull = np.concatenate([-np.sin(ang), np.sin(ang)])
                for h in range(4):
                    eg = np.exp(lg[h] * idx)
                    tab[0, :, 0, h, c0 + t] = cfull * eg
                    tab[0, :, 1, h, c0 + t] = sfull * eg
                    tab[1, :, 0, h, c0 + t] = cfull / eg * 0.125
                    tab[1, :, 1, h, c0 + t] = sfull / eg * 0.125
        tabs.append(tab)
    c["rtab"] = np.stack(tabs)
    eg = np.zeros((64, 3, 4), np.float32)
    for ki, n in enumerate((128, 16, 4)):
        for h in range(4):
            eg[:, ki, h] = np.exp(lg[h] * n)
    c["regend"] = eg
    return c


CONST_NAMES = ["ca_b", "sl_b", "ca_s", "sl_s", "ident", "smtok", "smT", "rtab", "regend"]


class Tok:
    __slots__ = ("w", "r")

    def __init__(self):
        self.w = None
        self.r = []


class Prog:
    ENG = ("pe", "act", "dve", "pool", "sp")

    def __init__(self, nc):
        self.nc = nc
        self.q = {e: [] for e in self.ENG}
        self.cnt = {e: 0 for e in self.ENG}
        self.sems = {}
        self.semvals = {}
        self.waited = {e: {} for e in self.ENG}
        self.pending_out = []
        self._stack = []
        self.phase = "init"
        self.phases = {e: [] for e in self.ENG}

    def sem(self, key):
        if key not in self.sems:
            cm = self.nc.semaphore("s%d" % len(self.sems))
            s = cm.__enter__()
            self._stack.append(cm)
            self.sems[key] = s
            self.semvals[key] = 0
        return self.sems[key]

    def _deps(self, eng, reads, writes):
        ev = {}

        def add(e):
            if e is None:
                return
            if ev.get(e[0], 0) < e[1]:
                ev[e[0]] = e[1]
        for t in reads:
            add(t.w)
        for t in writes:
            add(t.w)
            for r in t.r:
                if r[0] == eng:
                    continue
                add(r)
        out = []
        wd = self.waited[eng]
        for k, v in ev.items():
            if k == "pe" and eng == "pe":
                continue
            if wd.get(k, 0) >= v:
                continue
            wd[k] = v
            out.append((k, v))
        return out

    def op(self, eng, fn, r=(), w=()):
        waits = self._deps(eng, r, w)
        self.cnt[eng] += 1
        seq = self.cnt[eng]
        s_self = self.sem(eng)
        wl = [(self.sem(k), v) for k, v in waits]

        def emit(e, wl=wl, fn=fn, s_self=s_self):
            for s, v in wl:
                e.wait_ge(s, v)
            fn(e).then_inc(s_self, 1)
        self.q[eng].append(emit)
        self.phases[eng].append(self.phase)
        evt = (eng, seq)
        for t in r:
            t.r.append(evt)
        for t in w:
            t.w = evt
            t.r = []
        return evt

    def dma(self, eng, out, in_, r=(), w=(), key=None, final=False, **kw):
        waits = self._deps(eng, r, w)
        semkey = ("dma", key)
        s = self.sem(semkey)
        self.semvals[semkey] += 16
        val = self.semvals[semkey]
        wl = [(self.sem(k), v) for k, v in waits]

        def emit(e, wl=wl, s=s):
            for ss, v in wl:
                e.wait_ge(ss, v)
            e.dma_start(out=out, in_=in_, **kw).then_inc(s, 16)
        self.q[eng].append(emit)
        evt = (semkey, val)
        for t in r:
            t.r.append(evt)
        for t in w:
            t.w = evt
            t.r = []
        if final:
            self.pending_out.append(evt)
        return evt

    def finish(self):
        ev = {}
        for k, v in self.pending_out:
            ev[k] = max(ev.get(k, 0), v)
        wl = [(self.sem(k), v) for k, v in ev.items()]

        def emit(e):
            for s, v in wl:
                e.wait_ge(s, v)
        self.q["sp"].append(emit)
        nc = self.nc
        with nc.Block() as block:
            @block.tensor
            def _(e):
                for f in self.q["pe"]:
                    f(e)

            @block.scalar
            def _(e):
                for f in self.q["act"]:
                    f(e)

            @block.vector
            def _(e):
                for f in self.q["dve"]:
                    f(e)

            @block.gpsimd
            def _(e):
                for f in self.q["pool"]:
                    f(e)

            @block.sync
            def _(e):
                for f in self.q["sp"]:
                    f(e)
        for cm in reversed(self._stack):
            cm.__exit__(None, None, None)


def build_program():
    nc = bass.Bass("TRN2", target_bir_lowering=False)
    P = Prog(nc)
    cms = []

    def din(name, shape, dt=F32):
        return nc.dram_tensor(name, list(shape), dt, kind="ExternalInput").ap()

    def dout(name, shape):
        return nc.dram_tensor(name, list(shape), F32, kind="ExternalOutput").ap()

    def sb(name, shape, dt=F32):
        cm = nc.sbuf_tensor(name, list(shape), dt)
        t = cm.__enter__()
        cms.append(cm)
        return t

    xp = din("xp", [SEQ, D])
    xs = din("xs", [NSEQ * DSEQ, D])
    meta = din("meta", [NMETA, D])
    wst = din("wst", [L, 128, WTOT])
    lnbc = din("lnbc", [5, 2, 128, D])
    lncol = din("lncol", [128, 5, 2, 8])
    small = din("small", [128, L, 80])
    rowp = din("rowp", [1, L * 1280])
    wa2 = din("wa2", [16, L, 256])
    b1col = din("b1col", [128, L, 32])
    sg_in = din("sg_in", [L, 2, 64, NSEQ, 2, 128])
    sr_in = din("sr_in", [L, 2, 64, NSEQ, 2, 128])
    ss_in = din("ss_in", [L, 2, 64, NSEQ, 4, 64])
    sc_in = din("sc_in", [L, 128, 8, NSEQ, 3])
    cst = {}
    cshape = {"ca_b": [128, 128], "sl_b": [128, 128], "ca_s": [128, 128], "sl_s": [128, 128],
              "ident": [128, 128], "smtok": [128, 16], "smT": [128, 16, 64],
              "rtab": [NG, 2, 64, 2, 4, 512], "regend": [64, 3, 4]}
    for n in CONST_NAMES:
        cst[n] = din("c_" + n, cshape[n])
    yp = dout("yp", [SEQ, D])
    ys = dout("ys", [NSEQ * DSEQ, D])
    sgp = dout("sgp", [L, 64, 4, 128])
    srp = dout("srp", [L, 64, 4, 128])
    ssp = dout("ssp", [L, 64, 8, 64])
    scp = dout("scp", [L, 3, 768])
    sgs = dout("sgs", [L, 2, 64, NSEQ, 2, 128])
    srs = dout("srs", [L, 2, 64, NSEQ, 2, 128])
    sss = dout("sss", [L, 2, 64, NSEQ, 4, 64])
    scs = dout("scs", [L, 3, NSEQ, 768])

    pf = []
    for i in range(6):
        cm = nc.psum_tensor("pf%d" % i, [128, 512], F32)
        pf.append((cm.__enter__(), Tok()))
        cms.append(cm)
    pb = []
    for i in range(2):
        cm = nc.psum_tensor("pb%d" % i, [128, 1024], BF16)
        pb.append((cm.__enter__(), Tok()))
        cms.append(cm)
    rr = {"f": 0, "b": 0}

    def PF():
        rr["f"] = (rr["f"] + 1) % 4
        return pf[rr["f"]]

    def PL(i):
        return pf[4 + i]

    def PB():
        rr["b"] = (rr["b"] + 1) % 2
        return pb[rr["b"]]

    ca_b = sb("ca_b", [128, 128]); sl_b = sb("sl_b", [128, 128])
    ca_s = sb("ca_s", [128, 128]); sl_s = sb("sl_s", [128, 128])
    identb = sb("identb", [128, 128], BF16)
    smtok = sb("smtok", [128, 16]); smT = sb("smT", [128, 16, 64], BF16)
    regend = sb("regend", [64, 3, 4])
    ones_f = sb("ones_f", [128, 128]); ones_b = sb("ones_b", [128, 128], BF16)
    lncol_t = sb("lncol_t", [128, 5, 2, 8])
    small_t = sb("small_t", [128, L, 80])
    rowhl = sb("rowhl", [33, L, 1280], BF16)
    wa2_t = sb("wa2_t", [16, L, 256], BF16)
    b1col_t = sb("b1col_t", [128, L, 32])
    aneg = sb("aneg", [128, L, 8])
    dmat = sb("dmat", [128, 8, 128], BF16); t_dmat = Tok()
    tc_ = Tok()
    wslot = [(sb("wslot%d" % i, [128, 4096], BF16), Tok()) for i in range(NSLOT)]
    xT = sb("xT", [128, 8, 512], BF16); t_xT = Tok()
    xtok = sb("xtok", [128, 4, D]); t_xtok = [Tok() for _ in range(4)]
    yT = sb("yT", [128, 12, 512], BF16); t_yT = [Tok() for _ in range(3)]
    mT = sb("mT", [128, 8, 512], BF16); t_mT = Tok()
    big = sb("big", [128, 16384], BF16)
    hid = big[:].rearrange("p (f t) -> p f t", f=32); t_hid = [Tok() for _ in range(8)]
    xraw = big[:, 0:8240].bitcast(F32).rearrange("p (c t) -> p c t", c=8); t_xraw = Tok(); t_xraw8 = [Tok() for _ in range(8)]
    cacc = big[:, 8240:10288].bitcast(F32).rearrange("p (c t) -> p c t", c=2); t_cacc2 = [Tok(), Tok()]
    qTm = big[0:64, 10288:12336].rearrange("p (a j t) -> p a j t", a=2, j=16); t_qTm = Tok()
    cTm = big[0:64, 12336:13360].rearrange("p (j t) -> p j t", j=16); t_cTm = Tok()
    xsin = big[:, 13360:15152].bitcast(F32).rearrange("p (c j t) -> p c j t", c=8, j=16); t_xsin = Tok()
    lnrt = sb("lnrt", [128, 4096])
    lnw_bc = lnrt[:, 0:1024]; lnb_bc = lnrt[:, 1024:2048]; t_lnbc = Tok()
    rtab = lnrt[:].rearrange("p (a b c) -> p a b c", a=4, b=2); t_rtab = t_lnbc
    scrA = sb("scrA", [128, 2048]); t_qk = Tok()
    _sab = scrA[0:64, :].bitcast(BF16)
    qTf = _sab[:, 0:2048].rearrange("p (a t) -> p a t", a=4); kTf = _sab[:, 2048:4096].rearrange("p (a t) -> p a t", a=4)
    lhd = scrA[:, 0:1024].rearrange("p (h t) -> p h t", h=8); dec = scrA[:, 1024:2048].rearrange("p (h t) -> p h t", h=8)
    t_lhd = Tok(); t_dec = Tok()
    qTb = sb("qTb", [64, 4, 512], BF16); kTb = sb("kTb", [64, 4, 512], BF16); t_qkb = Tok()
    aTb = sb("aTb", [16, 512], BF16); t_aT = Tok()
    eG_a = sb("eG", [64, 4, 128]); eGn_a = sb("eGn", [64, 4, 128]); t_eG_a = Tok(); t_eG = t_eG_a
    ktok = sb("ktok", [128, 256], BF16); t_ktok = Tok()
    vtok2 = sb("vtok2", [128, 2, 512], BF16); t_vtok2 = [Tok(), Tok()]
    srt2 = sb("srt2", [128, 2, 512]); t_srt2 = [Tok(), Tok()]
    srt = srt2[:, 0, :]; t_srt = t_srt2[0]
    egp_all = sb("egp_all", [64, 4, 4]); egs_all = sb("egs_all", [64, 16, 4]); t_egp = Tok()
    attb = sb("attb", [128, 8, 128], BF16); t_att = Tok()
    ysb = sb("ysb", [128, 512]); t_ysb = Tok()
    ybf = sb("ybf", [128, 512], BF16); t_ybf = Tok()
    ssq = sb("ssq", [128, 8]); t_ssq = Tok()
    junkA = sb("junkA", [128, 512]); junkB = sb("junkB", [128, 512]); t_junkA = Tok(); t_junkB = Tok()
    junk = junkA; t_junk = t_junkA
    xcT = sb("xcT", [128, 8, 512], BF16); t_xcT = Tok()
    xstok = sb("xstok", [128, 512], BF16); btok = sb("btok", [128, 128], BF16); t_xstok = Tok()
    dtt_all = sb("dtt_all", [128, 4, 8]); gss_all = sb("gss_all", [128, 4, 8]); t_dt4 = [Tok() for _ in range(4)]
    eGt_all = sb("eGt_all", [128, 4, 8]); wsd_all = sb("wsd_all", [128, 4, 8]); eGe_all = sb("eGe_all", [128, 4, 8])
    gm = sb("gm", [128, 16, 8]); t_gm = Tok()
    t_eGt4 = [Tok() for _ in range(4)]
    eGe = sb("eGe", [128, 16, 8]); t_eGe = Tok()
    m2 = sb("m2", [128, 8, 128]); t_m2 = Tok()
    _m2f = m2[0:64].rearrange("p h t -> p (h t)")
    eG_b = _m2f[:, 0:512].rearrange("p (h t) -> p h t", h=4); eGn_b = _m2f[:, 512:1024].rearrange("p (h t) -> p h t", h=4)
    xw = sb("xw", [128, 512], BF16); t_xw = Tok()
    hist = sb("hist", [128, L, 8, 3]); t_hist = [Tok() for _ in range(L)]
    sc3 = sb("sc3", [16, 768]); t_sc3 = Tok()
    Sg = sb("Sg", [64, L, 4, 128]); Sgb = sb("Sgb", [64, L, 4, 128], BF16)
    Sr = sb("Sr", [64, L, 4, 128]); Srb = sb("Srb", [64, L, 4, 128], BF16)
    Ss = sb("Ss", [64, L, 8, 64]); Ssb = sb("Ssb", [64, L, 8, 64], BF16)
    t_Sg = [Tok() for _ in range(L)]; t_Sr = [Tok() for _ in range(L)]; t_Ss = [Tok() for _ in range(L)]
    S0 = lnrt; t_S0 = t_lnbc
    S0b = mT[:].rearrange("p a b -> p (a b)"); t_S0b = t_mT
    ktm = S0b[0:64, 0:2048].rearrange("p (j n) -> p j n", j=16); t_ktm = t_S0b
    btm = S0b[0:64, 0:1024].rearrange("p (j n) -> p j n", j=16); t_btm = t_S0b

    def mm(out, lhsT, rhs, start, stop, r, w):
        P.op("pe", lambda e: e.matmul(out, lhsT, rhs, start=start, stop=stop), r=r, w=w)

    def tr(out, in_, r, w):
        n = in_.shape[0]
        P.op("pe", lambda e: e.transpose(out, in_, identb[:n, :n]), r=list(r) + [tc_], w=w)

    def act(out, in_, func, r, w, bias=None, scale=1.0, accum=None):
        kw = {}
        if bias is not None:
            kw["bias"] = bias
        if accum is not None:
            kw["accum_out"] = accum
        P.op("act", lambda e: e.activation(out=out, in_=in_, func=func, scale=scale, **kw), r=r, w=w)

    def tt(eng, out, in0, in1, op, r, w):
        P.op(eng, lambda e: e.tensor_tensor(out=out, in0=in0, in1=in1, op=op), r=r, w=w)

    def ts(eng, out, in0, s1, s2, op0, op1, r, w):
        if op1 is None:
            P.op(eng, lambda e: e.tensor_scalar(out=out, in0=in0, scalar1=s1, scalar2=None, op0=op0), r=r, w=w)
        else:
            P.op(eng, lambda e: e.tensor_scalar(out=out, in0=in0, scalar1=s1, scalar2=s2, op0=op0, op1=op1), r=r, w=w)

    def stt(out, in0, sc, in1, op0, op1, r, w):
        P.op("dve", lambda e: e.scalar_tensor_tensor(out=out, in0=in0, scalar=sc, in1=in1, op0=op0, op1=op1), r=r, w=w)

    def cp(eng, out, in_, r, w):
        if eng == "act":
            P.op("act", lambda e: e.copy(out=out, in_=in_), r=r, w=w)
        else:
            P.op(eng, lambda e: e.tensor_copy(out=out, in_=in_), r=r, w=w)

    def bc(ap, shape):
        return ap.to_broadcast(list(shape))

    for nm, t in (("ca_b", ca_b), ("sl_b", sl_b), ("ca_s", ca_s), ("sl_s", sl_s), ("smtok", smtok),
                  ("regend", regend)):
        P.dma("sp", t[:], cst[nm], w=[tc_], key="const")
    P.dma("pool", identb[:], cst["ident"], w=[tc_], key="constp")
    P.dma("pool", smT[:], cst["smT"], w=[tc_], key="constp")
    P.dma("sp", lncol_t[:], lncol, w=[tc_], key="const")
    P.dma("sp", small_t[:], small, w=[tc_], key="const")
    P.dma("pool", wa2_t[:], wa2, w=[tc_], key="constp")
    P.dma("sp", b1col_t[:], b1col, w=[tc_], key="const")
    P.op("dve", lambda e: e.memset(ones_f[:], 1.0), w=[tc_])
    P.op("dve", lambda e: e.memset(big[:], 0.0), w=t_hid + [t_xraw, t_xsin, t_qTm, t_cTm] + t_cacc2)
    P.op("dve", lambda e: e.memset(ones_b[:], 1.0), w=[tc_])
    P.op("dve", lambda e: e.memset(hist[:], 0.0), w=t_hist)
    for l in range(L):
        P.op("dve", lambda e, l=l: e.memset(Sg[:, l], 0.0), w=[t_Sg[l]])
        P.op("dve", lambda e, l=l: e.memset(Sgb[:, l], 0.0), w=[t_Sg[l]])
        P.op("dve", lambda e, l=l: e.memset(Sr[:, l], 0.0), w=[t_Sr[l]])
        P.op("dve", lambda e, l=l: e.memset(Srb[:, l], 0.0), w=[t_Sr[l]])
        P.op("dve", lambda e, l=l: e.memset(Ss[:, l], 0.0), w=[t_Ss[l]])
        P.op("dve", lambda e, l=l: e.memset(Ssb[:, l], 0.0), w=[t_Ss[l]])
    NR = L * 1280
    rhl = rowhl[:].rearrange("p l n -> p (l n)")
    P.dma("sp", S0[0:1, 0:NR], rowp, w=[t_S0], key="S0")
    P.dma("sp", S0[32:33, 0:NR], rowp, w=[t_S0], key="S0")
    P.op("dve", lambda e: e.memset(rowhl[:], 0.0), w=[tc_])
    cp("dve", rhl[0:1, :], S0[0:1, 0:NR], [t_S0], [tc_])
    cp("dve", rhl[32:33, :], S0[32:33, 0:NR], [t_S0], [tc_])
    tt("dve", S0[32:33, 0:NR], S0[32:33, 0:NR], rhl[32:33, :], ALU.subtract, [t_S0, tc_], [t_S0])
    cp("dve", rhl[32:33, :], S0[32:33, 0:NR], [t_S0], [tc_])
    for l in range(L):
        act(aneg[:, l, :], small_t[:, l, 0:8], AF.Exp, [tc_], [tc_])
        ts("dve", aneg[:, l, :], aneg[:, l, :], -1.0, None, ALU.mult, None, [tc_], [tc_])
    identf = sb("identf", [128, 128])
    P.dma("sp", identf[:], cst["ident"], w=[tc_], key="const")

    wseq = [(l_, nm) for _gi in range(NG) for l_ in range(L) for (nm, _k, _n) in WT]
    wstate = {"issued": 0, "cur": 0}

    wbf = nc.dram_tensor("wbf", [L, 128, WTOT], BF16, kind="Internal").ap()
    t_wbf = {}
    NT_PER = len(WT)

    def _issue(idx):
        l_, nm = wseq[idx]
        off, kc, ncl = WOFF[nm]
        t, tk = wslot[idx % NSLOT]
        n = kc * ncl
        gi_ = idx // (L * NT_PER)
        if gi_ == 0:
            P.dma("pool", t[:, 0:n], wst[l_, :, off:off + n], w=[tk], key=("w", idx % NSLOT), max_dma_last_dim=4096)
            tkd = Tok()
            t_wbf[(l_, nm)] = tkd
            P.dma("sp", wbf[l_, :, off:off + n], t[:, 0:n], r=[tk], w=[tkd], key=("wbw", idx % NSLOT))
        else:
            P.dma("sp", t[:, 0:n], wbf[l_, :, off:off + n], r=[t_wbf[(l_, nm)]], w=[tk], key=("wr", idx % NSLOT))

    def wload(l, nm, live=0):
        idx = wstate["cur"]
        assert wseq[idx] == (l, nm), (wseq[idx], l, nm)
        while wstate["issued"] < min(len(wseq), idx - live + NSLOT):
            _issue(wstate["issued"])
            wstate["issued"] += 1
        assert wstate["issued"] > idx
        wstate["cur"] += 1
        off, kc, ncl = WOFF[nm]
        t, tk = wslot[idx % NSLOT]
        return t[:, 0:kc * ncl].rearrange("p (k n) -> p k n", k=kc), tk

    xhb4 = big[:, 0:4096].rearrange("p (b d) -> p b d", b=4)
    t_xhb4 = [t_hid[0], t_hid[1], t_xraw]
    lnst4 = sb("lnst4", [128, 4, 2, 6]); lnmv4 = sb("lnmv4", [128, 4, 4]); t_ln4 = [Tok() for _ in range(4)]

    def layernorm(g, lni):
        P.dma("sp", lnw_bc, lnbc[lni, 0], w=[t_lnbc], key="lnbc")
        P.dma("sp", lnb_bc, lnbc[lni, 1], w=[t_lnbc], key="lnbc")
        blks = list(enumerate(g))
        for bi, (kind, nt, c0, pos0) in blks:
            z = xtok[:nt, bi, :]
            for hh in range(2):
                P.op("dve", lambda e, hh=hh, z=z, nt=nt, bi=bi: e.bn_stats(out=lnst4[:nt, bi, hh, :], in_=z[:, hh * 512:(hh + 1) * 512]),
                     r=[t_xtok[bi]], w=[t_ln4[bi]])
            P.op("dve", lambda e, nt=nt, bi=bi: e.bn_aggr(out=lnmv4[:nt, bi, 0:2], in_=lnst4[:nt, bi].rearrange("p a b -> p (a b)")),
                 r=[t_ln4[bi]], w=[t_ln4[bi]])
        for bi, (kind, nt, c0, pos0) in blks:
            act(lnmv4[:nt, bi, 2:3], lnmv4[:nt, bi, 1:2], AF.Ln, [t_ln4[bi]], [t_ln4[bi]], bias=1e-5)
            act(lnmv4[:nt, bi, 3:4], lnmv4[:nt, bi, 2:3], AF.Exp, [t_ln4[bi]], [t_ln4[bi]], scale=-0.5)
        for bi, (kind, nt, c0, pos0) in blks:
            ts("dve", lnmv4[:nt, bi, 2:3], lnmv4[:nt, bi, 0:1], lnmv4[:nt, bi, 3:4], -1.0, ALU.mult, ALU.mult,
               [t_ln4[bi]], [t_ln4[bi]])
        for bi, (kind, nt, c0, pos0) in blks:
            z = xtok[:nt, bi, :]
            act(z, z, AF.Identity, [t_xtok[bi], t_ln4[bi]], [t_xtok[bi]],
                bias=lnmv4[:nt, bi, 2:3], scale=lnmv4[:nt, bi, 3:4])
        for bi, (kind, nt, c0, pos0) in blks:
            z = xtok[:nt, bi, :]
            tt("dve", z, z, lnw_bc[:nt, :], ALU.mult, [t_xtok[bi], t_lnbc], [t_xtok[bi]])
            tt("dve", z, z, lnb_bc[:nt, :], ALU.add, [t_xtok[bi], t_lnbc], [t_xtok[bi]])
        for bi, (kind, nt, c0, pos0) in blks:
            cp("act", xhb4[:nt, bi, :], xtok[:nt, bi, :], [t_xtok[bi]], t_xhb4)
        for bi, (kind, nt, c0, pos0) in blks:
            pt, tp = PB()
            for k in range(8):
                tr(pt[:, k * 128:k * 128 + nt], xhb4[:nt, bi, k * 128:(k + 1) * 128], t_xhb4, [tp])
            cp("act", xT[:, :, c0:c0 + nt], pt[:].rearrange("p (k t) -> p k t", k=8)[:, :, 0:nt], [tp], [t_xT])

    def proj_feat(W, tw, wc0, M, src, tsrc, nk, ncols, koff=0):
        pt, tp = PF()
        for k in range(nk):
            mm(pt[:M, :ncols], W[:, k, wc0:wc0 + M], src[:, koff + k, 0:ncols], k == 0, k == nk - 1, [tw, tsrc], [tp])
        return pt, tp

    def proj_tok(W, tw, wc0, N, c0, nt):
        pt, tp = PF()
        for k in range(8):
            mm(pt[:nt, :N], xT[:, k, c0:c0 + nt], W[:, k, wc0:wc0 + N], k == 0, k == 7, [tw, t_xT], [tp])
        return pt, tp

    KIND = {"B": 0, "M": 1, "S": 2}

    def lin_block(l, kind, nt, c0, Sm, Smb, tS, egp, egs, ybase, ycol_scale, st_in, st_out_prompt, st_out_sample,
                  last_prompt_block, vtok, t_vtok, srt, t_srt, hook=None):
        ca = ca_s if kind == "S" else ca_b
        pt, tp = PB()
        for h in range(4):
            tr(pt[:nt, h * 64:(h + 1) * 64], kTb[:, h, c0:c0 + nt], [t_qkb], [tp])
        cp("act", ktok[:nt, :], pt[:nt, 0:256], [tp], [t_ktok])
        chk("lb_ktok")
        pa, tpa = PF()
        for h in range(4):
            mm(pa[:nt, h * 128:h * 128 + nt], kTb[:, h, c0:c0 + nt], qTb[:, h, c0:c0 + nt], True, True, [t_qkb], [tpa])
        chk("lb_attmm")
        pav = pa[:].rearrange("p (h t) -> p h t", h=4)[:nt, :, :nt]
        tt("dve", attb[:nt, 0:4, :nt], pav, bc(ca[:nt, :nt].unsqueeze(1), [nt, 4, nt]), ALU.mult, [tpa, tc_], [t_att])
        chk("lb_att")
        po, tpo = PL(0)
        S0v = S0[0:64, :].rearrange("p (j a v) -> p j a v", j=16, a=2)
        S0bv = S0b[0:64, :].rearrange("p (j a v) -> p j a v", j=16, a=2)
        if kind != "S":
            for h in range(4):
                mm(po[:nt, h * 128:(h + 1) * 128], attb[:nt, h, :nt], vtok[:nt, h * 128:(h + 1) * 128], True, False,
                   [t_att, t_vtok], [tpo])
                mm(po[:nt, h * 128:(h + 1) * 128], qTb[:, h, c0:c0 + nt], Smb[:, l, h, :], False, True, [t_qkb, tS], [tpo])
        else:
            for p in range(2):
                P.dma("sp", S0[0:64, :], st_in[l, p].rearrange("p j a v -> p (j a v)"), w=[t_S0], key="S0")
                P.dma("pool", S0b[0:64, :], st_in[l, p].rearrange("p j a v -> p (j a v)"), w=[t_S0b], key="S0b",
                      max_dma_last_dim=4096)
                tt("dve", qTm[:], bc(qTb[:, 2 * p:2 * p + 2, c0:c0 + 64].unsqueeze(2), [64, 2, 16, 64]),
                   bc(smT[0:64].unsqueeze(1), [64, 2, 16, 64]), ALU.mult, [t_qkb, tc_], [t_qTm])
                for h2 in range(2):
                    h = 2 * p + h2
                    mm(po[:nt, h * 128:(h + 1) * 128], attb[:nt, h, :nt], vtok[:nt, h * 128:(h + 1) * 128], True, False,
                       [t_att, t_vtok], [tpo])
                    for j in range(16):
                        mm(po[:nt, h * 128:(h + 1) * 128], qTm[:, h2, j, :], S0bv[:, j, h2, :], False, j == 15,
                           [t_qTm, t_S0b], [tpo])
                tt("dve", ktm[:], bc(ktok[:64, p * 128:(p + 1) * 128].unsqueeze(1), [64, 16, 128]),
                   bc(smtok[:64, :].unsqueeze(2), [64, 16, 128]), ALU.mult, [t_ktok, tc_], [t_ktm])
                for rnd in range(2):
                    banks = []
                    for q in range(4):
                        ps_, tps = PF()
                        banks.append((ps_, tps))
                        for jj in range(2):
                            j = rnd * 8 + q * 2 + jj
                            for h2 in range(2):
                                h = 2 * p + h2
                                o_ = (jj * 2 + h2) * 128
                                mm(ps_[:64, o_:o_ + 128], ktm[:, j, h2 * 64:(h2 + 1) * 64], vtok[:64, h * 128:(h + 1) * 128],
                                   True, True, [t_ktm, t_vtok], [tps])
                    for q in range(4):
                        ps_, tps = banks[q]
                        j0 = rnd * 8 + q * 2
                        tt("dve", S0v[:, j0:j0 + 2], S0v[:, j0:j0 + 2],
                           ps_[:64, :].rearrange("p (j a v) -> p j a v", j=2, a=2), ALU.add, [t_S0, tps], [t_S0])
                        tt("dve", S0v[:, j0:j0 + 2], S0v[:, j0:j0 + 2],
                           bc(egs[:, j0:j0 + 2, 2 * p:2 * p + 2].unsqueeze(3), [64, 2, 2, 128]), ALU.mult,
                           [t_S0, t_eG, tc_], [t_S0])
                P.dma("sp", st_out_sample[l, p].rearrange("p j a v -> p (j a v)"), S0[0:64, :], r=[t_S0], key="S0out",
                      final=True)
        chk("lb_o")
        if kind != "S":
            ps_, tps = PF()
            for h in range(4):
                mm(ps_[:64, h * 128:(h + 1) * 128], ktok[:nt, h * 64:(h + 1) * 64], vtok[:nt, h * 128:(h + 1) * 128],
                   True, True, [t_ktok, t_vtok], [tps])
            tt("dve", Sm[:, l], Sm[:, l], ps_[:64, :].rearrange("p (h v) -> p h v", h=4), ALU.add, [tS, tps], [tS])
            tt("dve", Sm[:, l], Sm[:, l], bc(egp.unsqueeze(2), [64, 4, 128]), ALU.mult, [tS, t_eG, t_egp, tc_], [tS])
            cp("dve", Smb[:, l], Sm[:, l], [tS], [tS])
            if last_prompt_block:
                P.dma("sp", st_out_prompt[l], Sm[:, l], r=[tS], key="pfinal", final=True)
        hk = hook() if hook is not None else None
        if hk is not None:
            next(hk)
        act(junk[:nt, :], po[:nt, :], AF.Square, [tpo], [t_junk])
        P.op("dve", lambda e: e.reduce_sum(out=ssq[:nt, 0:4], in_=junk[:nt, :].rearrange("p (h v) -> p h v", h=4), axis=AX.X),
             r=[t_junk], w=[t_ssq])
        act(ssq[:nt, 0:4], ssq[:nt, 0:4], AF.Ln, [t_ssq], [t_ssq], bias=1e-6, scale=1.0 / 128.0)
        act(ssq[:nt, 4:8], ssq[:nt, 0:4], AF.Exp, [t_ssq], [t_ssq], scale=-0.5)
        tt("dve", ysb[:nt, :].rearrange("p (h v) -> p h v", h=4), po[:nt, :].rearrange("p (h v) -> p h v", h=4),
           bc(ssq[:nt, 4:8].unsqueeze(2), [nt, 4, 128]), ALU.mult, [tpo, t_ssq], [t_ysb])
        tt("dve", ybf[:nt, :], ysb[:nt, :], srt[:nt, :], ALU.mult, [t_ysb, t_srt], [t_ybf])
        if hk is not None:
            for _ in hk:
                pass
        pt, tp = PB()
        for c in range(4):
            tr(pt[:, c * 128:c * 128 + nt], ybf[:nt, c * 128:(c + 1) * 128], [t_ybf], [tp])
        pv = pt[:].rearrange("p (c t) -> p c t", c=8)[:, 0:4, 0:nt]
        if ycol_scale is not None:
            ts("dve", yT[:, ybase:ybase + 4, c0:c0 + nt], pv, ycol_scale, None, ALU.mult, None, [tp, tc_],
               [t_yT[ybase // 4]])
        else:
            cp("act", yT[:, ybase:ybase + 4, c0:c0 + nt], pv, [tp], [t_yT[ybase // 4]])
        chk("lb_yT")

    class _Stop(Exception):
        pass

    def chk(name):
        if STOP is not None and name == STOP:
            raise _Stop()

    def ph(name):
        P.phase = name

    def main():
        for gi, g in enumerate(GROUPS):
            ncols = gcols(g)
            nblk = len(g)
            last_group = gi == NG - 1
            for bi, (kind, nt, c0, pos0) in enumerate(g):
                if kind == "M":
                    src = meta
                elif kind == "S":
                    src = xs
                else:
                    src = xp[pos0 - 16:pos0 - 16 + nt, :]
                P.dma("sp", xtok[:nt, bi, :], src, w=[t_xtok[bi]], key=("xin", bi))
            ph("ln0")
            layernorm(g, 0)
            chk("ln0")
            for l in range(L):
                ph("gla_proj")
                W, tw = wload(l, "gqk")
                for h in range(4):
                    pt, tp = proj_feat(W, tw, h * 64, 64, xT, t_xT, 8, ncols)
                    cp("act", qTf[:, h, :ncols], pt[:64, :ncols], [tp], [t_qk, t_lhd])
                    pt, tp = proj_feat(W, tw, 256 + h * 64, 64, xT, t_xT, 8, ncols)
                    cp("act", kTf[:, h, :ncols], pt[:64, :ncols], [tp], [t_qk, t_dec])
                chk("gla_qk")
                Wa, twa = wload(l, "ga", live=1)
                pt, tp = proj_feat(Wa, twa, 0, 16, xT, t_xT, 8, ncols)
                cp("act", aTb[:, :ncols], pt[:16, :ncols], [tp], [t_aT])
                Wv, twv = wload(l, "gv")
                Wr_, twr = wload(l, "gr", live=1)
                ph("gla_blk")
                for bi, (kind, nt, c0, pos0) in enumerate(g):
                    U = ca_s if kind == "S" else ca_b
                    if bi % 2 == 0:
                        gpr, t_gpr, eG, eGn, t_eG = junkA[:, 0:256], t_junkA, eG_a, eGn_a, t_eG_a
                    else:
                        gpr, t_gpr, eG, eGn, t_eG = junkB[:, 0:256], t_junkB, eG_b, eGn_b, t_m2
                    pa_, tpa_ = PF()
                    mm(pa_[:nt, 0:256], aTb[0:16, c0:c0 + nt], wa2_t[0:16, l, :], True, False, [t_aT, tc_], [tpa_])
                    mm(pa_[:nt, 0:256], ones_b[0:33, :nt], rowhl[0:33, l, 0:256], False, True, [tc_], [tpa_])
                    act(gpr[:nt, :], pa_[:nt, 0:256], AF.Exp, [tpa_], [t_gpr], scale=-1.0)
                    act(gpr[:nt, :], gpr[:nt, :], AF.Ln, [t_gpr], [t_gpr], bias=1.0)
                    pg, tpg = PF()
                    for h in range(4):
                        mm(pg[:64, h * 128:h * 128 + nt], gpr[:nt, h * 64:(h + 1) * 64], U[:nt, :nt], True, True,
                           [t_gpr, tc_], [tpg])
                    pgv = pg[:].rearrange("p (a t) -> p a t", a=4)[:64, :, 0:nt]
                    act(eG[:, :, :nt], pgv, AF.Exp, [tpg], [t_eG], scale=-1.0 / 16.0)
                    act(eGn[:, :, :nt], pgv, AF.Exp, [tpg], [t_eG], scale=1.0 / 16.0)
                    stt(qTb[:, :, c0:c0 + nt], qTf[:, :, c0:c0 + nt], 0.125, eG[:, :, :nt], ALU.mult, ALU.mult,
                        [t_qk, t_eG], [t_qkb])
                    tt("dve", kTb[:, :, c0:c0 + nt], kTf[:, :, c0:c0 + nt], eGn[:, :, :nt], ALU.mult, [t_qk, t_eG], [t_qkb])
                    if kind == "S":
                        cp("dve", egs_all[:], eG[:, :, 3:64:4].rearrange("p h j -> p j h"), [t_eG], [t_egp])
                    else:
                        cp("dve", egp_all[:, bi, :], eG[:, :, nt - 1], [t_eG], [t_egp])

                def gla_pre(bi):
                    kind, nt, c0, pos0 = g[bi]
                    i2 = bi % 2
                    pt, tp = proj_tok(Wr_, twr, 0, 512, c0, nt)
                    pv_, tpv_ = proj_tok(Wv, twv, 0, 512, c0, nt)
                    yield
                    act(junkB[:nt, :], pt[:nt, :], AF.Exp, [tp], [t_junkB], scale=-1.0)
                    act(junkB[:nt, :], junkB[:nt, :], AF.Ln, [t_junkB], [t_junkB], bias=1.0)
                    act(junkB[:nt, :], junkB[:nt, :], AF.Exp, [t_junkB], [t_junkB], scale=-1.0)
                    cp("act", vtok2[:nt, i2, :], pv_[:nt, :], [tpv_], [t_vtok2[i2]])
                    tt("dve", srt2[:nt, i2, :], pt[:nt, :], junkB[:nt, :], ALU.mult, [tp, t_junkB], [t_srt2[i2]])
                for _ in gla_pre(0):
                    pass
                for bi, (kind, nt, c0, pos0) in enumerate(g):
                    i2 = bi % 2
                    lastp = last_group and bi == nblk - 1
                    hook = (lambda bi=bi: gla_pre(bi + 1)) if bi + 1 < nblk else None
                    lin_block(l, kind, nt, c0, Sg, Sgb, t_Sg[l], egp_all[:, bi, :], egs_all[:], 0, small_t[:, l, 68:69],
                              sg_in, sgp, sgs, lastp, vtok2[:, i2, :], t_vtok2[i2], srt2[:, i2, :], t_srt2[i2], hook)
                    chk("gla_b%d" % bi)
                chk("gla")
                ph("ssd_proj")
                tt("dve", dmat[:], bc(identf[:].unsqueeze(1), [128, 8, 128]),
                   bc(small_t[:, l, 16:24].unsqueeze(2), [128, 8, 128]), ALU.mult, [tc_], [t_dmat])
                Wx0, twx0 = wload(l, "sx0")
                Wx1, twx1 = wload(l, "sx1", live=1)
                has_s = any(b[0] == "S" for b in g)
                npr = ncols - (64 if has_s else 0)
                CP_ = [128, 128, 128, 128, 64, 64, 64, 64]
                cp("dve", xraw[:, :, 0:3], hist[:, l], [t_hist[l]], t_xraw8 + [t_xraw])
                if has_s:
                    P.dma("sp", junkB[:, 0:384], sc_in[l].rearrange("p c j i -> p (c j i)"), w=[t_junkB], key="xsin")
                    cp("dve", xsin[:, :, :, 0:3], junkB[:, 0:384].rearrange("p (c j i) -> p c j i", c=8, j=16),
                       [t_junkB], [t_xsin])

                def silu_chunk(c):
                    pc = CP_[c]
                    act(xcT[:pc, c, :ncols], cacc[:pc, c % 2, :ncols], AF.Silu, [t_cacc2[c % 2]], [t_xcT])
                for c in range(8):
                    pc = CP_[c]
                    txr = t_xraw8[c]
                    if c < 4:
                        pt, tp = proj_feat(Wx0, twx0, c * 128, 128, xT, t_xT, 8, ncols)
                    else:
                        pt, tp = proj_feat(Wx1, twx1, (c - 4) * 64, 64, xT, t_xT, 8, ncols)
                    cp("act", xraw[:pc, c, 3:3 + ncols], pt[:pc, :ncols], [tp], [txr])
                    cw = lambda i, c=c: small_t[:CP_[c], l, 32 + c * 4 + i:33 + c * 4 + i]
                    ca_ = cacc[:pc, c % 2, :]
                    tca = t_cacc2[c % 2]
                    act(ca_[:, 0:npr], xraw[:pc, c, 0:npr], AF.Identity, [txr, tc_], [tca],
                        bias=small_t[:pc, l, 24 + c:25 + c], scale=cw(0))
                    for i in range(1, 4):
                        stt(ca_[:, 0:npr], xraw[:pc, c, i:i + npr], cw(i), ca_[:, 0:npr], ALU.mult, ALU.add,
                            [txr, tc_, tca], [tca])
                    if has_s:
                        cp("dve", xsin[:pc, c, :, 3:7], xraw[:pc, c, 3 + npr:3 + ncols].rearrange("p (j t) -> p j t", j=16),
                           [txr], [t_xsin])
                        cv = ca_[:, npr:ncols].rearrange("p (j t) -> p j t", j=16)
                        act(cv, xsin[:pc, c, :, 0:4], AF.Identity, [t_xsin, tc_], [tca],
                            bias=small_t[:pc, l, 24 + c:25 + c], scale=cw(0))
                        for i in range(1, 4):
                            stt(cv, xsin[:pc, c, :, i:i + 4], cw(i), cv, ALU.mult, ALU.add, [t_xsin, tc_, tca], [tca])
                    if c > 0:
                        silu_chunk(c - 1)
                silu_chunk(7)
                cp("dve", hist[:, l], xraw[:, :, npr:npr + 3], t_xraw8, [t_hist[l]])
                cs_jobs = []
                if last_group:
                    cs_jobs.append((3, lambda k: xT[:, k, npr - 3:npr], scp[l]))
                if has_s:
                    for i in range(3):
                        cs_jobs.append((16, lambda k, i=i: xT[:, k, npr + i + 1:npr + 64:4], scs[l, i]))
                for (mrows, lvf, dst) in cs_jobs:
                    pc_, tpc = PF()
                    pc2, tpc2 = PF()
                    for k in range(8):
                        mm(pc_[:mrows, 0:512], lvf(k), Wx0[:, k, :], k == 0, k == 7, [t_xT, twx0], [tpc])
                    for k in range(8):
                        mm(pc2[:mrows, 0:256], lvf(k), Wx1[:, k, :], k == 0, k == 7, [t_xT, twx1], [tpc2])
                    cp("act", sc3[:mrows, 0:512], pc_[:mrows, 0:512], [tpc], [t_sc3])
                    cp("act", sc3[:mrows, 512:768], pc2[:mrows, 0:256], [tpc2], [t_sc3])
                    P.dma("sp", dst, sc3[:mrows, :], r=[t_sc3], key="sc3out", final=True)
                Wz, twz = wload(l, "sz")
                Wdt, twdt = wload(l, "sdt", live=1)
                ph("ssd_blk")
                blks = list(enumerate(g))
                V = lambda bi: (dtt_all[:, bi, :], gss_all[:, bi, :], eGt_all[:, bi, :], wsd_all[:, bi, :])
                pds = {}
                for bi, (kind, nt, c0, pos0) in blks:
                    pds[bi] = proj_tok(Wdt, twdt, 0, 8, c0, nt)
                for bi, (kind, nt, c0, pos0) in blks:
                    dtt, gss, eGt, wsd = V(bi)
                    pd, tpd = pds[bi]
                    tt("dve", dtt[:nt, :], pd[:nt, 0:8], small_t[:nt, l, 8:16], ALU.add, [tpd, tc_], [t_dt4[bi]])
                for bi, (kind, nt, c0, pos0) in blks:
                    dtt, gss, eGt, wsd = V(bi)
                    act(dtt[:nt, :], dtt[:nt, :], AF.Exp, [t_dt4[bi]], [t_dt4[bi]])
                    act(dtt[:nt, :], dtt[:nt, :], AF.Ln, [t_dt4[bi]], [t_dt4[bi]], bias=1.0)
                for bi, (kind, nt, c0, pos0) in blks:
                    dtt, gss, eGt, wsd = V(bi)
                    tt("dve", gss[:nt, :], dtt[:nt, :], aneg[:nt, l, :], ALU.mult, [t_dt4[bi], tc_], [t_dt4[bi]])
                    if kind == "S":
                        tt("dve", gm[:nt, :, :], bc(gss[:nt, :].unsqueeze(1), [nt, 16, 8]),
                           bc(smtok[:nt, :].unsqueeze(2), [nt, 16, 8]), ALU.mult, [t_dt4[bi], tc_], [t_gm])
                pqs = {}
                for bi, (kind, nt, c0, pos0) in blks:
                    dtt, gss, eGt, wsd = V(bi)
                    U = ca_s if kind == "S" else ca_b
                    SLm = sl_s if kind == "S" else sl_b
                    nsq = 16 if kind == "S" else 1
                    gmv = gm[:nt, :, :].rearrange("p j h -> p (j h)") if kind == "S" else gss[:nt, :]
                    pq, tpq = PF()
                    pqs[bi] = (pq, tpq)
                    mm(pq[:nt, 0:8], U[:nt, :nt], gss[:nt, :], True, True, [tc_, t_dt4[bi]], [tpq])
                    mm(pq[:nt, 8:16], SLm[:nt, :nt], gss[:nt, :], True, True, [tc_, t_dt4[bi]], [tpq])
                    mm(pq[:, 128:128 + nsq * 8], ones_f[:nt, :], gmv, True, True, [tc_, t_gm, t_dt4[bi]], [tpq])
                for bi, (kind, nt, c0, pos0) in blks:
                    dtt, gss, eGt, wsd = V(bi)
                    pq, tpq = pqs[bi]
                    act(eGt[:nt, :], pq[:nt, 0:8], AF.Exp, [tpq], [t_eGt4[bi]])
                    act(wsd[:nt, :], pq[:nt, 8:16], AF.Exp, [tpq], [t_eGt4[bi]])
                    if kind == "S":
                        act(eGe[:, 0:16, :], pq[:, 128:256].rearrange("p (j h) -> p j h", h=8), AF.Exp, [tpq], [t_eGe])
                    else:
                        act(eGe_all[:, bi, :], pq[:, 128:136], AF.Exp, [tpq], [t_eGe])
                for bi, (kind, nt, c0, pos0) in blks:
                    dtt, gss, eGt, wsd = V(bi)
                    tt("dve", wsd[:nt, :], wsd[:nt, :], dtt[:nt, :], ALU.mult, [t_eGt4[bi], t_dt4[bi]], [t_eGt4[bi]])
                for bi, (kind, nt, c0, pos0) in enumerate(g):
                    U = ca_s if kind == "S" else ca_b
                    SLm = sl_s if kind == "S" else sl_b
                    nsq = 16 if kind == "S" else 1
                    pt, tp = PB()
                    for c in range(4):
                        tr(pt[:nt, c * 128:(c + 1) * 128], xcT[:, c, c0:c0 + nt], [t_xcT], [tp])
                    for gg in range(2):
                        tr(pt[:nt, 512 + gg * 64:512 + (gg + 1) * 64], xcT[0:64, 4 + gg, c0:c0 + nt], [t_xcT], [tp])
                    cp("act", xstok[:nt, :], pt[:nt, 0:512], [tp], [t_xstok])
                    cp("act", btok[:nt, :], pt[:nt, 512:640], [tp], [t_xstok])
                    dtt = dtt_all[:, bi, :]; gss = gss_all[:, bi, :]; eGt = eGt_all[:, bi, :]; wsd = wsd_all[:, bi, :]
                    tt("dve", lhd[:nt, :, :nt], bc(SLm[:nt, :nt].unsqueeze(1), [nt, 8, nt]),
                       bc(gss[:nt, :].unsqueeze(2), [nt, 8, nt]), ALU.mult, [tc_, t_dt4[bi]], [t_lhd, t_qk])
                    tt("dve", m2[:nt, :, :nt], bc(U[:nt, :nt].unsqueeze(1), [nt, 8, nt]),
                       bc(dtt[:nt, :].unsqueeze(2), [nt, 8, nt]), ALU.mult, [tc_, t_dt4[bi]], [t_m2])
                    for half in range(2):
                        pdf, tpdf = PF()
                        for hh in range(4):
                            h = half * 4 + hh
                            mm(pdf[:nt, hh * 128:hh * 128 + nt], lhd[:nt, h, :nt], U[:nt, :nt], True, True,
                               [t_lhd, tc_], [tpdf])
                        act(dec[:nt, half * 4:half * 4 + 4, :nt],
                            pdf[:].rearrange("p (h t) -> p h t", h=4)[:nt, :, :nt], AF.Exp, [tpdf], [t_dec])
                    pz, tpz = proj_tok(Wz, twz, 0, 512, c0, nt)
                    tt("dve", dec[:nt, :, :nt], dec[:nt, :, :nt], m2[:nt, :, :nt], ALU.mult, [t_dec, t_m2], [t_dec])
                    pbc, tpbc = PF()
                    for gg in range(2):
                        mm(pbc[:nt, gg * 128:gg * 128 + nt], xcT[0:64, 4 + gg, c0:c0 + nt], xcT[0:64, 6 + gg, c0:c0 + nt],
                           True, True, [t_xcT], [tpbc])
                    for gg in range(2):
                        tt("dve", attb[:nt, gg * 4:gg * 4 + 4, :nt],
                           bc(pbc[:nt, gg * 128:gg * 128 + nt].unsqueeze(1), [nt, 4, nt]),
                           dec[:nt, gg * 4:gg * 4 + 4, :nt], ALU.mult, [tpbc, t_dec], [t_att])
                    act(junk[:nt, :], pz[:nt, :], AF.Exp, [tpz], [t_junk], scale=-1.0)
                    act(junk[:nt, :], junk[:nt, :], AF.Ln, [t_junk], [t_junk], bias=1.0)
                    act(junk[:nt, :], junk[:nt, :], AF.Exp, [t_junk], [t_junk], scale=-1.0)
                    tt("dve", srt[:nt, :], pz[:nt, :], junk[:nt, :], ALU.mult, [tpz, t_junk], [t_srt])
                    py, tpy = PL(0)
                    for h in range(8):
                        mm(py[:nt, h * 64:(h + 1) * 64], attb[:nt, h, :nt], xstok[:nt, h * 64:(h + 1) * 64], True, False,
                           [t_att, t_xstok], [tpy])
                        mm(py[:nt, h * 64:(h + 1) * 64], dmat[:nt, h, :nt], xstok[:nt, h * 64:(h + 1) * 64], False, True,
                           [t_dmat, t_xstok], [tpy])
                    tt("dve", xw[:nt, :].rearrange("p (h v) -> p h v", h=8), xstok[:nt, :].rearrange("p (h v) -> p h v", h=8),
                       bc(wsd[:nt, :].unsqueeze(2), [nt, 8, 64]), ALU.mult, [t_xstok, t_eGt4[bi]], [t_xw])
                    pi_, tpi = PL(1)
                    S0v = S0[0:64, :].rearrange("p (j a v) -> p j a v", j=16, a=4)
                    S0bv = S0b[0:64, :].rearrange("p (j a v) -> p j a v", j=16, a=4)
                    if kind != "S":
                        for h in range(8):
                            mm(pi_[:nt, h * 64:(h + 1) * 64], xcT[0:64, 6 + h // 4, c0:c0 + nt], Ssb[:, l, h, :], True, True,
                               [t_xcT, t_Ss[l]], [tpi])
                    else:
                        for gg in range(2):
                            P.dma("sp", S0[0:64, :], ss_in[l, gg].rearrange("p j a v -> p (j a v)"), w=[t_S0], key="S0")
                            P.dma("pool", S0b[0:64, :], ss_in[l, gg].rearrange("p j a v -> p (j a v)"), w=[t_S0b], key="S0b",
                                  max_dma_last_dim=4096)
                            tt("dve", cTm[:], bc(xcT[0:64, 6 + gg, c0:c0 + 64].unsqueeze(1), [64, 16, 64]), smT[0:64],
                               ALU.mult, [t_xcT, tc_], [t_cTm])
                            for hh in range(4):
                                h = gg * 4 + hh
                                for j in range(16):
                                    mm(pi_[:nt, h * 64:(h + 1) * 64], cTm[:, j, :], S0bv[:, j, hh, :], j == 0, j == 15,
                                       [t_cTm, t_S0b], [tpi])
                            tt("dve", btm[:], bc(btok[:64, gg * 64:(gg + 1) * 64].unsqueeze(1), [64, 16, 64]),
                               bc(smtok[:64, :].unsqueeze(2), [64, 16, 64]), ALU.mult, [t_xstok, tc_], [t_btm])
                            for rnd in range(2):
                                banks = []
                                for q in range(4):
                                    ps_, tps = PF()
                                    banks.append((ps_, tps))
                                    for jj in range(2):
                                        j = rnd * 8 + q * 2 + jj
                                        mm(ps_[:64, jj * 256:(jj + 1) * 256], btm[:, j, :], xw[:64, gg * 256:(gg + 1) * 256],
                                           True, True, [t_btm, t_xw], [tps])
                                for q in range(4):
                                    ps_, tps = banks[q]
                                    j0 = rnd * 8 + q * 2
                                    tt("dve", S0v[:, j0:j0 + 2], S0v[:, j0:j0 + 2],
                                       bc(eGe[0:64, j0:j0 + 2, gg * 4:gg * 4 + 4].unsqueeze(3), [64, 2, 4, 64]), ALU.mult,
                                       [t_S0, t_eGe], [t_S0])
                                    tt("dve", S0v[:, j0:j0 + 2], S0v[:, j0:j0 + 2],
                                       ps_[:64, :].rearrange("p (j a v) -> p j a v", j=2, a=4), ALU.add, [t_S0, tps], [t_S0])
                            P.dma("sp", sss[l, gg].rearrange("p j a v -> p (j a v)"), S0[0:64, :], r=[t_S0], key="S0out",
                                  final=True)
                    tt("dve", ysb[:nt, :].rearrange("p (h v) -> p h v", h=8), pi_[:nt, :].rearrange("p (h v) -> p h v", h=8),
                       bc(eGt[:nt, :].unsqueeze(2), [nt, 8, 64]), ALU.mult, [tpi, t_eGt4[bi]], [t_ysb])
                    tt("dve", ysb[:nt, :], ysb[:nt, :], py[:nt, :], ALU.add, [t_ysb, tpy], [t_ysb])
                    tt("dve", ysb[:nt, :], ysb[:nt, :], srt[:nt, :], ALU.mult, [t_ysb, t_srt], [t_ysb])
                    act(junk[:nt, :], ysb[:nt, :], AF.Square, [t_ysb], [t_junk])
                    P.op("dve", lambda e, nt=nt: e.reduce_sum(out=ssq[:nt, 0:2], in_=junk[:nt, :].rearrange("p (g v) -> p g v", g=2), axis=AX.X),
                         r=[t_junk], w=[t_ssq])
                    act(ssq[:nt, 0:2], ssq[:nt, 0:2], AF.Ln, [t_ssq], [t_ssq], bias=1e-6, scale=1.0 / 256.0)
                    act(ssq[:nt, 4:6], ssq[:nt, 0:2], AF.Exp, [t_ssq], [t_ssq], scale=-0.5)
                    tt("dve", ybf[:nt, :].rearrange("p (g v) -> p g v", g=2), ysb[:nt, :].rearrange("p (g v) -> p g v", g=2),
                       bc(ssq[:nt, 4:6].unsqueeze(2), [nt, 2, 256]), ALU.mult, [t_ysb, t_ssq], [t_ybf])
                    pt, tp = PB()
                    for c in range(4):
                        tr(pt[:, c * 128:c * 128 + nt], ybf[:nt, c * 128:(c + 1) * 128], [t_ybf], [tp])
                    pv = pt[:].rearrange("p (c t) -> p c t", c=8)[:, 0:4, 0:nt]
                    tt("dve", yT[:, 4:8, c0:c0 + nt], pv, bc(small_t[:, l, 64:68].unsqueeze(2), [128, 4, nt]), ALU.mult,
                       [tp, tc_], [t_yT[1]])
                    if kind != "S":
                        ps_, tps = PF()
                        for gg in range(2):
                            mm(ps_[:64, gg * 256:(gg + 1) * 256], btok[:nt, gg * 64:(gg + 1) * 64], xw[:nt, gg * 256:(gg + 1) * 256],
                               True, True, [t_xstok, t_xw], [tps])
                        tt("dve", Ss[:, l], Ss[:, l], bc(eGe_all[0:64, bi, :].unsqueeze(2), [64, 8, 64]), ALU.mult,
                           [t_Ss[l], t_eGe], [t_Ss[l]])
                        tt("dve", Ss[:, l], Ss[:, l], ps_[:64, :].rearrange("p (h v) -> p h v", h=8), ALU.add,
                           [t_Ss[l], tps], [t_Ss[l]])
                        cp("act", Ssb[:, l], Ss[:, l], [t_Ss[l]], [t_Ss[l]])
                        if last_group and bi == nblk - 1:
                            P.dma("sp", ssp[l], Ss[:, l], r=[t_Ss[l]], key="pfinal", final=True)
                    else:
                        pass
                chk("ssd")
                ph("ret_proj")
                W, tw = wload(l, "rqk")
                W2_, tw2 = wload(l, "rsw", live=1)
                rt = lnrt[0:64, :].rearrange("p (a h c) -> p a h c", a=2, h=4)
                for which, dstb in ((0, qTb), (1, kTb)):
                    P.dma("sp", lnrt[0:64, :], cst["rtab"][gi, which].rearrange("p a h c -> p (a h c)"), w=[t_rtab], key="lnbc")
                    for h in range(4):
                        pt, tp = proj_feat(W, tw, which * 256 + h * 64, 64, xT, t_xT, 8, ncols)
                        tt("dve", qTf[:, h, :ncols], pt[:64, :ncols], rt[:, 0, h, :ncols], ALU.mult, [tp, t_rtab], [t_qk, t_lhd])
                        pt, tp = proj_feat(W2_, tw2, which * 256 + h * 64, 64, xT, t_xT, 8, ncols)
                        tt("dve", kTf[:, h, :ncols], pt[:64, :ncols], rt[:, 1, h, :ncols], ALU.mult, [tp, t_rtab], [t_qk, t_dec])
                        tt("dve", dstb[:, h, :ncols], qTf[:, h, :ncols], kTf[:, h, :ncols], ALU.add, [t_qk], [t_qkb])
                ph("ret_blk")
                Wv, twv = wload(l, "rv")
                Wg_, twg = wload(l, "rg", live=1)
                def ret_pre(bi):
                    kind, nt, c0, pos0 = g[bi]
                    i2 = bi % 2
                    pt, tp = proj_tok(Wg_, twg, 0, 512, c0, nt)
                    pv_, tpv_ = proj_tok(Wv, twv, 0, 512, c0, nt)
                    yield
                    act(junkB[:nt, :], pt[:nt, :], AF.Exp, [tp], [t_junkB], scale=-1.0)
                    act(junkB[:nt, :], junkB[:nt, :], AF.Ln, [t_junkB], [t_junkB], bias=1.0)
                    act(junkB[:nt, :], junkB[:nt, :], AF.Exp, [t_junkB], [t_junkB], scale=-1.0)
                    cp("act", vtok2[:nt, i2, :], pv_[:nt, :], [tpv_], [t_vtok2[i2]])
                    tt("dve", srt2[:nt, i2, :], pt[:nt, :], junkB[:nt, :], ALU.mult, [tp, t_junkB], [t_srt2[i2]])
                for _ in ret_pre(0):
                    pass
                for bi, (kind, nt, c0, pos0) in enumerate(g):
                    i2 = bi % 2
                    egp = regend[:, KIND[kind], :]
                    egs = bc(regend[:, 2, :].unsqueeze(1), [64, 16, 4])
                    lastp = last_group and bi == nblk - 1
                    hook = (lambda bi=bi: ret_pre(bi + 1)) if bi + 1 < nblk else None
                    lin_block(l, kind, nt, c0, Sr, Srb, t_Sr[l], egp, egs, 8, None, sr_in, srp, srs, lastp,
                              vtok2[:, i2, :], t_vtok2[i2], srt2[:, i2, :], t_srt2[i2], hook)
                chk("ret")
                ph("merge")
                macc = big[:, 0:8192].bitcast(F32).rearrange("p (c t) -> p c t", c=8)
                for q in range(2):
                    for br in range(3):
                        Wg2, twg2 = wload(l, "mg%d%d" % (q, br))
                        Wo2, two2 = wload(l, "mo%d%d" % (q, br), live=1)
                        for cc in range(4):
                            dc = q * 4 + cc
                            pg_, tpg_ = proj_feat(Wg2, twg2, cc * 128, 128, xT, t_xT, 8, ncols)
                            jk, tjk = (junkA, t_junkA) if cc % 2 == 0 else (junkB, t_junkB)
                            act(jk[:, :ncols], pg_[:, :ncols], AF.Sigmoid, [tpg_], [tjk])
                            po_, tpo_ = proj_feat(Wo2, two2, cc * 128, 128, yT, t_yT[br], 4, ncols, koff=br * 4)
                            if br == 0:
                                tt("dve", macc[:, dc, :ncols], jk[:, :ncols], po_[:, :ncols], ALU.mult, [tjk, tpo_], [t_xraw])
                            else:
                                tca = t_cacc2[cc % 2]
                                tt("dve", cacc[:, cc % 2, :ncols], jk[:, :ncols], po_[:, :ncols], ALU.mult, [tjk, tpo_], [tca])
                                if br == 1:
                                    tt("dve", macc[:, dc, :ncols], macc[:, dc, :ncols], cacc[:, cc % 2, :ncols], ALU.add,
                                       [t_xraw, tca], [t_xraw])
                                else:
                                    tt("dve", mT[:, dc, :ncols], macc[:, dc, :ncols], cacc[:, cc % 2, :ncols], ALU.add,
                                       [t_xraw, tca], [t_mT])
                chk("merge")
                ph("wo_ln1")
                Wo0 = wload(l, "wo0")
                Wo1 = wload(l, "wo1", live=1)
                for bi, (kind, nt, c0, pos0) in enumerate(g):
                    for hf, (Wq, twq) in enumerate((Wo0, Wo1)):
                        pt, tp = PF()
                        for k in range(8):
                            mm(pt[:nt, :], mT[:, k, c0:c0 + nt], Wq[:, k, :], k == 0, k == 7, [t_mT, twq], [tp])
                        stt(xtok[:nt, bi, hf * 512:(hf + 1) * 512], xtok[:nt, bi, hf * 512:(hf + 1) * 512], ALPHA, pt[:nt, :],
                            ALU.mult, ALU.add, [t_xtok[bi], tp], [t_xtok[bi]])
                layernorm(g, 1 + 2 * l)
                chk("ln1")
                ph("ff1")
                for i in range(8):
                    W1, tw1 = wload(l, "f1%d" % i)
                    for cc in range(4):
                        f = i * 4 + cc
                        pt, tp = proj_feat(W1, tw1, cc * 128, 128, xT, t_xT, 8, ncols)
                        jk, tjk = (junkA, t_junkA) if f % 2 == 0 else (junkB, t_junkB)
                        ts("dve", jk[:, :ncols], pt[:, :ncols], b1col_t[:, l, f:f + 1], 0.0, ALU.add, ALU.max, [tp, tc_], [tjk])
                        act(hid[:, f, :ncols], jk[:, :ncols], AF.Square, [tjk], [t_hid[i]])
                ph("ff2_ln2")
                for half in range(2):
                    accs = [PF() for _ in range(nblk)]
                    for fg in range(4):
                        W2f, tw2f = wload(l, "f2%d%d" % (half, fg))
                        for bi, (kind, nt, c0, pos0) in enumerate(g):
                            pt, tp = accs[bi]
                            for k in range(8):
                                f = fg * 8 + k
                                mm(pt[:nt, :], hid[:, f, c0:c0 + nt], W2f[:, k, :], fg == 0 and k == 0, False,
                                   [t_hid[f // 4], tw2f], [tp])
                            if fg == 3:
                                o_ = 256 + half * 512
                                mm(pt[:nt, :], ones_b[0:33, :nt], rowhl[0:33, l, o_:o_ + 512], False, True, [tc_], [tp])
                    for bi, (kind, nt, c0, pos0) in enumerate(g):
                        pt, tp = accs[bi]
                        stt(xtok[:nt, bi, half * 512:(half + 1) * 512], xtok[:nt, bi, half * 512:(half + 1) * 512], ALPHA,
                            pt[:nt, :], ALU.mult, ALU.add, [t_xtok[bi], tp], [t_xtok[bi]])
                layernorm(g, 2 + 2 * l)
            for bi, (kind, nt, c0, pos0) in enumerate(g):
                if kind == "B":
                    P.dma("sp", yp[pos0 - 16:pos0 - 16 + nt, :], xtok[:nt, bi, :], r=[t_xtok[bi]], key=("yout", bi), final=True)
                elif kind == "S":
                    P.dma("sp", ys, xtok[:nt, bi, :], r=[t_xtok[bi]], key=("yout", bi), final=True)


    try:
        main()
    except _Stop:
        pass
    P.finish()
    for cm in reversed(cms):
        cm.__exit__(None, None, None)
    global LAST_PHASES
    LAST_PHASES = P.phases
    return nc


LAST_PHASES = None
_CACHE = {}


def kernel(x_prompt, x_sample, state_gla, state_ssm, state_conv, state_ret, meta_tokens,
           ln_in_w, ln_in_b, w_in, w_gla_a2, b_gla_a, w_gla_norm, conv_w, conv_b, dt_bias,
           a_log, d_skip, w_ssm_norm, w_gla_out, w_ssm_out, w_ret_out, w_o, ln1_w, ln1_b,
           w_ff1, b_ff1, w_ff2, b_ff2, ln2_w, ln2_b):
    f = lambda a: np.ascontiguousarray(np.asarray(a, dtype=np.float32))
    (x_prompt, x_sample, state_gla, state_ssm, state_conv, state_ret, meta_tokens, ln_in_w, ln_in_b, w_in,
     w_gla_a2, b_gla_a, w_gla_norm, conv_w, conv_b, dt_bias, a_log, d_skip, w_ssm_norm, w_gla_out, w_ssm_out,
     w_ret_out, w_o, ln1_w, ln1_b, w_ff1, b_ff1, w_ff2, b_ff2, ln2_w, ln2_b) = [f(a) for a in (
        x_prompt, x_sample, state_gla, state_ssm, state_conv, state_ret, meta_tokens, ln_in_w, ln_in_b, w_in,
        w_gla_a2, b_gla_a, w_gla_norm, conv_w, conv_b, dt_bias, a_log, d_skip, w_ssm_norm, w_gla_out, w_ssm_out,
        w_ret_out, w_o, ln1_w, ln1_b, w_ff1, b_ff1, w_ff2, b_ff2, ln2_w, ln2_b)]
    if "nc" not in _CACHE:
        _CACHE["nc"] = build_program()
        _CACHE["consts"] = build_consts()
    nc = _CACHE["nc"]
    consts = _CACHE["consts"]
    wstream = build_wstream(w_in, w_gla_out, w_ssm_out, w_ret_out, w_o, w_ff1, w_ff2)
    lnw = [ln_in_w, ln1_w[0], ln2_w[0], ln1_w[1], ln2_w[1]]
    lnb = [ln_in_b, ln1_b[0], ln2_b[0], ln1_b[1], ln2_b[1]]
    lnbc = np.empty((5, 2, 128, D), np.float32)
    lncol = np.empty((128, 5, 2, 8), np.float32)
    for i in range(5):
        lnbc[i, 0] = lnw[i][None, :]
        lnbc[i, 1] = lnb[i][None, :]
        lncol[:, i, 0, :] = lnw[i].reshape(8, 128).T
        lncol[:, i, 1, :] = lnb[i].reshape(8, 128).T
    small = np.zeros((128, L, 80), np.float32)
    rowp = np.zeros((1, L, 1280), np.float32)
    for l in range(L):
        small[:, l, 0:8] = a_log[l][None, :]
        small[:, l, 8:16] = dt_bias[l][None, :]
        small[:, l, 16:24] = d_skip[l][None, :]
        for c in range(8):
            if c < 4:
                ch = np.arange(c * 128, (c + 1) * 128)
            else:
                ch = np.arange(512 + (c - 4) * 64, 512 + (c - 3) * 64)
            small[:len(ch), l, 24 + c] = conv_b[l][ch]
            for i in range(4):
                small[:len(ch), l, 32 + c * 4 + i] = conv_w[l][i, ch]
        small[:, l, 64:68] = w_ssm_norm[l].reshape(4, 128).T
        small[:, l, 68] = w_gla_norm[l]
        rowp[0, l, 0:256] = b_gla_a[l]
        rowp[0, l, 256:1280] = b_ff2[l]
    wa2 = np.ascontiguousarray(w_gla_a2.transpose(1, 0, 2))
    b1col = np.ascontiguousarray(b_ff1.reshape(L, 32, 128).transpose(2, 0, 1))
    in_maps = []
    for c in range(NCORES):
        bs = slice(c * NSEQ, (c + 1) * NSEQ)
        def lin_state(s):
            s = s[:, bs].reshape(L, NSEQ, 2, 2, 64, 128)
            return np.ascontiguousarray(s.transpose(0, 2, 4, 1, 3, 5))
        s = state_ssm[:, bs].reshape(L, NSEQ, 2, 4, 64, 64)
        ss_in = np.ascontiguousarray(s.transpose(0, 2, 4, 1, 3, 5))
        sc_in = np.zeros((L, 128, 8, NSEQ, 3), np.float32)
        sc = state_conv[:, bs]
        for cc in range(8):
            if cc < 4:
                ch = np.arange(cc * 128, (cc + 1) * 128)
            else:
                ch = np.arange(512 + (cc - 4) * 64, 512 + (cc - 3) * 64)
            sc_in[:, :len(ch), cc] = sc[:, :, :, ch].transpose(0, 3, 1, 2)
        m = {"xp": x_prompt[c], "xs": x_sample[bs].reshape(NSEQ * DSEQ, D), "meta": meta_tokens,
             "wst": wstream, "lnbc": lnbc, "lncol": lncol, "small": small, "rowp": rowp.reshape(1, L * 1280), "wa2": wa2, "b1col": b1col,
             "sg_in": lin_state(state_gla), "sr_in": lin_state(state_ret), "ss_in": ss_in, "sc_in": sc_in}
        for n in CONST_NAMES:
            m["c_" + n] = consts[n]
        in_maps.append(m)
    if DBG_CORES:
        res = run_bass_kernel_spmd(nc, in_maps[:DBG_CORES], core_ids=list(range(DBG_CORES)))
        R = [res.results[c % DBG_CORES] for c in range(NCORES)]
    else:
        res = run_bass_kernel_spmd(nc, in_maps, core_ids=list(range(NCORES)))
        R = res.results
    y_prompt = np.stack([R[c]["yp"] for c in range(NCORES)])
    y_sample = np.concatenate([R[c]["ys"].reshape(NSEQ, DSEQ, D) for c in range(NCORES)], axis=0)

    def lin_p(name):
        o = np.stack([R[c][name] for c in range(NCORES)], axis=1)
        return np.ascontiguousarray(o.transpose(0, 1, 3, 2, 4))

    def lin_s(name):
        o = np.concatenate([R[c][name].transpose(0, 3, 1, 4, 2, 5).reshape(L, NSEQ, 4, 64, 128)
                            for c in range(NCORES)], axis=1)
        return np.ascontiguousarray(o)
    ssm_p = np.stack([R[c]["ssp"] for c in range(NCORES)], axis=1)
    ssm_p = np.ascontiguousarray(ssm_p.transpose(0, 1, 3, 2, 4))
    ssm_s = np.concatenate([R[c]["sss"].transpose(0, 3, 1, 4, 2, 5).reshape(L, NSEQ, 8, 64, 64)
                            for c in range(NCORES)], axis=1)
    conv_p = np.stack([R[c]["scp"] for c in range(NCORES)], axis=1)
    conv_s = np.concatenate([R[c]["scs"].transpose(0, 2, 1, 3) for c in range(NCORES)], axis=1)
    return (y_prompt.astype(np.float32), y_sample.astype(np.float32),
            lin_p("sgp"), lin_s("sgs"), ssm_p, np.ascontiguousarray(ssm_s),
            np.ascontiguousarray(conv_p), np.ascontiguousarray(conv_s), lin_p("srp"), lin_s("srs"))
```

```python
import numpy as np
import concourse.bass as bass
import concourse.mybir as mybir
from concourse.bass_utils import run_bass_kernel_spmd

F32 = mybir.dt.float32
BF16 = mybir.dt.bfloat16
AF = mybir.ActivationFunctionType
ALU = mybir.AluOpType
AX = mybir.AxisListType

D = 1024
SEQ = 2048
NMETA = 16
NSEQ = 16
DSEQ = 4
PAST = 16384
L = 2
ALPHA = float((2 * L) ** 0.25)
NCORES = 8
STOP = None
DBG_CORES = None

GROUPS = []
_g0 = [("M", 16, 0, 0)]
for i in range(2):
    _g0.append(("B", 128, 16 + 128 * i, 16 + 128 * i))
_g0.append(("S", 64, 272, PAST))
GROUPS.append(_g0)
_b = 2
for n in (3, 3, 4, 4):
    g = []
    for i in range(n):
        g.append(("B", 128, 128 * i, 16 + 128 * (_b + i)))
    _b += n
    GROUPS.append(g)
NG = len(GROUPS)


def gcols(g):
    return sum(b[1] for b in g)


WT = [("gqk", 8, 512), ("ga", 8, 16), ("gv", 8, 512), ("gr", 8, 512),
      ("sx0", 8, 512), ("sx1", 8, 256), ("sz", 8, 512), ("sdt", 8, 8),
      ("rqk", 8, 512), ("rsw", 8, 512), ("rv", 8, 512), ("rg", 8, 512)]
for q in range(2):
    for br in range(3):
        WT.append(("mg%d%d" % (q, br), 8, 512))
        WT.append(("mo%d%d" % (q, br), 4, 512))
WT += [("wo0", 8, 512), ("wo1", 8, 512)]
for i in range(8):
    WT.append(("f1%d" % i, 8, 512))
for half in range(2):
    for fg in range(4):
        WT.append(("f2%d%d" % (half, fg), 8, 512))
WOFF = {}
_o = 0
for nm, kc, ncl in WT:
    WOFF[nm] = (_o, kc, ncl)
    _o += kc * ncl
WTOT = _o
NSLOT = 3


def _perm_half():
    p = np.arange(256).reshape(4, 2, 32)[:, ::-1, :].reshape(256)
    return p


def build_wstream(w_in, w_gla_out, w_ssm_out, w_ret_out, w_o, w_ff1, w_ff2):
    out = np.empty((L, 128, WTOT), np.float32)
    offs = np.cumsum([0, 256, 256, 512, 512, 16, 512, 768, 8, 256, 256, 512, 512, 3072])
    o = {n: offs[i] for i, n in enumerate(["gq", "gk", "gv", "gr", "ga", "sz", "sx", "sdt", "rq", "rk", "rv", "rg", "gate"])}
    ph = _perm_half()
    for l in range(L):
        wi = w_in[l]
        def put(nm, mat):
            off, kc, ncl = WOFF[nm]
            assert mat.shape == (kc * 128, ncl), (nm, mat.shape)
            out[l, :, off:off + kc * ncl] = mat.reshape(kc, 128, ncl).transpose(1, 0, 2).reshape(128, kc * ncl)
        put("gqk", wi[:, o["gq"]:o["gq"] + 512])
        put("gv", wi[:, o["gv"]:o["gv"] + 512])
        put("gr", wi[:, o["gr"]:o["gr"] + 512])
        put("ga", wi[:, o["ga"]:o["ga"] + 16])
        put("sz", wi[:, o["sz"]:o["sz"] + 512])
        put("sdt", wi[:, o["sdt"]:o["sdt"] + 8])
        put("sx0", wi[:, o["sx"]:o["sx"] + 512])
        put("sx1", wi[:, o["sx"] + 512:o["sx"] + 768])
        put("rqk", wi[:, o["rq"]:o["rq"] + 512])
        rq = wi[:, o["rq"]:o["rq"] + 256][:, ph]
        rk = wi[:, o["rk"]:o["rk"] + 256][:, ph]
        put("rsw", np.concatenate([rq, rk], axis=1))
        put("rv", wi[:, o["rv"]:o["rv"] + 512])
        put("rg", wi[:, o["rg"]:o["rg"] + 512])
        outs = [w_gla_out[l], w_ssm_out[l], w_ret_out[l]]
        for q in range(2):
            for br in range(3):
                put("mg%d%d" % (q, br), wi[:, o["gate"] + br * 1024 + q * 512:o["gate"] + br * 1024 + q * 512 + 512])
                put("mo%d%d" % (q, br), outs[br][:, q * 512:(q + 1) * 512])
        put("wo0", w_o[l][:, 0:512])
        put("wo1", w_o[l][:, 512:1024])
        for i in range(8):
            put("f1%d" % i, w_ff1[l][:, i * 512:(i + 1) * 512])
        for half in range(2):
            for fg in range(4):
                put("f2%d%d" % (half, fg), w_ff2[l][fg * 1024:(fg + 1) * 1024, half * 512:(half + 1) * 512])
    return out


def build_consts():
    c = {}
    i = np.arange(128)
    ca = (i[:, None] <= i[None, :]).astype(np.float32)
    sl = (i[:, None] > i[None, :]).astype(np.float32)
    same = ((i[:, None] // 4) == (i[None, :] // 4)).astype(np.float32)
    c["ca_b"] = ca
    c["sl_b"] = sl
    c["ca_s"] = ca * same
    c["sl_s"] = sl * same
    c["ident"] = np.eye(128, dtype=np.float32)
    sm = np.zeros((128, 16), np.float32)
    for s in range(64):
        sm[s, s // 4] = 1.0
    c["smtok"] = sm
    smT = np.zeros((128, 16, 64), np.float32)
    for t in range(64):
        smT[:, t // 4, t] = 1.0
    c["smT"] = smT
    lg = np.log1p(-np.exp2(-5.0 - np.arange(4, dtype=np.float64)))
    invf = (10000.0 ** (-(np.arange(32, dtype=np.float32) / np.float32(32.0)))).astype(np.float32)
    tabs = []
    for g in GROUPS:
        tab = np.zeros((2, 64, 2, 4, 512), np.float32)
        for (kind, nt, c0, pos0) in g:
            for t in range(nt):
                if kind == "S":
                    pos = PAST + (t % 4)
                    idx = (t % 4) + 1
                else:
                    pos = pos0 + t
                    idx = t + 1
                ang = (np.float32(pos) * invf).astype(np.float32).astype(np.float64)
                cfull = np.concatenate([np.cos(ang), np.cos(ang)])
                sfull = np.concatenate([-np.sin(ang), np.sin(ang)])
                for h in range(4):
                    eg = np.exp(lg[h] * idx)
                    tab[0, :, 0, h, c0 + t] = cfull * eg
                    tab[0, :, 1, h, c0 + t] = sfull * eg
                    tab[1, :, 0, h, c0 + t] = cfull / eg * 0.125
                    tab[1, :, 1, h, c0 + t] = sfull / eg * 0.125
        tabs.append(tab)
    c["rtab"] = np.stack(tabs)
    eg = np.zeros((64, 3, 4), np.float32)
    for ki, n in enumerate((128, 16, 4)):
        for h in range(4):
            eg[:, ki, h] = np.exp(lg[h] * n)
    c["regend"] = eg
    return c


CONST_NAMES = ["ca_b", "sl_b", "ca_s", "sl_s", "ident", "smtok", "smT", "rtab", "regend"]


class Tok:
    __slots__ = ("w", "r")

    def __init__(self):
        self.w = None
        self.r = []


class Prog:
    ENG = ("pe", "act", "dve", "pool", "sp")

    def __init__(self, nc):
        self.nc = nc
        self.q = {e: [] for e in self.ENG}
        self.cnt = {e: 0 for e in self.ENG}
        self.sems = {}
        self.semvals = {}
        self.waited = {e: {} for e in self.ENG}
        self.pending_out = []
        self._stack = []
        self.phase = "init"
        self.phases = {e: [] for e in self.ENG}

    def sem(self, key):
        if key not in self.sems:
            cm = self.nc.semaphore("s%d" % len(self.sems))
            s = cm.__enter__()
            self._stack.append(cm)
            self.sems[key] = s
            self.semvals[key] = 0
        return self.sems[key]

    def _deps(self, eng, reads, writes):
        ev = {}

        def add(e):
            if e is None:
                return
            if ev.get(e[0], 0) < e[1]:
                ev[e[0]] = e[1]
        for t in reads:
            add(t.w)
        for t in writes:
            add(t.w)
            for r in t.r:
                if r[0] == eng:
                    continue
                add(r)
        out = []
        wd = self.waited[eng]
        for k, v in ev.items():
            if k == "pe" and eng == "pe":
                continue
            if wd.get(k, 0) >= v:
                continue
            wd[k] = v
            out.append((k, v))
        return out

    def op(self, eng, fn, r=(), w=()):
        waits = self._deps(eng, r, w)
        self.cnt[eng] += 1
        seq = self.cnt[eng]
        s_self = self.sem(eng)
        wl = [(self.sem(k), v) for k, v in waits]

        def emit(e, wl=wl, fn=fn, s_self=s_self):
            for s, v in wl:
                e.wait_ge(s, v)
            fn(e).then_inc(s_self, 1)
        self.q[eng].append(emit)
        self.phases[eng].append(self.phase)
        evt = (eng, seq)
        for t in r:
            t.r.append(evt)
        for t in w:
            t.w = evt
            t.r = []
        return evt

    def dma(self, eng, out, in_, r=(), w=(), key=None, final=False, **kw):
        waits = self._deps(eng, r, w)
        semkey = ("dma", key)
        s = self.sem(semkey)
        self.semvals[semkey] += 16
        val = self.semvals[semkey]
        wl = [(self.sem(k), v) for k, v in waits]

        def emit(e, wl=wl, s=s):
            for ss, v in wl:
                e.wait_ge(ss, v)
            e.dma_start(out=out, in_=in_, **kw).then_inc(s, 16)
        self.q[eng].append(emit)
        evt = (semkey, val)
        for t in r:
            t.r.append(evt)
        for t in w:
            t.w = evt
            t.r = []
        if final:
            self.pending_out.append(evt)
        return evt

    def finish(self):
        ev = {}
        for k, v in self.pending_out:
            ev[k] = max(ev.get(k, 0), v)
        wl = [(self.sem(k), v) for k, v in ev.items()]

        def emit(e):
            for s, v in wl:
                e.wait_ge(s, v)
        self.q["sp"].append(emit)
        nc = self.nc
        with nc.Block() as block:
            @block.tensor
            def _(e):
                for f in self.q["pe"]:
                    f(e)

            @block.scalar
            def _(e):
                for f in self.q["act"]:
                    f(e)

            @block.vector
            def _(e):
                for f in self.q["dve"]:
                    f(e)

            @block.gpsimd
            def _(e):
                for f in self.q["pool"]:
                    f(e)

            @block.sync
            def _(e):
                for f in self.q["sp"]:
                    f(e)
        for cm in reversed(self._stack):
            cm.__exit__(None, None, None)


def build_program():
    nc = bass.Bass("TRN2", target_bir_lowering=False)
    P = Prog(nc)
    cms = []

    def din(name, shape, dt=F32):
        return nc.dram_tensor(name, list(shape), dt, kind="ExternalInput").ap()

    def dout(name, shape):
        return nc.dram_tensor(name, list(shape), F32, kind="ExternalOutput").ap()

    def sb(name, shape, dt=F32):
        cm = nc.sbuf_tensor(name, list(shape), dt)
        t = cm.__enter__()
        cms.append(cm)
        return t

    xp = din("xp", [SEQ, D])
    xs = din("xs", [NSEQ * DSEQ, D])
    meta = din("meta", [NMETA, D])
    wst = din("wst", [L, 128, WTOT])
    lnbc = din("lnbc", [5, 2, 128, D])
    lncol = din("lncol", [128, 5, 2, 8])
    small = din("small", [128, L, 80])
    rowp = din("rowp", [1, L * 1280])
    wa2 = din("wa2", [16, L, 256])
    b1col = din("b1col", [128, L, 32])
    sg_in = din("sg_in", [L, 2, 64, NSEQ, 2, 128])
    sr_in = din("sr_in", [L, 2, 64, NSEQ, 2, 128])
    ss_in = din("ss_in", [L, 2, 64, NSEQ, 4, 64])
    sc_in = din("sc_in", [L, 128, 8, NSEQ, 3])
    cst = {}
    cshape = {"ca_b": [128, 128], "sl_b": [128, 128], "ca_s": [128, 128], "sl_s": [128, 128],
              "ident": [128, 128], "smtok": [128, 16], "smT": [128, 16, 64],
              "rtab": [NG, 2, 64, 2, 4, 512], "regend": [64, 3, 4]}
    for n in CONST_NAMES:
        cst[n] = din("c_" + n, cshape[n])
    yp = dout("yp", [SEQ, D])
    ys = dout("ys", [NSEQ * DSEQ, D])
    sgp = dout("sgp", [L, 64, 4, 128])
    srp = dout("srp", [L, 64, 4, 128])
    ssp = dout("ssp", [L, 64, 8, 64])
    scp = dout("scp", [L, 3, 768])
    sgs = dout("sgs", [L, 2, 64, NSEQ, 2, 128])
    srs = dout("srs", [L, 2, 64, NSEQ, 2, 128])
    sss = dout("sss", [L, 2, 64, NSEQ, 4, 64])
    scs = dout("scs", [L, 3, NSEQ, 768])

    pf = []
    for i in range(6):
        cm = nc.psum_tensor("pf%d" % i, [128, 512], F32)
        pf.append((cm.__enter__(), Tok()))
        cms.append(cm)
    pb = []
    for i in range(2):
        cm = nc.psum_tensor("pb%d" % i, [128, 1024], BF16)
        pb.append((cm.__enter__(), Tok()))
        cms.append(cm)
    rr = {"f": 0, "b": 0}

    def PF():
        rr["f"] = (rr["f"] + 1) % 4
        return pf[rr["f"]]

    def PL(i):
        return pf[4 + i]

    def PB():
        rr["b"] = (rr["b"] + 1) % 2
        return pb[rr["b"]]

    ca_b = sb("ca_b", [128, 128]); sl_b = sb("sl_b", [128, 128])
    ca_s = sb("ca_s", [128, 128]); sl_s = sb("sl_s", [128, 128])
    identb = sb("identb", [128, 128], BF16)
    smtok = sb("smtok", [128, 16]); smT = sb("smT", [128, 16, 64], BF16)
    regend = sb("regend", [64, 3, 4])
    ones_f = sb("ones_f", [128, 128]); ones_b = sb("ones_b", [128, 128], BF16)
    lncol_t = sb("lncol_t", [128, 5, 2, 8])
    small_t = sb("small_t", [128, L, 80])
    rowhl = sb("rowhl", [33, L, 1280], BF16)
    wa2_t = sb("wa2_t", [16, L, 256], BF16)
    b1col_t = sb("b1col_t", [128, L, 32])
    aneg = sb("aneg", [128, L, 8])
    dmat = sb("dmat", [128, 8, 128], BF16); t_dmat = Tok()
    tc_ = Tok()
    wslot = [(sb("wslot%d" % i, [128, 4096], BF16), Tok()) for i in range(NSLOT)]
    xT = sb("xT", [128, 8, 512], BF16); t_xT = Tok()
    xtok = sb("xtok", [128, 4, D]); t_xtok = [Tok() for _ in range(4)]
    yT = sb("yT", [128, 12, 512], BF16); t_yT = [Tok() for _ in range(3)]
    mT = sb("mT", [128, 8, 512], BF16); t_mT = Tok()
    big = sb("big", [128, 16384], BF16)
    hid = big[:].rearrange("p (f t) -> p f t", f=32); t_hid = [Tok() for _ in range(8)]
    xraw = big[:, 0:8240].bitcast(F32).rearrange("p (c t) -> p c t", c=8); t_xraw = Tok(); t_xraw8 = [Tok() for _ in range(8)]
    cacc = big[:, 8240:10288].bitcast(F32).rearrange("p (c t) -> p c t", c=2); t_cacc2 = [Tok(), Tok()]
    qTm = big[0:64, 10288:12336].rearrange("p (a j t) -> p a j t", a=2, j=16); t_qTm = Tok()
    cTm = big[0:64, 12336:13360].rearrange("p (j t) -> p j t", j=16); t_cTm = Tok()
    xsin = big[:, 13360:15152].bitcast(F32).rearrange("p (c j t) -> p c j t", c=8, j=16); t_xsin = Tok()
    lnrt = sb("lnrt", [128, 4096])
    lnw_bc = lnrt[:, 0:1024]; lnb_bc = lnrt[:, 1024:2048]; t_lnbc = Tok()
    rtab = lnrt[:].rearrange("p (a b c) -> p a b c", a=4, b=2); t_rtab = t_lnbc
    scrA = sb("scrA", [128, 2048]); t_qk = Tok()
    _sab = scrA[0:64, :].bitcast(BF16)
    qTf = _sab[:, 0:2048].rearrange("p (a t) -> p a t", a=4); kTf = _sab[:, 2048:4096].rearrange("p (a t) -> p a t", a=4)
    lhd = scrA[:, 0:1024].rearrange("p (h t) -> p h t", h=8); dec = scrA[:, 1024:2048].rearrange("p (h t) -> p h t", h=8)
    t_lhd = Tok(); t_dec = Tok()
    qTb = sb("qTb", [64, 4, 512], BF16); kTb = sb("kTb", [64, 4, 512], BF16); t_qkb = Tok()
    aTb = sb("aTb", [16, 512], BF16); t_aT = Tok()
    eG_a = sb("eG", [64, 4, 128]); eGn_a = sb("eGn", [64, 4, 128]); t_eG_a = Tok(); t_eG = t_eG_a
    ktok = sb("ktok", [128, 256], BF16); t_ktok = Tok()
    vtok2 = sb("vtok2", [128, 2, 512], BF16); t_vtok2 = [Tok(), Tok()]
    srt2 = sb("srt2", [128, 2, 512]); t_srt2 = [Tok(), Tok()]
    srt = srt2[:, 0, :]; t_srt = t_srt2[0]
    egp_all = sb("egp_all", [64, 4, 4]); egs_all = sb("egs_all", [64, 16, 4]); t_egp = Tok()
    attb = sb("attb", [128, 8, 128], BF16); t_att = Tok()
    ysb = sb("ysb", [128, 512]); t_ysb = Tok()
    ybf = sb("ybf", [128, 512], BF16); t_ybf = Tok()
    ssq = sb("ssq", [128, 8]); t_ssq = Tok()
    junkA = sb("junkA", [128, 512]); junkB = sb("junkB", [128, 512]); t_junkA = Tok(); t_junkB = Tok()
    junk = junkA; t_junk = t_junkA
    xcT = sb("xcT", [128, 8, 512], BF16); t_xcT = Tok()
    xstok = sb("xstok", [128, 512], BF16); btok = sb("btok", [128, 128], BF16); t_xstok = Tok()
    dtt_all = sb("dtt_all", [128, 4, 8]); gss_all = sb("gss_all", [128, 4, 8]); t_dt4 = [Tok() for _ in range(4)]
    eGt_all = sb("eGt_all", [128, 4, 8]); wsd_all = sb("wsd_all", [128, 4, 8]); eGe_all = sb("eGe_all", [128, 4, 8])
    gm = sb("gm", [128, 16, 8]); t_gm = Tok()
    t_eGt4 = [Tok() for _ in range(4)]
    eGe = sb("eGe", [128, 16, 8]); t_eGe = Tok()
    m2 = sb("m2", [128, 8, 128]); t_m2 = Tok()
    _m2f = m2[0:64].rearrange("p h t -> p (h t)")
    eG_b = _m2f[:, 0:512].rearrange("p (h t) -> p h t", h=4); eGn_b = _m2f[:, 512:1024].rearrange("p (h t) -> p h t", h=4)
    xw = sb("xw", [128, 512], BF16); t_xw = Tok()
    hist = sb("hist", [128, L, 8, 3]); t_hist = [Tok() for _ in range(L)]
    sc3 = sb("sc3", [16, 768]); t_sc3 = Tok()
    Sg = sb("Sg", [64, L, 4, 128]); Sgb = sb("Sgb", [64, L, 4, 128], BF16)
    Sr = sb("Sr", [64, L, 4, 128]); Srb = sb("Srb", [64, L, 4, 128], BF16)
    Ss = sb("Ss", [64, L, 8, 64]); Ssb = sb("Ssb", [64, L, 8, 64], BF16)
    t_Sg = [Tok() for _ in range(L)]; t_Sr = [Tok() for _ in range(L)]; t_Ss = [Tok() for _ in range(L)]
    S0 = lnrt; t_S0 = t_lnbc
    S0b = mT[:].rearrange("p a b -> p (a b)"); t_S0b = t_mT
    ktm = S0b[0:64, 0:2048].rearrange("p (j n) -> p j n", j=16); t_ktm = t_S0b
    btm = S0b[0:64, 0:1024].rearrange("p (j n) -> p j n", j=16); t_btm = t_S0b

    def mm(out, lhsT, rhs, start, stop, r, w):
        P.op("pe", lambda e: e.matmul(out, lhsT, rhs, start=start, stop=stop), r=r, w=w)

    def tr(out, in_, r, w):
        n = in_.shape[0]
        P.op("pe", lambda e: e.transpose(out, in_, identb[:n, :n]), r=list(r) + [tc_], w=w)

    def act(out, in_, func, r, w, bias=None, scale=1.0, accum=None):
        kw = {}
        if bias is not None:
            kw["bias"] = bias
        if accum is not None:
            kw["accum_out"] = accum
        P.op("act", lambda e: e.activation(out=out, in_=in_, func=func, scale=scale, **kw), r=r, w=w)

    def tt(eng, out, in0, in1, op, r, w):
        P.op(eng, lambda e: e.tensor_tensor(out=out, in0=in0, in1=in1, op=op), r=r, w=w)

    def ts(eng, out, in0, s1, s2, op0, op1, r, w):
        if op1 is None:
            P.op(eng, lambda e: e.tensor_scalar(out=out, in0=in0, scalar1=s1, scalar2=None, op0=op0), r=r, w=w)
        else:
            P.op(eng, lambda e: e.tensor_scalar(out=out, in0=in0, scalar1=s1, scalar2=s2, op0=op0, op1=op1), r=r, w=w)

    def stt(out, in0, sc, in1, op0, op1, r, w):
        P.op("dve", lambda e: e.scalar_tensor_tensor(out=out, in0=in0, scalar=sc, in1=in1, op0=op0, op1=op1), r=r, w=w)

    def cp(eng, out, in_, r, w):
        if eng == "act":
            P.op("act", lambda e: e.copy(out=out, in_=in_), r=r, w=w)
        else:
            P.op(eng, lambda e: e.tensor_copy(out=out, in_=in_), r=r, w=w)

    def bc(ap, shape):
        return ap.to_broadcast(list(shape))

    for nm, t in (("ca_b", ca_b), ("sl_b", sl_b), ("ca_s", ca_s), ("sl_s", sl_s), ("smtok", smtok),
                  ("regend", regend)):
        P.dma("sp", t[:], cst[nm], w=[tc_], key="const")
    P.dma("pool", identb[:], cst["ident"], w=[tc_], key="constp")
    P.dma("pool", smT[:], cst["smT"], w=[tc_], key="constp")
    P.dma("sp", lncol_t[:], lncol, w=[tc_], key="const")
    P.dma("sp", small_t[:], small, w=[tc_], key="const")
    P.dma("pool", wa2_t[:], wa2, w=[tc_], key="constp")
    P.dma("sp", b1col_t[:], b1col, w=[tc_], key="const")
    P.op("dve", lambda e: e.memset(ones_f[:], 1.0), w=[tc_])
    P.op("dve", lambda e: e.memset(big[:], 0.0), w=t_hid + [t_xraw, t_xsin, t_qTm, t_cTm] + t_cacc2)
    P.op("dve", lambda e: e.memset(ones_b[:], 1.0), w=[tc_])
    P.op("dve", lambda e: e.memset(hist[:], 0.0), w=t_hist)
    for l in range(L):
        P.op("dve", lambda e, l=l: e.memset(Sg[:, l], 0.0), w=[t_Sg[l]])
        P.op("dve", lambda e, l=l: e.memset(Sgb[:, l], 0.0), w=[t_Sg[l]])
        P.op("dve", lambda e, l=l: e.memset(Sr[:, l], 0.0), w=[t_Sr[l]])
        P.op("dve", lambda e, l=l: e.memset(Srb[:, l], 0.0), w=[t_Sr[l]])
        P.op("dve", lambda e, l=l: e.memset(Ss[:, l], 0.0), w=[t_Ss[l]])
        P.op("dve", lambda e, l=l: e.memset(Ssb[:, l], 0.0), w=[t_Ss[l]])
    NR = L * 1280
    rhl = rowhl[:].rearrange("p l n -> p (l n)")
    P.dma("sp", S0[0:1, 0:NR], rowp, w=[t_S0], key="S0")
    P.dma("sp", S0[32:33, 0:NR], rowp, w=[t_S0], key="S0")
    P.op("dve", lambda e: e.memset(rowhl[:], 0.0), w=[tc_])
    cp("dve", rhl[0:1, :], S0[0:1, 0:NR], [t_S0], [tc_])
    cp("dve", rhl[32:33, :], S0[32:33, 0:NR], [t_S0], [tc_])
    tt("dve", S0[32:33, 0:NR], S0[32:33, 0:NR], rhl[32:33, :], ALU.subtract, [t_S0, tc_], [t_S0])
    cp("dve", rhl[32:33, :], S0[32:33, 0:NR], [t_S0], [tc_])
    for l in range(L):
        act(aneg[:, l, :], small_t[:, l, 0:8], AF.Exp, [tc_], [tc_])
        ts("dve", aneg[:, l, :], aneg[:, l, :], -1.0, None, ALU.mult, None, [tc_], [tc_])
    identf = sb("identf", [128, 128])
    P.dma("sp", identf[:], cst["ident"], w=[tc_], key="const")

    wseq = [(l_, nm) for _gi in range(NG) for l_ in range(L) for (nm, _k, _n) in WT]
    wstate = {"issued": 0, "cur": 0}

    wbf = nc.dram_tensor("wbf", [L, 128, WTOT], BF16, kind="Internal").ap()
    t_wbf = {}
    NT_PER = len(WT)

    def _issue(idx):
        l_, nm = wseq[idx]
        off, kc, ncl = WOFF[nm]
        t, tk = wslot[idx % NSLOT]
        n = kc * ncl
        gi_ = idx // (L * NT_PER)
        if gi_ == 0:
            P.dma("pool", t[:, 0:n], wst[l_, :, off:off + n], w=[tk], key=("w", idx % NSLOT), max_dma_last_dim=4096)
            tkd = Tok()
            t_wbf[(l_, nm)] = tkd
            P.dma("sp", wbf[l_, :, off:off + n], t[:, 0:n], r=[tk], w=[tkd], key=("wbw", idx % NSLOT))
        else:
            P.dma("sp", t[:, 0:n], wbf[l_, :, off:off + n], r=[t_wbf[(l_, nm)]], w=[tk], key=("wr", idx % NSLOT))

    def wload(l, nm, live=0):
        idx = wstate["cur"]
        assert wseq[idx] == (l, nm), (wseq[idx], l, nm)
        while wstate["issued"] < min(len(wseq), idx - live + NSLOT):
            _issue(wstate["issued"])
            wstate["issued"] += 1
        assert wstate["issued"] > idx
        wstate["cur"] += 1
        off, kc, ncl = WOFF[nm]
        t, tk = wslot[idx % NSLOT]
        return t[:, 0:kc * ncl].rearrange("p (k n) -> p k n", k=kc), tk

    xhb4 = big[:, 0:4096].rearrange("p (b d) -> p b d", b=4)
    t_xhb4 = [t_hid[0], t_hid[1], t_xraw]
    lnst4 = sb("lnst4", [128, 4, 2, 6]); lnmv4 = sb("lnmv4", [128, 4, 4]); t_ln4 = [Tok() for _ in range(4)]

    def layernorm(g, lni):
        P.dma("sp", lnw_bc, lnbc[lni, 0], w=[t_lnbc], key="lnbc")
        P.dma("sp", lnb_bc, lnbc[lni, 1], w=[t_lnbc], key="lnbc")
        blks = list(enumerate(g))
        for bi, (kind, nt, c0, pos0) in blks:
            z = xtok[:nt, bi, :]
            for hh in range(2):
                P.op("dve", lambda e, hh=hh, z=z, nt=nt, bi=bi: e.bn_stats(out=lnst4[:nt, bi, hh, :], in_=z[:, hh * 512:(hh + 1) * 512]),
                     r=[t_xtok[bi]], w=[t_ln4[bi]])
            P.op("dve", lambda e, nt=nt, bi=bi: e.bn_aggr(out=lnmv4[:nt, bi, 0:2], in_=lnst4[:nt, bi].rearrange("p a b -> p (a b)")),
                 r=[t_ln4[bi]], w=[t_ln4[bi]])
        for bi, (kind, nt, c0, pos0) in blks:
            act(lnmv4[:nt, bi, 2:3], lnmv4[:nt, bi, 1:2], AF.Ln, [t_ln4[bi]], [t_ln4[bi]], bias=1e-5)
            act(lnmv4[:nt, bi, 3:4], lnmv4[:nt, bi, 2:3], AF.Exp, [t_ln4[bi]], [t_ln4[bi]], scale=-0.5)
        for bi, (kind, nt, c0, pos0) in blks:
            ts("dve", lnmv4[:nt, bi, 2:3], lnmv4[:nt, bi, 0:1], lnmv4[:nt, bi, 3:4], -1.0, ALU.mult, ALU.mult,
               [t_ln4[bi]], [t_ln4[bi]])
        for bi, (kind, nt, c0, pos0) in blks:
            z = xtok[:nt, bi, :]
            act(z, z, AF.Identity, [t_xtok[bi], t_ln4[bi]], [t_xtok[bi]],
                bias=lnmv4[:nt, bi, 2:3], scale=lnmv4[:nt, bi, 3:4])
        for bi, (kind, nt, c0, pos0) in blks:
            z = xtok[:nt, bi, :]
            tt("dve", z, z, lnw_bc[:nt, :], ALU.mult, [t_xtok[bi], t_lnbc], [t_xtok[bi]])
            tt("dve", z, z, lnb_bc[:nt, :], ALU.add, [t_xtok[bi], t_lnbc], [t_xtok[bi]])
        for bi, (kind, nt, c0, pos0) in blks:
            cp("act", xhb4[:nt, bi, :], xtok[:nt, bi, :], [t_xtok[bi]], t_xhb4)
        for bi, (kind, nt, c0, pos0) in blks:
            pt, tp = PB()
            for k in range(8):
                tr(pt[:, k * 128:k * 128 + nt], xhb4[:nt, bi, k * 128:(k + 1) * 128], t_xhb4, [tp])
            cp("act", xT[:, :, c0:c0 + nt], pt[:].rearrange("p (k t) -> p k t", k=8)[:, :, 0:nt], [tp], [t_xT])

    def proj_feat(W, tw, wc0, M, src, tsrc, nk, ncols, koff=0):
        pt, tp = PF()
        for k in range(nk):
            mm(pt[:M, :ncols], W[:, k, wc0:wc0 + M], src[:, koff + k, 0:ncols], k == 0, k == nk - 1, [tw, tsrc], [tp])
        return pt, tp

    def proj_tok(W, tw, wc0, N, c0, nt):
        pt, tp = PF()
        for k in range(8):
            mm(pt[:nt, :N], xT[:, k, c0:c0 + nt], W[:, k, wc0:wc0 + N], k == 0, k == 7, [tw, t_xT], [tp])
        return pt, tp

    KIND = {"B": 0, "M": 1, "S": 2}

    def lin_block(l, kind, nt, c0, Sm, Smb, tS, egp, egs, ybase, ycol_scale, st_in, st_out_prompt, st_out_sample,
                  last_prompt_block, vtok, t_vtok, srt, t_srt, hook=None):
        ca = ca_s if kind == "S" else ca_b
        pt, tp = PB()
        for h in range(4):
            tr(pt[:nt, h * 64:(h + 1) * 64], kTb[:, h, c0:c0 + nt], [t_qkb], [tp])
        cp("act", ktok[:nt, :], pt[:nt, 0:256], [tp], [t_ktok])
        chk("lb_ktok")
        pa, tpa = PF()
        for h in range(4):
            mm(pa[:nt, h * 128:h * 128 + nt], kTb[:, h, c0:c0 + nt], qTb[:, h, c0:c0 + nt], True, True, [t_qkb], [tpa])
        chk("lb_attmm")
        pav = pa[:].rearrange("p (h t) -> p h t", h=4)[:nt, :, :nt]
        tt("dve", attb[:nt, 0:4, :nt], pav, bc(ca[:nt, :nt].unsqueeze(1), [nt, 4, nt]), ALU.mult, [tpa, tc_], [t_att])
        chk("lb_att")
        po, tpo = PL(0)
        S0v = S0[0:64, :].rearrange("p (j a v) -> p j a v", j=16, a=2)
        S0bv = S0b[0:64, :].rearrange("p (j a v) -> p j a v", j=16, a=2)
        if kind != "S":
            for h in range(4):
                mm(po[:nt, h * 128:(h + 1) * 128], attb[:nt, h, :nt], vtok[:nt, h * 128:(h + 1) * 128], True, False,
                   [t_att, t_vtok], [tpo])
                mm(po[:nt, h * 128:(h + 1) * 128], qTb[:, h, c0:c0 + nt], Smb[:, l, h, :], False, True, [t_qkb, tS], [tpo])
        else:
            for p in range(2):
                P.dma("sp", S0[0:64, :], st_in[l, p].rearrange("p j a v -> p (j a v)"), w=[t_S0], key="S0")
                P.dma("pool", S0b[0:64, :], st_in[l, p].rearrange("p j a v -> p (j a v)"), w=[t_S0b], key="S0b",
                      max_dma_last_dim=4096)
                tt("dve", qTm[:], bc(qTb[:, 2 * p:2 * p + 2, c0:c0 + 64].unsqueeze(2), [64, 2, 16, 64]),
                   bc(smT[0:64].unsqueeze(1), [64, 2, 16, 64]), ALU.mult, [t_qkb, tc_], [t_qTm])
                for h2 in range(2):
                    h = 2 * p + h2
                    mm(po[:nt, h * 128:(h + 1) * 128], attb[:nt, h, :nt], vtok[:nt, h * 128:(h + 1) * 128], True, False,
                       [t_att, t_vtok], [tpo])
                    for j in range(16):
                        mm(po[:nt, h * 128:(h + 1) * 128], qTm[:, h2, j, :], S0bv[:, j, h2, :], False, j == 15,
                           [t_qTm, t_S0b], [tpo])
                tt("dve", ktm[:], bc(ktok[:64, p * 128:(p + 1) * 128].unsqueeze(1), [64, 16, 128]),
                   bc(smtok[:64, :].unsqueeze(2), [64, 16, 128]), ALU.mult, [t_ktok, tc_], [t_ktm])
                for rnd in range(2):
                    banks = []
                    for q in range(4):
                        ps_, tps = PF()
                        banks.append((ps_, tps))
                        for jj in range(2):
                            j = rnd * 8 + q * 2 + jj
                            for h2 in range(2):
                                h = 2 * p + h2
                                o_ = (jj * 2 + h2) * 128
                                mm(ps_[:64, o_:o_ + 128], ktm[:, j, h2 * 64:(h2 + 1) * 64], vtok[:64, h * 128:(h + 1) * 128],
                                   True, True, [t_ktm, t_vtok], [tps])
                    for q in range(4):
                        ps_, tps = banks[q]
                        j0 = rnd * 8 + q * 2
                        tt("dve", S0v[:, j0:j0 + 2], S0v[:, j0:j0 + 2],
                           ps_[:64, :].rearrange("p (j a v) -> p j a v", j=2, a=2), ALU.add, [t_S0, tps], [t_S0])
                        tt("dve", S0v[:, j0:j0 + 2], S0v[:, j0:j0 + 2],
                           bc(egs[:, j0:j0 + 2, 2 * p:2 * p + 2].unsqueeze(3), [64, 2, 2, 128]), ALU.mult,
                           [t_S0, t_eG, tc_], [t_S0])
                P.dma("sp", st_out_sample[l, p].rearrange("p j a v -> p (j a v)"), S0[0:64, :], r=[t_S0], key="S0out",
                      final=True)
        chk("lb_o")
        if kind != "S":
            ps_, tps = PF()
            for h in range(4):
                mm(ps_[:64, h * 128:(h + 1) * 128], ktok[:nt, h * 64:(h + 1) * 64], vtok[:nt, h * 128:(h + 1) * 128],
                   True, True, [t_ktok, t_vtok], [tps])
            tt("dve", Sm[:, l], Sm[:, l], ps_[:64, :].rearrange("p (h v) -> p h v", h=4), ALU.add, [tS, tps], [tS])
            tt("dve", Sm[:, l], Sm[:, l], bc(egp.unsqueeze(2), [64, 4, 128]), ALU.mult, [tS, t_eG, t_egp, tc_], [tS])
            cp("dve", Smb[:, l], Sm[:, l], [tS], [tS])
            if last_prompt_block:
                P.dma("sp", st_out_prompt[l], Sm[:, l], r=[tS], key="pfinal", final=True)
        hk = hook() if hook is not None else None
        if hk is not None:
            next(hk)
        act(junk[:nt, :], po[:nt, :], AF.Square, [tpo], [t_junk])
        P.op("dve", lambda e: e.reduce_sum(out=ssq[:nt, 0:4], in_=junk[:nt, :].rearrange("p (h v) -> p h v", h=4), axis=AX.X),
             r=[t_junk], w=[t_ssq])
        act(ssq[:nt, 0:4], ssq[:nt, 0:4], AF.Ln, [t_ssq], [t_ssq], bias=1e-6, scale=1.0 / 128.0)
        act(ssq[:nt, 4:8], ssq[:nt, 0:4], AF.Exp, [t_ssq], [t_ssq], scale=-0.5)
        tt("dve", ysb[:nt, :].rearrange("p (h v) -> p h v", h=4), po[:nt, :].rearrange("p (h v) -> p h v", h=4),
           bc(ssq[:nt, 4:8].unsqueeze(2), [nt, 4, 128]), ALU.mult, [tpo, t_ssq], [t_ysb])
        tt("dve", ybf[:nt, :], ysb[:nt, :], srt[:nt, :], ALU.mult, [t_ysb, t_srt], [t_ybf])
        if hk is not None:
            for _ in hk:
                pass
        pt, tp = PB()
        for c in range(4):
            tr(pt[:, c * 128:c * 128 + nt], ybf[:nt, c * 128:(c + 1) * 128], [t_ybf], [tp])
        pv = pt[:].rearrange("p (c t) -> p c t", c=8)[:, 0:4, 0:nt]
        if ycol_scale is not None:
            ts("dve", yT[:, ybase:ybase + 4, c0:c0 + nt], pv, ycol_scale, None, ALU.mult, None, [tp, tc_],
               [t_yT[ybase // 4]])
        else:
            cp("act", yT[:, ybase:ybase + 4, c0:c0 + nt], pv, [tp], [t_yT[ybase // 4]])
        chk("lb_yT")

    class _Stop(Exception):
        pass

    def chk(name):
        if STOP is not None and name == STOP:
            raise _Stop()

    def ph(name):
        P.phase = name

    def main():
        for gi, g in enumerate(GROUPS):
            ncols = gcols(g)
            nblk = len(g)
            last_group = gi == NG - 1
            for bi, (kind, nt, c0, pos0) in enumerate(g):
                if kind == "M":
                    src = meta
                elif kind == "S":
                    src = xs
                else:
                    src = xp[pos0 - 16:pos0 - 16 + nt, :]
                P.dma("sp", xtok[:nt, bi, :], src, w=[t_xtok[bi]], key=("xin", bi))
            ph("ln0")
            layernorm(g, 0)
            chk("ln0")
            for l in range(L):
                ph("gla_proj")
                W, tw = wload(l, "gqk")
                for h in range(4):
                    pt, tp = proj_feat(W, tw, h * 64, 64, xT, t_xT, 8, ncols)
                    cp("act", qTf[:, h, :ncols], pt[:64, :ncols], [tp], [t_qk, t_lhd])
                    pt, tp = proj_feat(W, tw, 256 + h * 64, 64, xT, t_xT, 8, ncols)
                    cp("act", kTf[:, h, :ncols], pt[:64, :ncols], [tp], [t_qk, t_dec])
                chk("gla_qk")
                Wa, twa = wload(l, "ga", live=1)
                pt, tp = proj_feat(Wa, twa, 0, 16, xT, t_xT, 8, ncols)
                cp("act", aTb[:, :ncols], pt[:16, :ncols], [tp], [t_aT])
                Wv, twv = wload(l, "gv")
                Wr_, twr = wload(l, "gr", live=1)
                ph("gla_blk")
                def _sets(bi):
                    if bi % 2 == 0:
                        return junkA[:, 0:256], t_junkA, eG_a, eGn_a, t_eG_a
                    return junkB[:, 0:256], t_junkB, eG_b, eGn_b, t_m2
                for b0 in range(0, nblk, 2):
                    pair = [(bi,) + tuple(g[bi]) for bi in range(b0, min(b0 + 2, nblk))]
                    pas = {}
                    for (bi, kind, nt, c0, pos0) in pair:
                        pa_, tpa_ = PF()
                        pas[bi] = (pa_, tpa_)
                        mm(pa_[:nt, 0:256], aTb[0:16, c0:c0 + nt], wa2_t[0:16, l, :], True, False, [t_aT, tc_], [tpa_])
                        mm(pa_[:nt, 0:256], ones_b[0:33, :nt], rowhl[0:33, l, 0:256], False, True, [tc_], [tpa_])
                    for (bi, kind, nt, c0, pos0) in pair:
                        gpr, t_gpr, eG, eGn, t_eGx = _sets(bi)
                        pa_, tpa_ = pas[bi]
                        act(gpr[:nt, :], pa_[:nt, 0:256], AF.Exp, [tpa_], [t_gpr], scale=-1.0)
                        act(gpr[:nt, :], gpr[:nt, :], AF.Ln, [t_gpr], [t_gpr], bias=1.0)
                    pgs = {}
                    for (bi, kind, nt, c0, pos0) in pair:
                        gpr, t_gpr, eG, eGn, t_eGx = _sets(bi)
                        U = ca_s if kind == "S" else ca_b
                        pg, tpg = PF()
                        pgs[bi] = (pg, tpg)
                        for h in range(4):
                            mm(pg[:64, h * 128:h * 128 + nt], gpr[:nt, h * 64:(h + 1) * 64], U[:nt, :nt], True, True,
                               [t_gpr, tc_], [tpg])
                    for (bi, kind, nt, c0, pos0) in pair:
                        gpr, t_gpr, eG, eGn, t_eGx = _sets(bi)
                        pg, tpg = pgs[bi]
                        pgv = pg[:].rearrange("p (a t) -> p a t", a=4)[:64, :, 0:nt]
                        act(eG[:, :, :nt], pgv, AF.Exp, [tpg], [t_eGx], scale=-1.0 / 16.0)
                        act(eGn[:, :, :nt], pgv, AF.Exp, [tpg], [t_eGx], scale=1.0 / 16.0)
                    for (bi, kind, nt, c0, pos0) in pair:
                        gpr, t_gpr, eG, eGn, t_eGx = _sets(bi)
                        stt(qTb[:, :, c0:c0 + nt], qTf[:, :, c0:c0 + nt], 0.125, eG[:, :, :nt], ALU.mult, ALU.mult,
                            [t_qk, t_eGx], [t_qkb])
                        tt("dve", kTb[:, :, c0:c0 + nt], kTf[:, :, c0:c0 + nt], eGn[:, :, :nt], ALU.mult, [t_qk, t_eGx], [t_qkb])
                        if kind == "S":
                            cp("dve", egs_all[:], eG[:, :, 3:64:4].rearrange("p h j -> p j h"), [t_eGx], [t_egp])
                        else:
                            cp("dve", egp_all[:, bi, :], eG[:, :, nt - 1], [t_eGx], [t_egp])

                def gla_pre(bi):
                    kind, nt, c0, pos0 = g[bi]
                    i2 = bi % 2
                    pt, tp = proj_tok(Wr_, twr, 0, 512, c0, nt)
                    pv_, tpv_ = proj_tok(Wv, twv, 0, 512, c0, nt)
                    yield
                    act(junkB[:nt, :], pt[:nt, :], AF.Exp, [tp], [t_junkB], scale=-1.0)
                    act(junkB[:nt, :], junkB[:nt, :], AF.Ln, [t_junkB], [t_junkB], bias=1.0)
                    act(junkB[:nt, :], junkB[:nt, :], AF.Exp, [t_junkB], [t_junkB], scale=-1.0)
                    cp("act", vtok2[:nt, i2, :], pv_[:nt, :], [tpv_], [t_vtok2[i2]])
                    tt("dve", srt2[:nt, i2, :], pt[:nt, :], junkB[:nt, :], ALU.mult, [tp, t_junkB], [t_srt2[i2]])
                for _ in gla_pre(0):
                    pass
                for bi, (kind, nt, c0, pos0) in enumerate(g):
                    i2 = bi % 2
                    lastp = last_group and bi == nblk - 1
                    hook = (lambda bi=bi: gla_pre(bi + 1)) if bi + 1 < nblk else None
                    lin_block(l, kind, nt, c0, Sg, Sgb, t_Sg[l], egp_all[:, bi, :], egs_all[:], 0, small_t[:, l, 68:69],
                              sg_in, sgp, sgs, lastp, vtok2[:, i2, :], t_vtok2[i2], srt2[:, i2, :], t_srt2[i2], hook)
                    chk("gla_b%d" % bi)
                chk("gla")
                ph("ssd_proj")
                tt("dve", dmat[:], bc(identf[:].unsqueeze(1), [128, 8, 128]),
                   bc(small_t[:, l, 16:24].unsqueeze(2), [128, 8, 128]), ALU.mult, [tc_], [t_dmat])
                Wx0, twx0 = wload(l, "sx0")
                Wx1, twx1 = wload(l, "sx1", live=1)
                has_s = any(b[0] == "S" for b in g)
                npr = ncols - (64 if has_s else 0)
                CP_ = [128, 128, 128, 128, 64, 64, 64, 64]
                cp("dve", xraw[:, :, 0:3], hist[:, l], [t_hist[l]], t_xraw8 + [t_xraw])
                if has_s:
                    P.dma("sp", junkB[:, 0:384], sc_in[l].rearrange("p c j i -> p (c j i)"), w=[t_junkB], key="xsin")
                    cp("dve", xsin[:, :, :, 0:3], junkB[:, 0:384].rearrange("p (c j i) -> p c j i", c=8, j=16),
                       [t_junkB], [t_xsin])

                def silu_chunk(c):
                    pc = CP_[c]
                    act(xcT[:pc, c, :ncols], cacc[:pc, c % 2, :ncols], AF.Silu, [t_cacc2[c % 2]], [t_xcT])
                for c in range(8):
                    pc = CP_[c]
                    txr = t_xraw8[c]
                    if c < 4:
                        pt, tp = proj_feat(Wx0, twx0, c * 128, 128, xT, t_xT, 8, ncols)
                    else:
                        pt, tp = proj_feat(Wx1, twx1, (c - 4) * 64, 64, xT, t_xT, 8, ncols)
                    cp("act", xraw[:pc, c, 3:3 + ncols], pt[:pc, :ncols], [tp], [txr])
                    cw = lambda i, c=c: small_t[:CP_[c], l, 32 + c * 4 + i:33 + c * 4 + i]
                    ca_ = cacc[:pc, c % 2, :]
                    tca = t_cacc2[c % 2]
                    act(ca_[:, 0:npr], xraw[:pc, c, 0:npr], AF.Identity, [txr, tc_], [tca],
                        bias=small_t[:pc, l, 24 + c:25 + c], scale=cw(0))
                    for i in range(1, 4):
                        stt(ca_[:, 0:npr], xraw[:pc, c, i:i + npr], cw(i), ca_[:, 0:npr], ALU.mult, ALU.add,
                            [txr, tc_, tca], [tca])
                    if has_s:
                        cp("dve", xsin[:pc, c, :, 3:7], xraw[:pc, c, 3 + npr:3 + ncols].rearrange("p (j t) -> p j t", j=16),
                           [txr], [t_xsin])
                        cv = ca_[:, npr:ncols].rearrange("p (j t) -> p j t", j=16)
                        act(cv, xsin[:pc, c, :, 0:4], AF.Identity, [t_xsin, tc_], [tca],
                            bias=small_t[:pc, l, 24 + c:25 + c], scale=cw(0))
                        for i in range(1, 4):
                            stt(cv, xsin[:pc, c, :, i:i + 4], cw(i), cv, ALU.mult, ALU.add, [t_xsin, tc_, tca], [tca])
                    if c > 0:
                        silu_chunk(c - 1)
                silu_chunk(7)
                cp("dve", hist[:, l], xraw[:, :, npr:npr + 3], t_xraw8, [t_hist[l]])
                cs_jobs = []
                if last_group:
                    cs_jobs.append((3, lambda k: xT[:, k, npr - 3:npr], scp[l]))
                if has_s:
                    for i in range(3):
                        cs_jobs.append((16, lambda k, i=i: xT[:, k, npr + i + 1:npr + 64:4], scs[l, i]))
                for (mrows, lvf, dst) in cs_jobs:
                    pc_, tpc = PF()
                    pc2, tpc2 = PF()
                    for k in range(8):
                        mm(pc_[:mrows, 0:512], lvf(k), Wx0[:, k, :], k == 0, k == 7, [t_xT, twx0], [tpc])
                    for k in range(8):
                        mm(pc2[:mrows, 0:256], lvf(k), Wx1[:, k, :], k == 0, k == 7, [t_xT, twx1], [tpc2])
                    cp("act", sc3[:mrows, 0:512], pc_[:mrows, 0:512], [tpc], [t_sc3])
                    cp("act", sc3[:mrows, 512:768], pc2[:mrows, 0:256], [tpc2], [t_sc3])
                    P.dma("sp", dst, sc3[:mrows, :], r=[t_sc3], key="sc3out", final=True)
                Wz, twz = wload(l, "sz")
                Wdt, twdt = wload(l, "sdt", live=1)
                ph("ssd_blk")
                blks = list(enumerate(g))
                V = lambda bi: (dtt_all[:, bi, :], gss_all[:, bi, :], eGt_all[:, bi, :], wsd_all[:, bi, :])
                pds = {}
                for bi, (kind, nt, c0, pos0) in blks:
                    pds[bi] = proj_tok(Wdt, twdt, 0, 8, c0, nt)
                for bi, (kind, nt, c0, pos0) in blks:
                    dtt, gss, eGt, wsd = V(bi)
                    pd, tpd = pds[bi]
                    tt("dve", dtt[:nt, :], pd[:nt, 0:8], small_t[:nt, l, 8:16], ALU.add, [tpd, tc_], [t_dt4[bi]])
                for bi, (kind, nt, c0, pos0) in blks:
                    dtt, gss, eGt, wsd = V(bi)
                    act(dtt[:nt, :], dtt[:nt, :], AF.Exp, [t_dt4[bi]], [t_dt4[bi]])
                    act(dtt[:nt, :], dtt[:nt, :], AF.Ln, [t_dt4[bi]], [t_dt4[bi]], bias=1.0)
                for bi, (kind, nt, c0, pos0) in blks:
                    dtt, gss, eGt, wsd = V(bi)
                    tt("dve", gss[:nt, :], dtt[:nt, :], aneg[:nt, l, :], ALU.mult, [t_dt4[bi], tc_], [t_dt4[bi]])
                    if kind == "S":
                        tt("dve", gm[:nt, :, :], bc(gss[:nt, :].unsqueeze(1), [nt, 16, 8]),
                           bc(smtok[:nt, :].unsqueeze(2), [nt, 16, 8]), ALU.mult, [t_dt4[bi], tc_], [t_gm])
                pqs = {}
                for bi, (kind, nt, c0, pos0) in blks:
                    dtt, gss, eGt, wsd = V(bi)
                    U = ca_s if kind == "S" else ca_b
                    SLm = sl_s if kind == "S" else sl_b
                    nsq = 16 if kind == "S" else 1
                    gmv = gm[:nt, :, :].rearrange("p j h -> p (j h)") if kind == "S" else gss[:nt, :]
                    pq, tpq = PF()
                    pqs[bi] = (pq, tpq)
                    mm(pq[:nt, 0:8], U[:nt, :nt], gss[:nt, :], True, True, [tc_, t_dt4[bi]], [tpq])
                    mm(pq[:nt, 8:16], SLm[:nt, :nt], gss[:nt, :], True, True, [tc_, t_dt4[bi]], [tpq])
                    mm(pq[:, 128:128 + nsq * 8], ones_f[:nt, :], gmv, True, True, [tc_, t_gm, t_dt4[bi]], [tpq])
                for bi, (kind, nt, c0, pos0) in blks:
                    dtt, gss, eGt, wsd = V(bi)
                    pq, tpq = pqs[bi]
                    act(eGt[:nt, :], pq[:nt, 0:8], AF.Exp, [tpq], [t_eGt4[bi]])
                    act(wsd[:nt, :], pq[:nt, 8:16], AF.Exp, [tpq], [t_eGt4[bi]])
                    if kind == "S":
                        act(eGe[:, 0:16, :], pq[:, 128:256].rearrange("p (j h) -> p j h", h=8), AF.Exp, [tpq], [t_eGe])
                    else:
                        act(eGe_all[:, bi, :], pq[:, 128:136], AF.Exp, [tpq], [t_eGe])
                for bi, (kind, nt, c0, pos0) in blks:
                    dtt, gss, eGt, wsd = V(bi)
                    tt("dve", wsd[:nt, :], wsd[:nt, :], dtt[:nt, :], ALU.mult, [t_eGt4[bi], t_dt4[bi]], [t_eGt4[bi]])
                for bi, (kind, nt, c0, pos0) in enumerate(g):
                    U = ca_s if kind == "S" else ca_b
                    SLm = sl_s if kind == "S" else sl_b
                    nsq = 16 if kind == "S" else 1
                    pt, tp = PB()
                    for c in range(4):
                        tr(pt[:nt, c * 128:(c + 1) * 128], xcT[:, c, c0:c0 + nt], [t_xcT], [tp])
                    for gg in range(2):
                        tr(pt[:nt, 512 + gg * 64:512 + (gg + 1) * 64], xcT[0:64, 4 + gg, c0:c0 + nt], [t_xcT], [tp])
                    cp("act", xstok[:nt, :], pt[:nt, 0:512], [tp], [t_xstok])
                    cp("act", btok[:nt, :], pt[:nt, 512:640], [tp], [t_xstok])
                    dtt = dtt_all[:, bi, :]; gss = gss_all[:, bi, :]; eGt = eGt_all[:, bi, :]; wsd = wsd_all[:, bi, :]
                    tt("dve", lhd[:nt, :, :nt], bc(SLm[:nt, :nt].unsqueeze(1), [nt, 8, nt]),
                       bc(gss[:nt, :].unsqueeze(2), [nt, 8, nt]), ALU.mult, [tc_, t_dt4[bi]], [t_lhd, t_qk])
                    tt("dve", m2[:nt, :, :nt], bc(U[:nt, :nt].unsqueeze(1), [nt, 8, nt]),
                       bc(dtt[:nt, :].unsqueeze(2), [nt, 8, nt]), ALU.mult, [tc_, t_dt4[bi]], [t_m2])
                    for half in range(2):
                        pdf, tpdf = PF()
                        for hh in range(4):
                            h = half * 4 + hh
                            mm(pdf[:nt, hh * 128:hh * 128 + nt], lhd[:nt, h, :nt], U[:nt, :nt], True, True,
                               [t_lhd, tc_], [tpdf])
                        act(dec[:nt, half * 4:half * 4 + 4, :nt],
                            pdf[:].rearrange("p (h t) -> p h t", h=4)[:nt, :, :nt], AF.Exp, [tpdf], [t_dec])
                    pz, tpz = proj_tok(Wz, twz, 0, 512, c0, nt)
                    tt("dve", dec[:nt, :, :nt], dec[:nt, :, :nt], m2[:nt, :, :nt], ALU.mult, [t_dec, t_m2], [t_dec])
                    pbc, tpbc = PF()
                    for gg in range(2):
                        mm(pbc[:nt, gg * 128:gg * 128 + nt], xcT[0:64, 4 + gg, c0:c0 + nt], xcT[0:64, 6 + gg, c0:c0 + nt],
                           True, True, [t_xcT], [tpbc])
                    for gg in range(2):
                        tt("dve", attb[:nt, gg * 4:gg * 4 + 4, :nt],
                           bc(pbc[:nt, gg * 128:gg * 128 + nt].unsqueeze(1), [nt, 4, nt]),
                           dec[:nt, gg * 4:gg * 4 + 4, :nt], ALU.mult, [tpbc, t_dec], [t_att])
                    act(junk[:nt, :], pz[:nt, :], AF.Exp, [tpz], [t_junk], scale=-1.0)
                    act(junk[:nt, :], junk[:nt, :], AF.Ln, [t_junk], [t_junk], bias=1.0)
                    act(junk[:nt, :], junk[:nt, :], AF.Exp, [t_junk], [t_junk], scale=-1.0)
                    tt("dve", srt[:nt, :], pz[:nt, :], junk[:nt, :], ALU.mult, [tpz, t_junk], [t_srt])
                    py, tpy = PL(0)
                    for h in range(8):
                        mm(py[:nt, h * 64:(h + 1) * 64], attb[:nt, h, :nt], xstok[:nt, h * 64:(h + 1) * 64], True, False,
                           [t_att, t_xstok], [tpy])
                        mm(py[:nt, h * 64:(h + 1) * 64], dmat[:nt, h, :nt], xstok[:nt, h * 64:(h + 1) * 64], False, True,
                           [t_dmat, t_xstok], [tpy])
                    tt("dve", xw[:nt, :].rearrange("p (h v) -> p h v", h=8), xstok[:nt, :].rearrange("p (h v) -> p h v", h=8),
                       bc(wsd[:nt, :].unsqueeze(2), [nt, 8, 64]), ALU.mult, [t_xstok, t_eGt4[bi]], [t_xw])
                    pi_, tpi = PL(1)
                    S0v = S0[0:64, :].rearrange("p (j a v) -> p j a v", j=16, a=4)
                    S0bv = S0b[0:64, :].rearrange("p (j a v) -> p j a v", j=16, a=4)
                    if kind != "S":
                        for h in range(8):
                            mm(pi_[:nt, h * 64:(h + 1) * 64], xcT[0:64, 6 + h // 4, c0:c0 + nt], Ssb[:, l, h, :], True, True,
                               [t_xcT, t_Ss[l]], [tpi])
                    else:
                        for gg in range(2):
                            P.dma("sp", S0[0:64, :], ss_in[l, gg].rearrange("p j a v -> p (j a v)"), w=[t_S0], key="S0")
                            P.dma("pool", S0b[0:64, :], ss_in[l, gg].rearrange("p j a v -> p (j a v)"), w=[t_S0b], key="S0b",
                                  max_dma_last_dim=4096)
                            tt("dve", cTm[:], bc(xcT[0:64, 6 + gg, c0:c0 + 64].unsqueeze(1), [64, 16, 64]), smT[0:64],
                               ALU.mult, [t_xcT, tc_], [t_cTm])
                            for hh in range(4):
                                h = gg * 4 + hh
                                for j in range(16):
                                    mm(pi_[:nt, h * 64:(h + 1) * 64], cTm[:, j, :], S0bv[:, j, hh, :], j == 0, j == 15,
                                       [t_cTm, t_S0b], [tpi])
                            tt("dve", btm[:], bc(btok[:64, gg * 64:(gg + 1) * 64].unsqueeze(1), [64, 16, 64]),
                               bc(smtok[:64, :].unsqueeze(2), [64, 16, 64]), ALU.mult, [t_xstok, tc_], [t_btm])
                            for rnd in range(2):
                                banks = []
                                for q in range(4):
                                    ps_, tps = PF()
                                    banks.append((ps_, tps))
                                    for jj in range(2):
                                        j = rnd * 8 + q * 2 + jj
                                        mm(ps_[:64, jj * 256:(jj + 1) * 256], btm[:, j, :], xw[:64, gg * 256:(gg + 1) * 256],
                                           True, True, [t_btm, t_xw], [tps])
                                for q in range(4):
                                    ps_, tps = banks[q]
                                    j0 = rnd * 8 + q * 2
                                    tt("dve", S0v[:, j0:j0 + 2], S0v[:, j0:j0 + 2],
                                       bc(eGe[0:64, j0:j0 + 2, gg * 4:gg * 4 + 4].unsqueeze(3), [64, 2, 4, 64]), ALU.mult,
                                       [t_S0, t_eGe], [t_S0])
                                    tt("dve", S0v[:, j0:j0 + 2], S0v[:, j0:j0 + 2],
                                       ps_[:64, :].rearrange("p (j a v) -> p j a v", j=2, a=4), ALU.add, [t_S0, tps], [t_S0])
                            P.dma("sp", sss[l, gg].rearrange("p j a v -> p (j a v)"), S0[0:64, :], r=[t_S0], key="S0out",
                                  final=True)
                    tt("dve", ysb[:nt, :].rearrange("p (h v) -> p h v", h=8), pi_[:nt, :].rearrange("p (h v) -> p h v", h=8),
                       bc(eGt[:nt, :].unsqueeze(2), [nt, 8, 64]), ALU.mult, [tpi, t_eGt4[bi]], [t_ysb])
                    tt("dve", ysb[:nt, :], ysb[:nt, :], py[:nt, :], ALU.add, [t_ysb, tpy], [t_ysb])
                    tt("dve", ysb[:nt, :], ysb[:nt, :], srt[:nt, :], ALU.mult, [t_ysb, t_srt], [t_ysb])
                    act(junk[:nt, :], ysb[:nt, :], AF.Square, [t_ysb], [t_junk])
                    P.op("dve", lambda e, nt=nt: e.reduce_sum(out=ssq[:nt, 0:2], in_=junk[:nt, :].rearrange("p (g v) -> p g v", g=2), axis=AX.X),
                         r=[t_junk], w=[t_ssq])
                    act(ssq[:nt, 0:2], ssq[:nt, 0:2], AF.Ln, [t_ssq], [t_ssq], bias=1e-6, scale=1.0 / 256.0)
                    act(ssq[:nt, 4:6], ssq[:nt, 0:2], AF.Exp, [t_ssq], [t_ssq], scale=-0.5)
                    tt("dve", ybf[:nt, :].rearrange("p (g v) -> p g v", g=2), ysb[:nt, :].rearrange("p (g v) -> p g v", g=2),
                       bc(ssq[:nt, 4:6].unsqueeze(2), [nt, 2, 256]), ALU.mult, [t_ysb, t_ssq], [t_ybf])
                    pt, tp = PB()
                    for c in range(4):
                        tr(pt[:, c * 128:c * 128 + nt], ybf[:nt, c * 128:(c + 1) * 128], [t_ybf], [tp])
                    pv = pt[:].rearrange("p (c t) -> p c t", c=8)[:, 0:4, 0:nt]
                    tt("dve", yT[:, 4:8, c0:c0 + nt], pv, bc(small_t[:, l, 64:68].unsqueeze(2), [128, 4, nt]), ALU.mult,
                       [tp, tc_], [t_yT[1]])
                    if kind != "S":
                        ps_, tps = PF()
                        for gg in range(2):
                            mm(ps_[:64, gg * 256:(gg + 1) * 256], btok[:nt, gg * 64:(gg + 1) * 64], xw[:nt, gg * 256:(gg + 1) * 256],
                               True, True, [t_xstok, t_xw], [tps])
                        tt("dve", Ss[:, l], Ss[:, l], bc(eGe_all[0:64, bi, :].unsqueeze(2), [64, 8, 64]), ALU.mult,
                           [t_Ss[l], t_eGe], [t_Ss[l]])
                        tt("dve", Ss[:, l], Ss[:, l], ps_[:64, :].rearrange("p (h v) -> p h v", h=8), ALU.add,
                           [t_Ss[l], tps], [t_Ss[l]])
                        cp("act", Ssb[:, l], Ss[:, l], [t_Ss[l]], [t_Ss[l]])
                        if last_group and bi == nblk - 1:
                            P.dma("sp", ssp[l], Ss[:, l], r=[t_Ss[l]], key="pfinal", final=True)
                    else:
                        pass
                chk("ssd")
                ph("ret_proj")
                W, tw = wload(l, "rqk")
                W2_, tw2 = wload(l, "rsw", live=1)
                rt = lnrt[0:64, :].rearrange("p (a h c) -> p a h c", a=2, h=4)
                for which, dstb in ((0, qTb), (1, kTb)):
                    P.dma("sp", lnrt[0:64, :], cst["rtab"][gi, which].rearrange("p a h c -> p (a h c)"), w=[t_rtab], key="lnbc")
                    for h in range(4):
                        pt, tp = proj_feat(W, tw, which * 256 + h * 64, 64, xT, t_xT, 8, ncols)
                        tt("dve", qTf[:, h, :ncols], pt[:64, :ncols], rt[:, 0, h, :ncols], ALU.mult, [tp, t_rtab], [t_qk, t_lhd])
                        pt, tp = proj_feat(W2_, tw2, which * 256 + h * 64, 64, xT, t_xT, 8, ncols)
                        tt("dve", kTf[:, h, :ncols], pt[:64, :ncols], rt[:, 1, h, :ncols], ALU.mult, [tp, t_rtab], [t_qk, t_dec])
                        tt("dve", dstb[:, h, :ncols], qTf[:, h, :ncols], kTf[:, h, :ncols], ALU.add, [t_qk], [t_qkb])
                ph("ret_blk")
                Wv, twv = wload(l, "rv")
                Wg_, twg = wload(l, "rg", live=1)
                def ret_pre(bi):
                    kind, nt, c0, pos0 = g[bi]
                    i2 = bi % 2
                    pt, tp = proj_tok(Wg_, twg, 0, 512, c0, nt)
                    pv_, tpv_ = proj_tok(Wv, twv, 0, 512, c0, nt)
                    yield
                    act(junkB[:nt, :], pt[:nt, :], AF.Exp, [tp], [t_junkB], scale=-1.0)
                    act(junkB[:nt, :], junkB[:nt, :], AF.Ln, [t_junkB], [t_junkB], bias=1.0)
                    act(junkB[:nt, :], junkB[:nt, :], AF.Exp, [t_junkB], [t_junkB], scale=-1.0)
                    cp("act", vtok2[:nt, i2, :], pv_[:nt, :], [tpv_], [t_vtok2[i2]])
                    tt("dve", srt2[:nt, i2, :], pt[:nt, :], junkB[:nt, :], ALU.mult, [tp, t_junkB], [t_srt2[i2]])
                for _ in ret_pre(0):
                    pass
                for bi, (kind, nt, c0, pos0) in enumerate(g):
                    i2 = bi % 2
                    egp = regend[:, KIND[kind], :]
                    egs = bc(regend[:, 2, :].unsqueeze(1), [64, 16, 4])
                    lastp = last_group and bi == nblk - 1
                    hook = (lambda bi=bi: ret_pre(bi + 1)) if bi + 1 < nblk else None
                    lin_block(l, kind, nt, c0, Sr, Srb, t_Sr[l], egp, egs, 8, None, sr_in, srp, srs, lastp,
                              vtok2[:, i2, :], t_vtok2[i2], srt2[:, i2, :], t_srt2[i2], hook)
                chk("ret")
                ph("merge")
                macc = big[:, 0:8192].bitcast(F32).rearrange("p (c t) -> p c t", c=8)
                for q in range(2):
                    for br in range(3):
                        Wg2, twg2 = wload(l, "mg%d%d" % (q, br))
                        Wo2, two2 = wload(l, "mo%d%d" % (q, br), live=1)
                        for cc in range(4):
                            dc = q * 4 + cc
                            pg_, tpg_ = proj_feat(Wg2, twg2, cc * 128, 128, xT, t_xT, 8, ncols)
                            jk, tjk = (junkA, t_junkA) if cc % 2 == 0 else (junkB, t_junkB)
                            act(jk[:, :ncols], pg_[:, :ncols], AF.Sigmoid, [tpg_], [tjk])
                            po_, tpo_ = proj_feat(Wo2, two2, cc * 128, 128, yT, t_yT[br], 4, ncols, koff=br * 4)
                            if br == 0:
                                tt("dve", macc[:, dc, :ncols], jk[:, :ncols], po_[:, :ncols], ALU.mult, [tjk, tpo_], [t_xraw])
                            else:
                                tca = t_cacc2[cc % 2]
                                tt("dve", cacc[:, cc % 2, :ncols], jk[:, :ncols], po_[:, :ncols], ALU.mult, [tjk, tpo_], [tca])
                                if br == 1:
                                    tt("dve", macc[:, dc, :ncols], macc[:, dc, :ncols], cacc[:, cc % 2, :ncols], ALU.add,
                                       [t_xraw, tca], [t_xraw])
                                else:
                                    tt("dve", mT[:, dc, :ncols], macc[:, dc, :ncols], cacc[:, cc % 2, :ncols], ALU.add,
                                       [t_xraw, tca], [t_mT])
                chk("merge")
                ph("wo_ln1")
                Wo0 = wload(l, "wo0")
                Wo1 = wload(l, "wo1", live=1)
                for bi, (kind, nt, c0, pos0) in enumerate(g):
                    for hf, (Wq, twq) in enumerate((Wo0, Wo1)):
                        pt, tp = PF()
                        for k in range(8):
                            mm(pt[:nt, :], mT[:, k, c0:c0 + nt], Wq[:, k, :], k == 0, k == 7, [t_mT, twq], [tp])
                        stt(xtok[:nt, bi, hf * 512:(hf + 1) * 512], xtok[:nt, bi, hf * 512:(hf + 1) * 512], ALPHA, pt[:nt, :],
                            ALU.mult, ALU.add, [t_xtok[bi], tp], [t_xtok[bi]])
                layernorm(g, 1 + 2 * l)
                chk("ln1")
                ph("ff1")
                for i in range(8):
                    W1, tw1 = wload(l, "f1%d" % i)
                    for cc in range(4):
                        f = i * 4 + cc
                        pt, tp = proj_feat(W1, tw1, cc * 128, 128, xT, t_xT, 8, ncols)
                        jk, tjk = (junkA, t_junkA) if f % 2 == 0 else (junkB, t_junkB)
                        ts("dve", jk[:, :ncols], pt[:, :ncols], b1col_t[:, l, f:f + 1], 0.0, ALU.add, ALU.max, [tp, tc_], [tjk])
                        act(hid[:, f, :ncols], jk[:, :ncols], AF.Square, [tjk], [t_hid[i]])
                ph("ff2_ln2")
                for half in range(2):
                    accs = [PF() for _ in range(nblk)]
                    for fg in range(4):
                        W2f, tw2f = wload(l, "f2%d%d" % (half, fg))
                        for bi, (kind, nt, c0, pos0) in enumerate(g):
                            pt, tp = accs[bi]
                            for k in range(8):
                                f = fg * 8 + k
                                mm(pt[:nt, :], hid[:, f, c0:c0 + nt], W2f[:, k, :], fg == 0 and k == 0, False,
                                   [t_hid[f // 4], tw2f], [tp])
                            if fg == 3:
                                o_ = 256 + half * 512
                                mm(pt[:nt, :], ones_b[0:33, :nt], rowhl[0:33, l, o_:o_ + 512], False, True, [tc_], [tp])
                    for bi, (kind, nt, c0, pos0) in enumerate(g):
                        pt, tp = accs[bi]
                        stt(xtok[:nt, bi, half * 512:(half + 1) * 512], xtok[:nt, bi, half * 512:(half + 1) * 512], ALPHA,
                            pt[:nt, :], ALU.mult, ALU.add, [t_xtok[bi], tp], [t_xtok[bi]])
                layernorm(g, 2 + 2 * l)
            for bi, (kind, nt, c0, pos0) in enumerate(g):
                if kind == "B":
                    P.dma("sp", yp[pos0 - 16:pos0 - 16 + nt, :], xtok[:nt, bi, :], r=[t_xtok[bi]], key=("yout", bi), final=True)
                elif kind == "S":
                    P.dma("sp", ys, xtok[:nt, bi, :], r=[t_xtok[bi]], key=("yout", bi), final=True)


    try:
        main()
    except _Stop:
        pass
    P.finish()
    for cm in reversed(cms):
        cm.__exit__(None, None, None)
    global LAST_PHASES
    LAST_PHASES = P.phases
    return nc


LAST_PHASES = None
_CACHE = {}


def kernel(x_prompt, x_sample, state_gla, state_ssm, state_conv, state_ret, meta_tokens,
           ln_in_w, ln_in_b, w_in, w_gla_a2, b_gla_a, w_gla_norm, conv_w, conv_b, dt_bias,
           a_log, d_skip, w_ssm_norm, w_gla_out, w_ssm_out, w_ret_out, w_o, ln1_w, ln1_b,
           w_ff1, b_ff1, w_ff2, b_ff2, ln2_w, ln2_b):
    f = lambda a: np.ascontiguousarray(np.asarray(a, dtype=np.float32))
    (x_prompt, x_sample, state_gla, state_ssm, state_conv, state_ret, meta_tokens, ln_in_w, ln_in_b, w_in,
     w_gla_a2, b_gla_a, w_gla_norm, conv_w, conv_b, dt_bias, a_log, d_skip, w_ssm_norm, w_gla_out, w_ssm_out,
     w_ret_out, w_o, ln1_w, ln1_b, w_ff1, b_ff1, w_ff2, b_ff2, ln2_w, ln2_b) = [f(a) for a in (
        x_prompt, x_sample, state_gla, state_ssm, state_conv, state_ret, meta_tokens, ln_in_w, ln_in_b, w_in,
        w_gla_a2, b_gla_a, w_gla_norm, conv_w, conv_b, dt_bias, a_log, d_skip, w_ssm_norm, w_gla_out, w_ssm_out,
        w_ret_out, w_o, ln1_w, ln1_b, w_ff1, b_ff1, w_ff2, b_ff2, ln2_w, ln2_b)]
    if "nc" not in _CACHE:
        _CACHE["nc"] = build_program()
        _CACHE["consts"] = build_consts()
    nc = _CACHE["nc"]
    consts = _CACHE["consts"]
    wstream = build_wstream(w_in, w_gla_out, w_ssm_out, w_ret_out, w_o, w_ff1, w_ff2)
    lnw = [ln_in_w, ln1_w[0], ln2_w[0], ln1_w[1], ln2_w[1]]
    lnb = [ln_in_b, ln1_b[0], ln2_b[0], ln1_b[1], ln2_b[1]]
    lnbc = np.empty((5, 2, 128, D), np.float32)
    lncol = np.empty((128, 5, 2, 8), np.float32)
    for i in range(5):
        lnbc[i, 0] = lnw[i][None, :]
        lnbc[i, 1] = lnb[i][None, :]
        lncol[:, i, 0, :] = lnw[i].reshape(8, 128).T
        lncol[:, i, 1, :] = lnb[i].reshape(8, 128).T
    small = np.zeros((128, L, 80), np.float32)
    rowp = np.zeros((1, L, 1280), np.float32)
    for l in range(L):
        small[:, l, 0:8] = a_log[l][None, :]
        small[:, l, 8:16] = dt_bias[l][None, :]
        small[:, l, 16:24] = d_skip[l][None, :]
        for c in range(8):
            if c < 4:
                ch = np.arange(c * 128, (c + 1) * 128)
            else:
                ch = np.arange(512 + (c - 4) * 64, 512 + (c - 3) * 64)
            small[:len(ch), l, 24 + c] = conv_b[l][ch]
            for i in range(4):
                small[:len(ch), l, 32 + c * 4 + i] = conv_w[l][i, ch]
        small[:, l, 64:68] = w_ssm_norm[l].reshape(4, 128).T
        small[:, l, 68] = w_gla_norm[l]
        rowp[0, l, 0:256] = b_gla_a[l]
        rowp[0, l, 256:1280] = b_ff2[l]
    wa2 = np.ascontiguousarray(w_gla_a2.transpose(1, 0, 2))
    b1col = np.ascontiguousarray(b_ff1.reshape(L, 32, 128).transpose(2, 0, 1))
    in_maps = []
    for c in range(NCORES):
        bs = slice(c * NSEQ, (c + 1) * NSEQ)
        def lin_state(s):
            s = s[:, bs].reshape(L, NSEQ, 2, 2, 64, 128)
            return np.ascontiguousarray(s.transpose(0, 2, 4, 1, 3, 5))
        s = state_ssm[:, bs].reshape(L, NSEQ, 2, 4, 64, 64)
        ss_in = np.ascontiguousarray(s.transpose(0, 2, 4, 1, 3, 5))
        sc_in = np.zeros((L, 128, 8, NSEQ, 3), np.float32)
        sc = state_conv[:, bs]
        for cc in range(8):
            if cc < 4:
                ch = np.arange(cc * 128, (cc + 1) * 128)
            else:
                ch = np.arange(512 + (cc - 4) * 64, 512 + (cc - 3) * 64)
            sc_in[:, :len(ch), cc] = sc[:, :, :, ch].transpose(0, 3, 1, 2)
        m = {"xp": x_prompt[c], "xs": x_sample[bs].reshape(NSEQ * DSEQ, D), "meta": meta_tokens,
             "wst": wstream, "lnbc": lnbc, "lncol": lncol, "small": small, "rowp": rowp.reshape(1, L * 1280), "wa2": wa2, "b1col": b1col,
             "sg_in": lin_state(state_gla), "sr_in": lin_state(state_ret), "ss_in": ss_in, "sc_in": sc_in}
        for n in CONST_NAMES:
            m["c_" + n] = consts[n]
        in_maps.append(m)
    if DBG_CORES:
        res = run_bass_kernel_spmd(nc, in_maps[:DBG_CORES], core_ids=list(range(DBG_CORES)))
        R = [res.results[c % DBG_CORES] for c in range(NCORES)]
    else:
        res = run_bass_kernel_spmd(nc, in_maps, core_ids=list(range(NCORES)))
        R = res.results
    y_prompt = np.stack([R[c]["yp"] for c in range(NCORES)])
    y_sample = np.concatenate([R[c]["ys"].reshape(NSEQ, DSEQ, D) for c in range(NCORES)], axis=0)

    def lin_p(name):
        o = np.stack([R[c][name] for c in range(NCORES)], axis=1)
        return np.ascontiguousarray(o.transpose(0, 1, 3, 2, 4))

    def lin_s(name):
        o = np.concatenate([R[c][name].transpose(0, 3, 1, 4, 2, 5).reshape(L, NSEQ, 4, 64, 128)
                            for c in range(NCORES)], axis=1)
        return np.ascontiguousarray(o)
    ssm_p = np.stack([R[c]["ssp"] for c in range(NCORES)], axis=1)
    ssm_p = np.ascontiguousarray(ssm_p.transpose(0, 1, 3, 2, 4))
    ssm_s = np.concatenate([R[c]["sss"].transpose(0, 3, 1, 4, 2, 5).reshape(L, NSEQ, 8, 64, 64)
                            for c in range(NCORES)], axis=1)
    conv_p = np.stack([R[c]["scp"] for c in range(NCORES)], axis=1)
    conv_s = np.concatenate([R[c]["scs"].transpose(0, 2, 1, 3) for c in range(NCORES)], axis=1)
    return (y_prompt.astype(np.float32), y_sample.astype(np.float32),
            lin_p("sgp"), lin_s("sgs"), ssm_p, np.ascontiguousarray(ssm_s),
            np.ascontiguousarray(conv_p), np.ascontiguousarray(conv_s), lin_p("srp"), lin_s("srs"))
```

```python
import numpy as np
import concourse.bass as bass
import concourse.mybir as mybir
from concourse.bass_utils import run_bass_kernel_spmd

F32 = mybir.dt.float32
BF16 = mybir.dt.bfloat16
AF = mybir.ActivationFunctionType
ALU = mybir.AluOpType
AX = mybir.AxisListType

D = 1024
SEQ = 2048
NMETA = 16
NSEQ = 16
DSEQ = 4
PAST = 16384
L = 2
ALPHA = float((2 * L) ** 0.25)
NCORES = 8
STOP = None
DBG_CORES = None

GROUPS = []
_g0 = [("M", 16, 0, 0)]
for i in range(2):
    _g0.append(("B", 128, 16 + 128 * i, 16 + 128 * i))
_g0.append(("S", 64, 272, PAST))
GROUPS.append(_g0)
_b = 2
for n in (3, 3, 4, 4):
    g = []
    for i in range(n):
        g.append(("B", 128, 128 * i, 16 + 128 * (_b + i)))
    _b += n
    GROUPS.append(g)
NG = len(GROUPS)


def gcols(g):
    return sum(b[1] for b in g)


WT = [("gqk", 8, 512), ("ga", 8, 16), ("gv", 8, 512), ("gr", 8, 512),
      ("sx0", 8, 512), ("sx1", 8, 256), ("sz", 8, 512), ("sdt", 8, 8),
      ("rqk", 8, 512), ("rsw", 8, 512), ("rv", 8, 512), ("rg", 8, 512)]
for q in range(2):
    for br in range(3):
        WT.append(("mg%d%d" % (q, br), 8, 512))
        WT.append(("mo%d%d" % (q, br), 4, 512))
WT += [("wo0", 8, 512), ("wo1", 8, 512)]
for i in range(8):
    WT.append(("f1%d" % i, 8, 512))
for half in range(2):
    for fg in range(4):
        WT.append(("f2%d%d" % (half, fg), 8, 512))
WOFF = {}
_o = 0
for nm, kc, ncl in WT:
    WOFF[nm] = (_o, kc, ncl)
    _o += kc * ncl
WTOT = _o
NSLOT = 3


def _perm_half():
    p = np.arange(256).reshape(4, 2, 32)[:, ::-1, :].reshape(256)
    return p


def build_wstream(w_in, w_gla_out, w_ssm_out, w_ret_out, w_o, w_ff1, w_ff2):
    out = np.empty((L, 128, WTOT), np.float32)
    offs = np.cumsum([0, 256, 256, 512, 512, 16, 512, 768, 8, 256, 256, 512, 512, 3072])
    o = {n: offs[i] for i, n in enumerate(["gq", "gk", "gv", "gr", "ga", "sz", "sx", "sdt", "rq", "rk", "rv", "rg", "gate"])}
    ph = _perm_half()
    for l in range(L):
        wi = w_in[l]
        def put(nm, mat):
            off, kc, ncl = WOFF[nm]
            assert mat.shape == (kc * 128, ncl), (nm, mat.shape)
            out[l, :, off:off + kc * ncl] = mat.reshape(kc, 128, ncl).transpose(1, 0, 2).reshape(128, kc * ncl)
        put("gqk", wi[:, o["gq"]:o["gq"] + 512])
        put("gv", wi[:, o["gv"]:o["gv"] + 512])
        put("gr", wi[:, o["gr"]:o["gr"] + 512])
        put("ga", wi[:, o["ga"]:o["ga"] + 16])
        put("sz", wi[:, o["sz"]:o["sz"] + 512])
        put("sdt", wi[:, o["sdt"]:o["sdt"] + 8])
        put("sx0", wi[:, o["sx"]:o["sx"] + 512])
        put("sx1", wi[:, o["sx"] + 512:o["sx"] + 768])
        put("rqk", wi[:, o["rq"]:o["rq"] + 512])
        rq = wi[:, o["rq"]:o["rq"] + 256][:, ph]
        rk = wi[:, o["rk"]:o["rk"] + 256][:, ph]
        put("rsw", np.concatenate([rq, rk], axis=1))
        put("rv", wi[:, o["rv"]:o["rv"] + 512])
        put("rg", wi[:, o["rg"]:o["rg"] + 512])
        outs = [w_gla_out[l], w_ssm_out[l], w_ret_out[l]]
        for q in range(2):
            for br in range(3):
                put("mg%d%d" % (q, br), wi[:, o["gate"] + br * 1024 + q * 512:o["gate"] + br * 1024 + q * 512 + 512])
                put("mo%d%d" % (q, br), outs[br][:, q * 512:(q + 1) * 512])
        put("wo0", w_o[l][:, 0:512])
        put("wo1", w_o[l][:, 512:1024])
        for i in range(8):
            put("f1%d" % i, w_ff1[l][:, i * 512:(i + 1) * 512])
        for half in range(2):
            for fg in range(4):
                put("f2%d%d" % (half, fg), w_ff2[l][fg * 1024:(fg + 1) * 1024, half * 512:(half + 1) * 512])
    return out


def build_consts():
    c = {}
    i = np.arange(128)
    ca = (i[:, None] <= i[None, :]).astype(np.float32)
    sl = (i[:, None] > i[None, :]).astype(np.float32)
    same = ((i[:, None] // 4) == (i[None, :] // 4)).astype(np.float32)
    c["ca_b"] = ca
    c["sl_b"] = sl
    c["ca_s"] = ca * same
    c["sl_s"] = sl * same
    c["ident"] = np.eye(128, dtype=np.float32)
    sm = np.zeros((128, 16), np.float32)
    for s in range(64):
        sm[s, s // 4] = 1.0
    c["smtok"] = sm
    smT = np.zeros((128, 16, 64), np.float32)
    for t in range(64):
        smT[:, t // 4, t] = 1.0
    c["smT"] = smT
    lg = np.log1p(-np.exp2(-5.0 - np.arange(4, dtype=np.float64)))
    invf = (10000.0 ** (-(np.arange(32, dtype=np.float32) / np.float32(32.0)))).astype(np.float32)
    tabs = []
    for g in GROUPS:
        tab = np.zeros((2, 64, 2, 4, 512), np.float32)
        for (kind, nt, c0, pos0) in g:
            for t in range(nt):
                if kind == "S":
                    pos = PAST + (t % 4)
                    idx = (t % 4) + 1
                else:
                    pos = pos0 + t
                    idx = t + 1
                ang = (np.float32(pos) * invf).astype(np.float32).astype(np.float64)
                cfull = np.concatenate([np.cos(ang), np.cos(ang)])
                sfull = np.concatenate([-np.sin(ang), np.sin(ang)])
                for h in range(4):
                    eg = np.exp(lg[h] * idx)
                    tab[0, :, 0, h, c0 + t] = cfull * eg
                    tab[0, :, 1, h, c0 + t] = sfull * eg
                    tab[1, :, 0, h, c0 + t] = cfull / eg * 0.125
                    tab[1, :, 1, h, c0 + t] = sfull / eg * 0.125
        tabs.append(tab)
    c["rtab"] = np.stack(tabs)
    eg = np.zeros((64, 3, 4), np.float32)
    for ki, n in enumerate((128, 16, 4)):
        for h in range(4):
            eg[:, ki, h] = np.exp(lg[h] * n)
    c["regend"] = eg
    return c


CONST_NAMES = ["ca_b", "sl_b", "ca_s", "sl_s", "ident", "smtok", "smT", "rtab", "regend"]


class Tok:
    __slots__ = ("w", "r")

    def __init__(self):
        self.w = None
        self.r = []


class Prog:
    ENG = ("pe", "act", "dve", "pool", "sp")

    def __init__(self, nc):
        self.nc = nc
        self.q = {e: [] for e in self.ENG}
        self.cnt = {e: 0 for e in self.ENG}
        self.sems = {}
        self.semvals = {}
        self.waited = {e: {} for e in self.ENG}
        self.pending_out = []
        self._stack = []
        self.phase = "init"
        self.phases = {e: [] for e in self.ENG}

    def sem(self, key):
        if key not in self.sems:
            cm = self.nc.semaphore("s%d" % len(self.sems))
            s = cm.__enter__()
            self._stack.append(cm)
            self.sems[key] = s
            self.semvals[key] = 0
        return self.sems[key]

    def _deps(self, eng, reads, writes):
        ev = {}

        def add(e):
            if e is None:
                return
            if ev.get(e[0], 0) < e[1]:
                ev[e[0]] = e[1]
        for t in reads:
            add(t.w)
        for t in writes:
            add(t.w)
            for r in t.r:
                if r[0] == eng:
                    continue
                add(r)
        out = []
        wd = self.waited[eng]
        for k, v in ev.items():
            if k == "pe" and eng == "pe":
                continue
            if wd.get(k, 0) >= v:
                continue
            wd[k] = v
            out.append((k, v))
        return out

    def op(self, eng, fn, r=(), w=()):
        waits = self._deps(eng, r, w)
        self.cnt[eng] += 1
        seq = self.cnt[eng]
        s_self = self.sem(eng)
        wl = [(self.sem(k), v) for k, v in waits]

        def emit(e, wl=wl, fn=fn, s_self=s_self):
            for s, v in wl:
                e.wait_ge(s, v)
            fn(e).then_inc(s_self, 1)
        self.q[eng].append(emit)
        self.phases[eng].append(self.phase)
        evt = (eng, seq)
        for t in r:
            t.r.append(evt)
        for t in w:
            t.w = evt
            t.r = []
        return evt

    def dma(self, eng, out, in_, r=(), w=(), key=None, final=False, **kw):
        waits = self._deps(eng, r, w)
        semkey = ("dma", key)
        s = self.sem(semkey)
        self.semvals[semkey] += 16
        val = self.semvals[semkey]
        wl = [(self.sem(k), v) for k, v in waits]

        def emit(e, wl=wl, s=s):
            for ss, v in wl:
                e.wait_ge(ss, v)
            e.dma_start(out=out, in_=in_, **kw).then_inc(s, 16)
        self.q[eng].append(emit)
        evt = (semkey, val)
        for t in r:
            t.r.append(evt)
        for t in w:
            t.w = evt
            t.r = []
        if final:
            self.pending_out.append(evt)
        return evt

    def finish(self):
        ev = {}
        for k, v in self.pending_out:
            ev[k] = max(ev.get(k, 0), v)
        wl = [(self.sem(k), v) for k, v in ev.items()]

        def emit(e):
            for s, v in wl:
                e.wait_ge(s, v)
        self.q["sp"].append(emit)
        nc = self.nc
        with nc.Block() as block:
            @block.tensor
            def _(e):
                for f in self.q["pe"]:
                    f(e)

            @block.scalar
            def _(e):
                for f in self.q["act"]:
                    f(e)

            @block.vector
            def _(e):
                for f in self.q["dve"]:
                    f(e)

            @block.gpsimd
            def _(e):
                for f in self.q["pool"]:
                    f(e)

            @block.sync
            def _(e):
                for f in self.q["sp"]:
                    f(e)
        for cm in reversed(self._stack):
            cm.__exit__(None, None, None)


def build_program():
    nc = bass.Bass("TRN2", target_bir_lowering=False)
    P = Prog(nc)
    cms = []

    def din(name, shape, dt=F32):
        return nc.dram_tensor(name, list(shape), dt, kind="ExternalInput").ap()

    def dout(name, shape):
        return nc.dram_tensor(name, list(shape), F32, kind="ExternalOutput").ap()

    def sb(name, shape, dt=F32):
        cm = nc.sbuf_tensor(name, list(shape), dt)
        t = cm.__enter__()
        cms.append(cm)
        return t

    xp = din("xp", [SEQ, D])
    xs = din("xs", [NSEQ * DSEQ, D])
    meta = din("meta", [NMETA, D])
    wst = din("wst", [L, 128, WTOT])
    lnbc = din("lnbc", [5, 2, 128, D])
    lncol = din("lncol", [128, 5, 2, 8])
    small = din("small", [128, L, 80])
    rowp = din("rowp", [1, L * 1280])
    wa2 = din("wa2", [16, L, 256])
    b1col = din("b1col", [128, L, 32])
    sg_in = din("sg_in", [L, 2, 64, NSEQ, 2, 128])
    sr_in = din("sr_in", [L, 2, 64, NSEQ, 2, 128])
    ss_in = din("ss_in", [L, 2, 64, NSEQ, 4, 64])
    sc_in = din("sc_in", [L, 128, 8, NSEQ, 3])
    cst = {}
    cshape = {"ca_b": [128, 128], "sl_b": [128, 128], "ca_s": [128, 128], "sl_s": [128, 128],
              "ident": [128, 128], "smtok": [128, 16], "smT": [128, 16, 64],
              "rtab": [NG, 2, 64, 2, 4, 512], "regend": [64, 3, 4]}
    for n in CONST_NAMES:
        cst[n] = din("c_" + n, cshape[n])
    yp = dout("yp", [SEQ, D])
    ys = dout("ys", [NSEQ * DSEQ, D])
    sgp = dout("sgp", [L, 64, 4, 128])
    srp = dout("srp", [L, 64, 4, 128])
    ssp = dout("ssp", [L, 64, 8, 64])
    scp = dout("scp", [L, 3, 768])
    sgs = dout("sgs", [L, 2, 64, NSEQ, 2, 128])
    srs = dout("srs", [L, 2, 64, NSEQ, 2, 128])
    sss = dout("sss", [L, 2, 64, NSEQ, 4, 64])
    scs = dout("scs", [L, 3, NSEQ, 768])

    pf = []
    for i in range(6):
        cm = nc.psum_tensor("pf%d" % i, [128, 512], F32)
        pf.append((cm.__enter__(), Tok()))
        cms.append(cm)
    pb = []
    for i in range(2):
        cm = nc.psum_tensor("pb%d" % i, [128, 1024], BF16)
        pb.append((cm.__enter__(), Tok()))
        cms.append(cm)
    rr = {"f": 0, "b": 0}

    def PF():
        rr["f"] = (rr["f"] + 1) % 4
        return pf[rr["f"]]

    def PL(i):
        return pf[4 + i]

    def PB():
        rr["b"] = (rr["b"] + 1) % 2
        return pb[rr["b"]]

    ca_b = sb("ca_b", [128, 128]); sl_b = sb("sl_b", [128, 128])
    ca_s = sb("ca_s", [128, 128]); sl_s = sb("sl_s", [128, 128])
    identb = sb("identb", [128, 128], BF16)
    smtok = sb("smtok", [128, 16]); smT = sb("smT", [128, 16, 64], BF16)
    regend = sb("regend", [64, 3, 4])
    ones_f = sb("ones_f", [128, 128]); ones_b = sb("ones_b", [128, 128], BF16)
    lncol_t = sb("lncol_t", [128, 5, 2, 8])
    small_t = sb("small_t", [128, L, 80])
    rowhl = sb("rowhl", [33, L, 1280], BF16)
    wa2_t = sb("wa2_t", [16, L, 256], BF16)
    b1col_t = sb("b1col_t", [128, L, 32])
    aneg = sb("aneg", [128, L, 8])
    dmat = sb("dmat", [128, 8, 128], BF16); t_dmat = Tok()
    tc_ = Tok()
    wslot = [(sb("wslot%d" % i, [128, 4096], BF16), Tok()) for i in range(NSLOT)]
    xT = sb("xT", [128, 8, 512], BF16); t_xT = Tok()
    xtok = sb("xtok", [128, 4, D]); t_xtok = [Tok() for _ in range(4)]
    yT = sb("yT", [128, 12, 512], BF16); t_yT = [Tok() for _ in range(3)]
    mT = sb("mT", [128, 8, 512], BF16); t_mT = Tok()
    big = sb("big", [128, 16384], BF16)
    hid = big[:].rearrange("p (f t) -> p f t", f=32); t_hid = [Tok() for _ in range(8)]
    xraw = big[:, 0:8240].bitcast(F32).rearrange("p (c t) -> p c t", c=8); t_xraw = Tok(); t_xraw8 = [Tok() for _ in range(8)]
    cacc = big[:, 8240:10288].bitcast(F32).rearrange("p (c t) -> p c t", c=2); t_cacc2 = [Tok(), Tok()]
    qTm = big[0:64, 10288:12336].rearrange("p (a j t) -> p a j t", a=2, j=16); t_qTm = Tok()
    cTm = big[0:64, 12336:13360].rearrange("p (j t) -> p j t", j=16); t_cTm = Tok()
    xsin = big[:, 13360:15152].bitcast(F32).rearrange("p (c j t) -> p c j t", c=8, j=16); t_xsin = Tok()
    lnrt = sb("lnrt", [128, 4096])
    lnw_bc = lnrt[:, 0:1024]; lnb_bc = lnrt[:, 1024:2048]; t_lnbc = Tok()
    rtab = lnrt[:].rearrange("p (a b c) -> p a b c", a=4, b=2); t_rtab = t_lnbc
    scrA = sb("scrA", [128, 2048]); t_qk = Tok()
    _sab = scrA[0:64, :].bitcast(BF16)
    qTf = _sab[:, 0:2048].rearrange("p (a t) -> p a t", a=4); kTf = _sab[:, 2048:4096].rearrange("p (a t) -> p a t", a=4)
    lhd = scrA[:, 0:1024].rearrange("p (h t) -> p h t", h=8); dec = scrA[:, 1024:2048].rearrange("p (h t) -> p h t", h=8)
    t_lhd = Tok(); t_dec = Tok()
    qTb = sb("qTb", [64, 4, 512], BF16); kTb = sb("kTb", [64, 4, 512], BF16); t_qkb = Tok()
    aTb = sb("aTb", [16, 512], BF16); t_aT = Tok()
    eG_a = sb("eG", [64, 4, 128]); eGn_a = sb("eGn", [64, 4, 128]); t_eG_a = Tok(); t_eG = t_eG_a
    ktok2 = sb("ktok2", [128, 2, 256], BF16); t_ktok2 = [Tok(), Tok()]
    vtok2 = sb("vtok2", [128, 2, 512], BF16); t_vtok2 = [Tok(), Tok()]
    srt2 = sb("srt2", [128, 2, 512]); t_srt2 = [Tok(), Tok()]
    srt = srt2[:, 0, :]; t_srt = t_srt2[0]
    egp_all = sb("egp_all", [64, 4, 4]); egs_all = sb("egs_all", [64, 16, 4]); t_egp = Tok()
    attb = sb("attb", [128, 8, 128], BF16); t_att2 = [Tok(), Tok()]
    ysb = sb("ysb", [128, 512]); t_ysb = Tok()
    ybf = sb("ybf", [128, 512], BF16); t_ybf = Tok()
    ssq = sb("ssq", [128, 8]); t_ssq = Tok()
    junkA = sb("junkA", [128, 512]); junkB = sb("junkB", [128, 512]); t_junkA = Tok(); t_junkB = Tok()
    junk = junkA; t_junk = t_junkA
    xcT = sb("xcT", [128, 8, 512], BF16); t_xcT = Tok()
    xstok = sb("xstok", [128, 512], BF16); btok = sb("btok", [128, 128], BF16); t_xstok = Tok()
    dtt_all = sb("dtt_all", [128, 4, 8]); gss_all = sb("gss_all", [128, 4, 8]); t_dt4 = [Tok() for _ in range(4)]
    eGt_all = sb("eGt_all", [128, 4, 8]); wsd_all = sb("wsd_all", [128, 4, 8]); eGe_all = sb("eGe_all", [128, 4, 8])
    gm = sb("gm", [128, 16, 8]); t_gm = Tok()
    t_eGt4 = [Tok() for _ in range(4)]
    eGe = sb("eGe", [128, 16, 8]); t_eGe = Tok()
    m2 = sb("m2", [128, 8, 128]); t_m2 = Tok()
    _m2f = m2[0:64].rearrange("p h t -> p (h t)")
    eG_b = _m2f[:, 0:512].rearrange("p (h t) -> p h t", h=4); eGn_b = _m2f[:, 512:1024].rearrange("p (h t) -> p h t", h=4)
    xw = sb("xw", [128, 512], BF16); t_xw = Tok()
    hist = sb("hist", [128, L, 8, 3]); t_hist = [Tok() for _ in range(L)]
    sc3 = sb("sc3", [16, 768]); t_sc3 = Tok()
    Sg = sb("Sg", [64, L, 4, 128]); Sgb = sb("Sgb", [64, L, 4, 128], BF16)
    Sr = sb("Sr", [64, L, 4, 128]); Srb = sb("Srb", [64, L, 4, 128], BF16)
    Ss = sb("Ss", [64, L, 8, 64]); Ssb = sb("Ssb", [64, L, 8, 64], BF16)
    t_Sg = [Tok() for _ in range(L)]; t_Sr = [Tok() for _ in range(L)]; t_Ss = [Tok() for _ in range(L)]
    S0 = lnrt; t_S0 = t_lnbc
    S0b = mT[:].rearrange("p a b -> p (a b)"); t_S0b = t_mT
    ktm = S0b[0:64, 0:2048].rearrange("p (j n) -> p j n", j=16); t_ktm = t_S0b
    btm = S0b[0:64, 0:1024].rearrange("p (j n) -> p j n", j=16); t_btm = t_S0b

    def mm(out, lhsT, rhs, start, stop, r, w):
        P.op("pe", lambda e: e.matmul(out, lhsT, rhs, start=start, stop=stop), r=r, w=w)

    def tr(out, in_, r, w):
        n = in_.shape[0]
        P.op("pe", lambda e: e.transpose(out, in_, identb[:n, :n]), r=list(r) + [tc_], w=w)

    def act(out, in_, func, r, w, bias=None, scale=1.0, accum=None):
        kw = {}
        if bias is not None:
            kw["bias"] = bias
        if accum is not None:
            kw["accum_out"] = accum
        P.op("act", lambda e: e.activation(out=out, in_=in_, func=func, scale=scale, **kw), r=r, w=w)

    def tt(eng, out, in0, in1, op, r, w):
        P.op(eng, lambda e: e.tensor_tensor(out=out, in0=in0, in1=in1, op=op), r=r, w=w)

    def ts(eng, out, in0, s1, s2, op0, op1, r, w):
        if op1 is None:
            P.op(eng, lambda e: e.tensor_scalar(out=out, in0=in0, scalar1=s1, scalar2=None, op0=op0), r=r, w=w)
        else:
            P.op(eng, lambda e: e.tensor_scalar(out=out, in0=in0, scalar1=s1, scalar2=s2, op0=op0, op1=op1), r=r, w=w)

    def stt(out, in0, sc, in1, op0, op1, r, w):
        P.op("dve", lambda e: e.scalar_tensor_tensor(out=out, in0=in0, scalar=sc, in1=in1, op0=op0, op1=op1), r=r, w=w)

    def cp(eng, out, in_, r, w):
        if eng == "act":
            P.op("act", lambda e: e.copy(out=out, in_=in_), r=r, w=w)
        else:
            P.op(eng, lambda e: e.tensor_copy(out=out, in_=in_), r=r, w=w)

    def bc(ap, shape):
        return ap.to_broadcast(list(shape))

    for nm, t in (("ca_b", ca_b), ("sl_b", sl_b), ("ca_s", ca_s), ("sl_s", sl_s), ("smtok", smtok),
                  ("regend", regend)):
        P.dma("sp", t[:], cst[nm], w=[tc_], key="const")
    P.dma("pool", identb[:], cst["ident"], w=[tc_], key="constp")
    P.dma("pool", smT[:], cst["smT"], w=[tc_], key="constp")
    P.dma("sp", lncol_t[:], lncol, w=[tc_], key="const")
    P.dma("sp", small_t[:], small, w=[tc_], key="const")
    P.dma("pool", wa2_t[:], wa2, w=[tc_], key="constp")
    P.dma("sp", b1col_t[:], b1col, w=[tc_], key="const")
    P.op("dve", lambda e: e.memset(ones_f[:], 1.0), w=[tc_])
    P.op("dve", lambda e: e.memset(big[:], 0.0), w=t_hid + [t_xraw, t_xsin, t_qTm, t_cTm] + t_cacc2)
    P.op("dve", lambda e: e.memset(ones_b[:], 1.0), w=[tc_])
    P.op("dve", lambda e: e.memset(hist[:], 0.0), w=t_hist)
    for l in range(L):
        P.op("dve", lambda e, l=l: e.memset(Sg[:, l], 0.0), w=[t_Sg[l]])
        P.op("dve", lambda e, l=l: e.memset(Sgb[:, l], 0.0), w=[t_Sg[l]])
        P.op("dve", lambda e, l=l: e.memset(Sr[:, l], 0.0), w=[t_Sr[l]])
        P.op("dve", lambda e, l=l: e.memset(Srb[:, l], 0.0), w=[t_Sr[l]])
        P.op("dve", lambda e, l=l: e.memset(Ss[:, l], 0.0), w=[t_Ss[l]])
        P.op("dve", lambda e, l=l: e.memset(Ssb[:, l], 0.0), w=[t_Ss[l]])
    NR = L * 1280
    rhl = rowhl[:].rearrange("p l n -> p (l n)")
    P.dma("sp", S0[0:1, 0:NR], rowp, w=[t_S0], key="S0")
    P.dma("sp", S0[32:33, 0:NR], rowp, w=[t_S0], key="S0")
    P.op("dve", lambda e: e.memset(rowhl[:], 0.0), w=[tc_])
    cp("dve", rhl[0:1, :], S0[0:1, 0:NR], [t_S0], [tc_])
    cp("dve", rhl[32:33, :], S0[32:33, 0:NR], [t_S0], [tc_])
    tt("dve", S0[32:33, 0:NR], S0[32:33, 0:NR], rhl[32:33, :], ALU.subtract, [t_S0, tc_], [t_S0])
    cp("dve", rhl[32:33, :], S0[32:33, 0:NR], [t_S0], [tc_])
    for l in range(L):
        act(aneg[:, l, :], small_t[:, l, 0:8], AF.Exp, [tc_], [tc_])
        ts("dve", aneg[:, l, :], aneg[:, l, :], -1.0, None, ALU.mult, None, [tc_], [tc_])
    identf = sb("identf", [128, 128])
    P.dma("sp", identf[:], cst["ident"], w=[tc_], key="const")

    wseq = [(l_, nm) for _gi in range(NG) for l_ in range(L) for (nm, _k, _n) in WT]
    wstate = {"issued": 0, "cur": 0}

    wbf = nc.dram_tensor("wbf", [L, 128, WTOT], BF16, kind="Internal").ap()
    t_wbf = {}
    NT_PER = len(WT)

    def _issue(idx):
        l_, nm = wseq[idx]
        off, kc, ncl = WOFF[nm]
        t, tk = wslot[idx % NSLOT]
        n = kc * ncl
        gi_ = idx // (L * NT_PER)
        if gi_ == 0:
            P.dma("pool", t[:, 0:n], wst[l_, :, off:off + n], w=[tk], key=("w", idx % NSLOT), max_dma_last_dim=4096)
            tkd = Tok()
            t_wbf[(l_, nm)] = tkd
            P.dma("sp", wbf[l_, :, off:off + n], t[:, 0:n], r=[tk], w=[tkd], key=("wbw", idx % NSLOT))
        else:
            P.dma("sp", t[:, 0:n], wbf[l_, :, off:off + n], r=[t_wbf[(l_, nm)]], w=[tk], key=("wr", idx % NSLOT))

    def wload(l, nm, live=0):
        idx = wstate["cur"]
        assert wseq[idx] == (l, nm), (wseq[idx], l, nm)
        while wstate["issued"] < min(len(wseq), idx - live + NSLOT):
            _issue(wstate["issued"])
            wstate["issued"] += 1
        assert wstate["issued"] > idx
        wstate["cur"] += 1
        off, kc, ncl = WOFF[nm]
        t, tk = wslot[idx % NSLOT]
        return t[:, 0:kc * ncl].rearrange("p (k n) -> p k n", k=kc), tk

    xhb4 = big[:, 0:4096].rearrange("p (b d) -> p b d", b=4)
    t_xhb4 = [t_hid[0], t_hid[1], t_xraw]
    lnst4 = sb("lnst4", [128, 4, 2, 6]); lnmv4 = sb("lnmv4", [128, 4, 4]); t_ln4 = [Tok() for _ in range(4)]

    def layernorm(g, lni):
        P.dma("sp", lnw_bc, lnbc[lni, 0], w=[t_lnbc], key="lnbc")
        P.dma("sp", lnb_bc, lnbc[lni, 1], w=[t_lnbc], key="lnbc")
        blks = list(enumerate(g))
        for bi, (kind, nt, c0, pos0) in blks:
            z = xtok[:nt, bi, :]
            for hh in range(2):
                P.op("dve", lambda e, hh=hh, z=z, nt=nt, bi=bi: e.bn_stats(out=lnst4[:nt, bi, hh, :], in_=z[:, hh * 512:(hh + 1) * 512]),
                     r=[t_xtok[bi]], w=[t_ln4[bi]])
            P.op("dve", lambda e, nt=nt, bi=bi: e.bn_aggr(out=lnmv4[:nt, bi, 0:2], in_=lnst4[:nt, bi].rearrange("p a b -> p (a b)")),
                 r=[t_ln4[bi]], w=[t_ln4[bi]])
        for bi, (kind, nt, c0, pos0) in blks:
            act(lnmv4[:nt, bi, 2:3], lnmv4[:nt, bi, 1:2], AF.Ln, [t_ln4[bi]], [t_ln4[bi]], bias=1e-5)
            act(lnmv4[:nt, bi, 3:4], lnmv4[:nt, bi, 2:3], AF.Exp, [t_ln4[bi]], [t_ln4[bi]], scale=-0.5)
        for bi, (kind, nt, c0, pos0) in blks:
            ts("dve", lnmv4[:nt, bi, 2:3], lnmv4[:nt, bi, 0:1], lnmv4[:nt, bi, 3:4], -1.0, ALU.mult, ALU.mult,
               [t_ln4[bi]], [t_ln4[bi]])
        for bi, (kind, nt, c0, pos0) in blks:
            z = xtok[:nt, bi, :]
            act(z, z, AF.Identity, [t_xtok[bi], t_ln4[bi]], [t_xtok[bi]],
                bias=lnmv4[:nt, bi, 2:3], scale=lnmv4[:nt, bi, 3:4])
        for bi, (kind, nt, c0, pos0) in blks:
            z = xtok[:nt, bi, :]
            tt("dve", z, z, lnw_bc[:nt, :], ALU.mult, [t_xtok[bi], t_lnbc], [t_xtok[bi]])
            tt("dve", z, z, lnb_bc[:nt, :], ALU.add, [t_xtok[bi], t_lnbc], [t_xtok[bi]])
        for bi, (kind, nt, c0, pos0) in blks:
            cp("act", xhb4[:nt, bi, :], xtok[:nt, bi, :], [t_xtok[bi]], t_xhb4)
        for bi, (kind, nt, c0, pos0) in blks:
            pt, tp = PB()
            for k in range(8):
                tr(pt[:, k * 128:k * 128 + nt], xhb4[:nt, bi, k * 128:(k + 1) * 128], t_xhb4, [tp])
            cp("act", xT[:, :, c0:c0 + nt], pt[:].rearrange("p (k t) -> p k t", k=8)[:, :, 0:nt], [tp], [t_xT])

    def proj_feat(W, tw, wc0, M, src, tsrc, nk, ncols, koff=0):
        pt, tp = PF()
        for k in range(nk):
            mm(pt[:M, :ncols], W[:, k, wc0:wc0 + M], src[:, koff + k, 0:ncols], k == 0, k == nk - 1, [tw, tsrc], [tp])
        return pt, tp

    def proj_tok(W, tw, wc0, N, c0, nt):
        pt, tp = PF()
        for k in range(8):
            mm(pt[:nt, :N], xT[:, k, c0:c0 + nt], W[:, k, wc0:wc0 + N], k == 0, k == 7, [tw, t_xT], [tp])
        return pt, tp

    KIND = {"B": 0, "M": 1, "S": 2}

    def lin_pre(kind, nt, c0, i2):
        ca = ca_s if kind == "S" else ca_b
        pt, tp = PB()
        for h in range(4):
            tr(pt[:nt, h * 64:(h + 1) * 64], kTb[:, h, c0:c0 + nt], [t_qkb], [tp])
        pa, tpa = PF()
        for h in range(4):
            mm(pa[:nt, h * 128:h * 128 + nt], kTb[:, h, c0:c0 + nt], qTb[:, h, c0:c0 + nt], True, True, [t_qkb], [tpa])
        yield
        cp("act", ktok2[:nt, i2, :], pt[:nt, 0:256], [tp], [t_ktok2[i2]])
        pav = pa[:].rearrange("p (h t) -> p h t", h=4)[:nt, :, :nt]
        tt("dve", attb[:nt, 4 * i2:4 * i2 + 4, :nt], pav, bc(ca[:nt, :nt].unsqueeze(1), [nt, 4, nt]), ALU.mult,
           [tpa, tc_], [t_att2[i2]])

    def lin_block(l, kind, nt, c0, Sm, Smb, tS, egp, egs, ybase, ycol_scale, st_in, st_out_prompt, st_out_sample,
                  last_prompt_block, vtok, t_vtok, srt, t_srt, i2, hook=None):
        ktok = ktok2[:, i2, :]
        t_ktok = t_ktok2[i2]
        t_att = t_att2[i2]
        attv = attb[:, 4 * i2:4 * i2 + 4, :]
        chk("lb_att")
        po, tpo = PL(0)
        S0v = S0[0:64, :].rearrange("p (j a v) -> p j a v", j=16, a=2)
        S0bv = S0b[0:64, :].rearrange("p (j a v) -> p j a v", j=16, a=2)
        if kind != "S":
            for h in range(4):
                mm(po[:nt, h * 128:(h + 1) * 128], attv[:nt, h, :nt], vtok[:nt, h * 128:(h + 1) * 128], True, False,
                   [t_att, t_vtok], [tpo])
                mm(po[:nt, h * 128:(h + 1) * 128], qTb[:, h, c0:c0 + nt], Smb[:, l, h, :], False, True, [t_qkb, tS], [tpo])
        else:
            for p in range(2):
                P.dma("sp", S0[0:64, :], st_in[l, p].rearrange("p j a v -> p (j a v)"), w=[t_S0], key="S0")
                P.dma("pool", S0b[0:64, :], st_in[l, p].rearrange("p j a v -> p (j a v)"), w=[t_S0b], key="S0b",
                      max_dma_last_dim=4096)
                tt("dve", qTm[:], bc(qTb[:, 2 * p:2 * p + 2, c0:c0 + 64].unsqueeze(2), [64, 2, 16, 64]),
                   bc(smT[0:64].unsqueeze(1), [64, 2, 16, 64]), ALU.mult, [t_qkb, tc_], [t_qTm])
                for h2 in range(2):
                    h = 2 * p + h2
                    mm(po[:nt, h * 128:(h + 1) * 128], attv[:nt, h, :nt], vtok[:nt, h * 128:(h + 1) * 128], True, False,
                       [t_att, t_vtok], [tpo])
                    for j in range(16):
                        mm(po[:nt, h * 128:(h + 1) * 128], qTm[:, h2, j, :], S0bv[:, j, h2, :], False, j == 15,
                           [t_qTm, t_S0b], [tpo])
                tt("dve", ktm[:], bc(ktok[:64, p * 128:(p + 1) * 128].unsqueeze(1), [64, 16, 128]),
                   bc(smtok[:64, :].unsqueeze(2), [64, 16, 128]), ALU.mult, [t_ktok, tc_], [t_ktm])
                for rnd in range(2):
                    banks = []
                    for q in range(4):
                        ps_, tps = PF()
                        banks.append((ps_, tps))
                        for jj in range(2):
                            j = rnd * 8 + q * 2 + jj
                            for h2 in range(2):
                                h = 2 * p + h2
                                o_ = (jj * 2 + h2) * 128
                                mm(ps_[:64, o_:o_ + 128], ktm[:, j, h2 * 64:(h2 + 1) * 64], vtok[:64, h * 128:(h + 1) * 128],
                                   True, True, [t_ktm, t_vtok], [tps])
                    for q in range(4):
                        ps_, tps = banks[q]
                        j0 = rnd * 8 + q * 2
                        tt("dve", S0v[:, j0:j0 + 2], S0v[:, j0:j0 + 2],
                           ps_[:64, :].rearrange("p (j a v) -> p j a v", j=2, a=2), ALU.add, [t_S0, tps], [t_S0])
                        tt("dve", S0v[:, j0:j0 + 2], S0v[:, j0:j0 + 2],
                           bc(egs[:, j0:j0 + 2, 2 * p:2 * p + 2].unsqueeze(3), [64, 2, 2, 128]), ALU.mult,
                           [t_S0, t_eG, tc_], [t_S0])
                P.dma("sp", st_out_sample[l, p].rearrange("p j a v -> p (j a v)"), S0[0:64, :], r=[t_S0], key="S0out",
                      final=True)
        chk("lb_o")
        if kind != "S":
            ps_, tps = PF()
            for h in range(4):
                mm(ps_[:64, h * 128:(h + 1) * 128], ktok[:nt, h * 64:(h + 1) * 64], vtok[:nt, h * 128:(h + 1) * 128],
                   True, True, [t_ktok, t_vtok], [tps])
            tt("dve", Sm[:, l], Sm[:, l], ps_[:64, :].rearrange("p (h v) -> p h v", h=4), ALU.add, [tS, tps], [tS])
            tt("dve", Sm[:, l], Sm[:, l], bc(egp.unsqueeze(2), [64, 4, 128]), ALU.mult, [tS, t_eG, t_egp, tc_], [tS])
            cp("dve", Smb[:, l], Sm[:, l], [tS], [tS])
            if last_prompt_block:
                P.dma("sp", st_out_prompt[l], Sm[:, l], r=[tS], key="pfinal", final=True)
        hk = hook() if hook is not None else None
        if hk is not None:
            next(hk)
        act(junk[:nt, :], po[:nt, :], AF.Square, [tpo], [t_junk])
        P.op("dve", lambda e: e.reduce_sum(out=ssq[:nt, 0:4], in_=junk[:nt, :].rearrange("p (h v) -> p h v", h=4), axis=AX.X),
             r=[t_junk], w=[t_ssq])
        act(ssq[:nt, 0:4], ssq[:nt, 0:4], AF.Ln, [t_ssq], [t_ssq], bias=1e-6, scale=1.0 / 128.0)
        act(ssq[:nt, 4:8], ssq[:nt, 0:4], AF.Exp, [t_ssq], [t_ssq], scale=-0.5)
        tt("dve", ysb[:nt, :].rearrange("p (h v) -> p h v", h=4), po[:nt, :].rearrange("p (h v) -> p h v", h=4),
           bc(ssq[:nt, 4:8].unsqueeze(2), [nt, 4, 128]), ALU.mult, [tpo, t_ssq], [t_ysb])
        tt("dve", ybf[:nt, :], ysb[:nt, :], srt[:nt, :], ALU.mult, [t_ysb, t_srt], [t_ybf])
        if hk is not None:
            for _ in hk:
                pass
        pt, tp = PB()
        for c in range(4):
            tr(pt[:, c * 128:c * 128 + nt], ybf[:nt, c * 128:(c + 1) * 128], [t_ybf], [tp])
        pv = pt[:].rearrange("p (c t) -> p c t", c=8)[:, 0:4, 0:nt]
        if ycol_scale is not None:
            ts("dve", yT[:, ybase:ybase + 4, c0:c0 + nt], pv, ycol_scale, None, ALU.mult, None, [tp, tc_],
               [t_yT[ybase // 4]])
        else:
            cp("act", yT[:, ybase:ybase + 4, c0:c0 + nt], pv, [tp], [t_yT[ybase // 4]])
        chk("lb_yT")

    class _Stop(Exception):
        pass

    def chk(name):
        if STOP is not None and name == STOP:
            raise _Stop()

    def ph(name):
        P.phase = name

    def main():
        for gi, g in enumerate(GROUPS):
            ncols = gcols(g)
            nblk = len(g)
            last_group = gi == NG - 1
            for bi, (kind, nt, c0, pos0) in enumerate(g):
                if kind == "M":
                    src = meta
                elif kind == "S":
                    src = xs
                else:
                    src = xp[pos0 - 16:pos0 - 16 + nt, :]
                P.dma("sp", xtok[:nt, bi, :], src, w=[t_xtok[bi]], key=("xin", bi))
            ph("ln0")
            layernorm(g, 0)
            chk("ln0")
            for l in range(L):
                ph("gla_proj")
                W, tw = wload(l, "gqk")
                for h in range(4):
                    pt, tp = proj_feat(W, tw, h * 64, 64, xT, t_xT, 8, ncols)
                    cp("act", qTf[:, h, :ncols], pt[:64, :ncols], [tp], [t_qk, t_lhd])
                    pt, tp = proj_feat(W, tw, 256 + h * 64, 64, xT, t_xT, 8, ncols)
                    cp("act", kTf[:, h, :ncols], pt[:64, :ncols], [tp], [t_qk, t_dec])
                chk("gla_qk")
                Wa, twa = wload(l, "ga", live=1)
                pt, tp = proj_feat(Wa, twa, 0, 16, xT, t_xT, 8, ncols)
                cp("act", aTb[:, :ncols], pt[:16, :ncols], [tp], [t_aT])
                Wv, twv = wload(l, "gv")
                Wr_, twr = wload(l, "gr", live=1)
                ph("gla_blk")
                def _sets(bi):
                    if bi % 2 == 0:
                        return junkA[:, 0:256], t_junkA, eG_a, eGn_a, t_eG_a
                    return junkB[:, 0:256], t_junkB, eG_b, eGn_b, t_m2
                for b0 in range(0, nblk, 2):
                    pair = [(bi,) + tuple(g[bi]) for bi in range(b0, min(b0 + 2, nblk))]
                    pas = {}
                    for (bi, kind, nt, c0, pos0) in pair:
                        pa_, tpa_ = PF()
                        pas[bi] = (pa_, tpa_)
                        mm(pa_[:nt, 0:256], aTb[0:16, c0:c0 + nt], wa2_t[0:16, l, :], True, False, [t_aT, tc_], [tpa_])
                        mm(pa_[:nt, 0:256], ones_b[0:33, :nt], rowhl[0:33, l, 0:256], False, True, [tc_], [tpa_])
                    for (bi, kind, nt, c0, pos0) in pair:
                        gpr, t_gpr, eG, eGn, t_eGx = _sets(bi)
                        pa_, tpa_ = pas[bi]
                        act(gpr[:nt, :], pa_[:nt, 0:256], AF.Exp, [tpa_], [t_gpr], scale=-1.0)
                        act(gpr[:nt, :], gpr[:nt, :], AF.Ln, [t_gpr], [t_gpr], bias=1.0)
                    pgs = {}
                    for (bi, kind, nt, c0, pos0) in pair:
                        gpr, t_gpr, eG, eGn, t_eGx = _sets(bi)
                        U = ca_s if kind == "S" else ca_b
                        pg, tpg = PF()
                        pgs[bi] = (pg, tpg)
                        for h in range(4):
                            mm(pg[:64, h * 128:h * 128 + nt], gpr[:nt, h * 64:(h + 1) * 64], U[:nt, :nt], True, True,
                               [t_gpr, tc_], [tpg])
                    for (bi, kind, nt, c0, pos0) in pair:
                        gpr, t_gpr, eG, eGn, t_eGx = _sets(bi)
                        pg, tpg = pgs[bi]
                        pgv = pg[:].rearrange("p (a t) -> p a t", a=4)[:64, :, 0:nt]
                        act(eG[:, :, :nt], pgv, AF.Exp, [tpg], [t_eGx], scale=-1.0 / 16.0)
                        act(eGn[:, :, :nt], pgv, AF.Exp, [tpg], [t_eGx], scale=1.0 / 16.0)
                    for (bi, kind, nt, c0, pos0) in pair:
                        gpr, t_gpr, eG, eGn, t_eGx = _sets(bi)
                        stt(qTb[:, :, c0:c0 + nt], qTf[:, :, c0:c0 + nt], 0.125, eG[:, :, :nt], ALU.mult, ALU.mult,
                            [t_qk, t_eGx], [t_qkb])
                        tt("dve", kTb[:, :, c0:c0 + nt], kTf[:, :, c0:c0 + nt], eGn[:, :, :nt], ALU.mult, [t_qk, t_eGx], [t_qkb])
                        if kind == "S":
                            cp("dve", egs_all[:], eG[:, :, 3:64:4].rearrange("p h j -> p j h"), [t_eGx], [t_egp])
                        else:
                            cp("dve", egp_all[:, bi, :], eG[:, :, nt - 1], [t_eGx], [t_egp])

                def gla_pre(bi):
                    kind, nt, c0, pos0 = g[bi]
                    i2 = bi % 2
                    pt, tp = proj_tok(Wr_, twr, 0, 512, c0, nt)
                    pv_, tpv_ = proj_tok(Wv, twv, 0, 512, c0, nt)
                    lp = lin_pre(kind, nt, c0, i2)
                    next(lp)
                    yield
                    act(junkB[:nt, :], pt[:nt, :], AF.Exp, [tp], [t_junkB], scale=-1.0)
                    act(junkB[:nt, :], junkB[:nt, :], AF.Ln, [t_junkB], [t_junkB], bias=1.0)
                    act(junkB[:nt, :], junkB[:nt, :], AF.Exp, [t_junkB], [t_junkB], scale=-1.0)
                    cp("act", vtok2[:nt, i2, :], pv_[:nt, :], [tpv_], [t_vtok2[i2]])
                    tt("dve", srt2[:nt, i2, :], pt[:nt, :], junkB[:nt, :], ALU.mult, [tp, t_junkB], [t_srt2[i2]])
                    for _ in lp:
                        pass
                for _ in gla_pre(0):
                    pass
                for bi, (kind, nt, c0, pos0) in enumerate(g):
                    i2 = bi % 2
                    lastp = last_group and bi == nblk - 1
                    hook = (lambda bi=bi: gla_pre(bi + 1)) if bi + 1 < nblk else None
                    lin_block(l, kind, nt, c0, Sg, Sgb, t_Sg[l], egp_all[:, bi, :], egs_all[:], 0, small_t[:, l, 68:69],
                              sg_in, sgp, sgs, lastp, vtok2[:, i2, :], t_vtok2[i2], srt2[:, i2, :], t_srt2[i2], i2, hook)
                    chk("gla_b%d" % bi)
                chk("gla")
                ph("ssd_proj")
                tt("dve", dmat[:], bc(identf[:].unsqueeze(1), [128, 8, 128]),
                   bc(small_t[:, l, 16:24].unsqueeze(2), [128, 8, 128]), ALU.mult, [tc_], [t_dmat])
                Wx0, twx0 = wload(l, "sx0")
                Wx1, twx1 = wload(l, "sx1", live=1)
                has_s = any(b[0] == "S" for b in g)
                npr = ncols - (64 if has_s else 0)
                CP_ = [128, 128, 128, 128, 64, 64, 64, 64]
                cp("dve", xraw[:, :, 0:3], hist[:, l], [t_hist[l]], t_xraw8 + [t_xraw])
                if has_s:
                    P.dma("sp", junkB[:, 0:384], sc_in[l].rearrange("p c j i -> p (c j i)"), w=[t_junkB], key="xsin")
                    cp("dve", xsin[:, :, :, 0:3], junkB[:, 0:384].rearrange("p (c j i) -> p c j i", c=8, j=16),
                       [t_junkB], [t_xsin])

                def silu_chunk(c):
                    pc = CP_[c]
                    act(xcT[:pc, c, :ncols], cacc[:pc, c % 2, :ncols], AF.Silu, [t_cacc2[c % 2]], [t_xcT])
                for c in range(8):
                    pc = CP_[c]
                    txr = t_xraw8[c]
                    if c < 4:
                        pt, tp = proj_feat(Wx0, twx0, c * 128, 128, xT, t_xT, 8, ncols)
                    else:
                        pt, tp = proj_feat(Wx1, twx1, (c - 4) * 64, 64, xT, t_xT, 8, ncols)
                    cp("act", xraw[:pc, c, 3:3 + ncols], pt[:pc, :ncols], [tp], [txr])
                    cw = lambda i, c=c: small_t[:CP_[c], l, 32 + c * 4 + i:33 + c * 4 + i]
                    ca_ = cacc[:pc, c % 2, :]
                    tca = t_cacc2[c % 2]
                    act(ca_[:, 0:npr], xraw[:pc, c, 0:npr], AF.Identity, [txr, tc_], [tca],
                        bias=small_t[:pc, l, 24 + c:25 + c], scale=cw(0))
                    for i in range(1, 4):
                        stt(ca_[:, 0:npr], xraw[:pc, c, i:i + npr], cw(i), ca_[:, 0:npr], ALU.mult, ALU.add,
                            [txr, tc_, tca], [tca])
                    if has_s:
                        cp("dve", xsin[:pc, c, :, 3:7], xraw[:pc, c, 3 + npr:3 + ncols].rearrange("p (j t) -> p j t", j=16),
                           [txr], [t_xsin])
                        cv = ca_[:, npr:ncols].rearrange("p (j t) -> p j t", j=16)
                        act(cv, xsin[:pc, c, :, 0:4], AF.Identity, [t_xsin, tc_], [tca],
                            bias=small_t[:pc, l, 24 + c:25 + c], scale=cw(0))
                        for i in range(1, 4):
                            stt(cv, xsin[:pc, c, :, i:i + 4], cw(i), cv, ALU.mult, ALU.add, [t_xsin, tc_, tca], [tca])
                    if c > 0:
                        silu_chunk(c - 1)
                silu_chunk(7)
                cp("dve", hist[:, l], xraw[:, :, npr:npr + 3], t_xraw8, [t_hist[l]])
                cs_jobs = []
                if last_group:
                    cs_jobs.append((3, lambda k: xT[:, k, npr - 3:npr], scp[l]))
                if has_s:
                    for i in range(3):
                        cs_jobs.append((16, lambda k, i=i: xT[:, k, npr + i + 1:npr + 64:4], scs[l, i]))
                for (mrows, lvf, dst) in cs_jobs:
                    pc_, tpc = PF()
                    pc2, tpc2 = PF()
                    for k in range(8):
                        mm(pc_[:mrows, 0:512], lvf(k), Wx0[:, k, :], k == 0, k == 7, [t_xT, twx0], [tpc])
                    for k in range(8):
                        mm(pc2[:mrows, 0:256], lvf(k), Wx1[:, k, :], k == 0, k == 7, [t_xT, twx1], [tpc2])
                    cp("act", sc3[:mrows, 0:512], pc_[:mrows, 0:512], [tpc], [t_sc3])
                    cp("act", sc3[:mrows, 512:768], pc2[:mrows, 0:256], [tpc2], [t_sc3])
                    P.dma("sp", dst, sc3[:mrows, :], r=[t_sc3], key="sc3out", final=True)
                Wz, twz = wload(l, "sz")
                Wdt, twdt = wload(l, "sdt", live=1)
                ph("ssd_blk")
                blks = list(enumerate(g))
                V = lambda bi: (dtt_all[:, bi, :], gss_all[:, bi, :], eGt_all[:, bi, :], wsd_all[:, bi, :])
                pds = {}
                for bi, (kind, nt, c0, pos0) in blks:
                    pds[bi] = proj_tok(Wdt, twdt, 0, 8, c0, nt)
                for bi, (kind, nt, c0, pos0) in blks:
                    dtt, gss, eGt, wsd = V(bi)
                    pd, tpd = pds[bi]
                    tt("dve", dtt[:nt, :], pd[:nt, 0:8], small_t[:nt, l, 8:16], ALU.add, [tpd, tc_], [t_dt4[bi]])
                for bi, (kind, nt, c0, pos0) in blks:
                    dtt, gss, eGt, wsd = V(bi)
                    act(dtt[:nt, :], dtt[:nt, :], AF.Exp, [t_dt4[bi]], [t_dt4[bi]])
                    act(dtt[:nt, :], dtt[:nt, :], AF.Ln, [t_dt4[bi]], [t_dt4[bi]], bias=1.0)
                for bi, (kind, nt, c0, pos0) in blks:
                    dtt, gss, eGt, wsd = V(bi)
                    tt("dve", gss[:nt, :], dtt[:nt, :], aneg[:nt, l, :], ALU.mult, [t_dt4[bi], tc_], [t_dt4[bi]])
                    if kind == "S":
                        tt("dve", gm[:nt, :, :], bc(gss[:nt, :].unsqueeze(1), [nt, 16, 8]),
                           bc(smtok[:nt, :].unsqueeze(2), [nt, 16, 8]), ALU.mult, [t_dt4[bi], tc_], [t_gm])
                pqs = {}
                for bi, (kind, nt, c0, pos0) in blks:
                    dtt, gss, eGt, wsd = V(bi)
                    U = ca_s if kind == "S" else ca_b
                    SLm = sl_s if kind == "S" else sl_b
                    nsq = 16 if kind == "S" else 1
                    gmv = gm[:nt, :, :].rearrange("p j h -> p (j h)") if kind == "S" else gss[:nt, :]
                    pq, tpq = PF()
                    pqs[bi] = (pq, tpq)
                    mm(pq[:nt, 0:8], U[:nt, :nt], gss[:nt, :], True, True, [tc_, t_dt4[bi]], [tpq])
                    mm(pq[:nt, 8:16], SLm[:nt, :nt], gss[:nt, :], True, True, [tc_, t_dt4[bi]], [tpq])
                    mm(pq[:, 128:128 + nsq * 8], ones_f[:nt, :], gmv, True, True, [tc_, t_gm, t_dt4[bi]], [tpq])
                for bi, (kind, nt, c0, pos0) in blks:
                    dtt, gss, eGt, wsd = V(bi)
                    pq, tpq = pqs[bi]
                    act(eGt[:nt, :], pq[:nt, 0:8], AF.Exp, [tpq], [t_eGt4[bi]])
                    act(wsd[:nt, :], pq[:nt, 8:16], AF.Exp, [tpq], [t_eGt4[bi]])
                    if kind == "S":
                        act(eGe[:, 0:16, :], pq[:, 128:256].rearrange("p (j h) -> p j h", h=8), AF.Exp, [tpq], [t_eGe])
                    else:
                        act(eGe_all[:, bi, :], pq[:, 128:136], AF.Exp, [tpq], [t_eGe])
                for bi, (kind, nt, c0, pos0) in blks:
                    dtt, gss, eGt, wsd = V(bi)
                    tt("dve", wsd[:nt, :], wsd[:nt, :], dtt[:nt, :], ALU.mult, [t_eGt4[bi], t_dt4[bi]], [t_eGt4[bi]])
                for bi, (kind, nt, c0, pos0) in enumerate(g):
                    U = ca_s if kind == "S" else ca_b
                    SLm = sl_s if kind == "S" else sl_b
                    nsq = 16 if kind == "S" else 1
                    pt, tp = PB()
                    for c in range(4):
                        tr(pt[:nt, c * 128:(c + 1) * 128], xcT[:, c, c0:c0 + nt], [t_xcT], [tp])
                    for gg in range(2):
                        tr(pt[:nt, 512 + gg * 64:512 + (gg + 1) * 64], xcT[0:64, 4 + gg, c0:c0 + nt], [t_xcT], [tp])
                    cp("act", xstok[:nt, :], pt[:nt, 0:512], [tp], [t_xstok])
                    cp("act", btok[:nt, :], pt[:nt, 512:640], [tp], [t_xstok])
                    dtt = dtt_all[:, bi, :]; gss = gss_all[:, bi, :]; eGt = eGt_all[:, bi, :]; wsd = wsd_all[:, bi, :]
                    tt("dve", lhd[:nt, :, :nt], bc(SLm[:nt, :nt].unsqueeze(1), [nt, 8, nt]),
                       bc(gss[:nt, :].unsqueeze(2), [nt, 8, nt]), ALU.mult, [tc_, t_dt4[bi]], [t_lhd, t_qk])
                    tt("dve", m2[:nt, :, :nt], bc(U[:nt, :nt].unsqueeze(1), [nt, 8, nt]),
                       bc(dtt[:nt, :].unsqueeze(2), [nt, 8, nt]), ALU.mult, [tc_, t_dt4[bi]], [t_m2])
                    for half in range(2):
                        pdf, tpdf = PF()
                        for hh in range(4):
                            h = half * 4 + hh
                            mm(pdf[:nt, hh * 128:hh * 128 + nt], lhd[:nt, h, :nt], U[:nt, :nt], True, True,
                               [t_lhd, tc_], [tpdf])
                        act(dec[:nt, half * 4:half * 4 + 4, :nt],
                            pdf[:].rearrange("p (h t) -> p h t", h=4)[:nt, :, :nt], AF.Exp, [tpdf], [t_dec])
                    pz, tpz = proj_tok(Wz, twz, 0, 512, c0, nt)
                    tt("dve", dec[:nt, :, :nt], dec[:nt, :, :nt], m2[:nt, :, :nt], ALU.mult, [t_dec, t_m2], [t_dec])
                    pbc, tpbc = PF()
                    for gg in range(2):
                        mm(pbc[:nt, gg * 128:gg * 128 + nt], xcT[0:64, 4 + gg, c0:c0 + nt], xcT[0:64, 6 + gg, c0:c0 + nt],
                           True, True, [t_xcT], [tpbc])
                    for gg in range(2):
                        tt("dve", attb[:nt, gg * 4:gg * 4 + 4, :nt],
                           bc(pbc[:nt, gg * 128:gg * 128 + nt].unsqueeze(1), [nt, 4, nt]),
                           dec[:nt, gg * 4:gg * 4 + 4, :nt], ALU.mult, [tpbc, t_dec], [t_att2[gg]])
                    act(junk[:nt, :], pz[:nt, :], AF.Exp, [tpz], [t_junk], scale=-1.0)
                    act(junk[:nt, :], junk[:nt, :], AF.Ln, [t_junk], [t_junk], bias=1.0)
                    act(junk[:nt, :], junk[:nt, :], AF.Exp, [t_junk], [t_junk], scale=-1.0)
                    tt("dve", srt[:nt, :], pz[:nt, :], junk[:nt, :], ALU.mult, [tpz, t_junk], [t_srt])
                    py, tpy = PL(0)
                    for h in range(8):
                        mm(py[:nt, h * 64:(h + 1) * 64], attb[:nt, h, :nt], xstok[:nt, h * 64:(h + 1) * 64], True, False,
                           [t_att2[h // 4], t_xstok], [tpy])
                        mm(py[:nt, h * 64:(h + 1) * 64], dmat[:nt, h, :nt], xstok[:nt, h * 64:(h + 1) * 64], False, True,
                           [t_dmat, t_xstok], [tpy])
                    tt("dve", xw[:nt, :].rearrange("p (h v) -> p h v", h=8), xstok[:nt, :].rearrange("p (h v) -> p h v", h=8),
                       bc(wsd[:nt, :].unsqueeze(2), [nt, 8, 64]), ALU.mult, [t_xstok, t_eGt4[bi]], [t_xw])
                    pi_, tpi = PL(1)
                    S0v = S0[0:64, :].rearrange("p (j a v) -> p j a v", j=16, a=4)
                    S0bv = S0b[0:64, :].rearrange("p (j a v) -> p j a v", j=16, a=4)
                    if kind != "S":
                        for h in range(8):
                            mm(pi_[:nt, h * 64:(h + 1) * 64], xcT[0:64, 6 + h // 4, c0:c0 + nt], Ssb[:, l, h, :], True, True,
                               [t_xcT, t_Ss[l]], [tpi])
                    else:
                        for gg in range(2):
                            P.dma("sp", S0[0:64, :], ss_in[l, gg].rearrange("p j a v -> p (j a v)"), w=[t_S0], key="S0")
                            P.dma("pool", S0b[0:64, :], ss_in[l, gg].rearrange("p j a v -> p (j a v)"), w=[t_S0b], key="S0b",
                                  max_dma_last_dim=4096)
                            tt("dve", cTm[:], bc(xcT[0:64, 6 + gg, c0:c0 + 64].unsqueeze(1), [64, 16, 64]), smT[0:64],
                               ALU.mult, [t_xcT, tc_], [t_cTm])
                            for hh in range(4):
                                h = gg * 4 + hh
                                for j in range(16):
                                    mm(pi_[:nt, h * 64:(h + 1) * 64], cTm[:, j, :], S0bv[:, j, hh, :], j == 0, j == 15,
                                       [t_cTm, t_S0b], [tpi])
                            tt("dve", btm[:], bc(btok[:64, gg * 64:(gg + 1) * 64].unsqueeze(1), [64, 16, 64]),
                               bc(smtok[:64, :].unsqueeze(2), [64, 16, 64]), ALU.mult, [t_xstok, tc_], [t_btm])
                            for rnd in range(2):
                                banks = []
                                for q in range(4):
                                    ps_, tps = PF()
                                    banks.append((ps_, tps))
                                    for jj in range(2):
                                        j = rnd * 8 + q * 2 + jj
                                        mm(ps_[:64, jj * 256:(jj + 1) * 256], btm[:, j, :], xw[:64, gg * 256:(gg + 1) * 256],
                                           True, True, [t_btm, t_xw], [tps])
                                for q in range(4):
                                    ps_, tps = banks[q]
                                    j0 = rnd * 8 + q * 2
                                    tt("dve", S0v[:, j0:j0 + 2], S0v[:, j0:j0 + 2],
                                       bc(eGe[0:64, j0:j0 + 2, gg * 4:gg * 4 + 4].unsqueeze(3), [64, 2, 4, 64]), ALU.mult,
                                       [t_S0, t_eGe], [t_S0])
                                    tt("dve", S0v[:, j0:j0 + 2], S0v[:, j0:j0 + 2],
                                       ps_[:64, :].rearrange("p (j a v) -> p j a v", j=2, a=4), ALU.add, [t_S0, tps], [t_S0])
                            P.dma("sp", sss[l, gg].rearrange("p j a v -> p (j a v)"), S0[0:64, :], r=[t_S0], key="S0out",
                                  final=True)
                    tt("dve", ysb[:nt, :].rearrange("p (h v) -> p h v", h=8), pi_[:nt, :].rearrange("p (h v) -> p h v", h=8),
                       bc(eGt[:nt, :].unsqueeze(2), [nt, 8, 64]), ALU.mult, [tpi, t_eGt4[bi]], [t_ysb])
                    tt("dve", ysb[:nt, :], ysb[:nt, :], py[:nt, :], ALU.add, [t_ysb, tpy], [t_ysb])
                    tt("dve", ysb[:nt, :], ysb[:nt, :], srt[:nt, :], ALU.mult, [t_ysb, t_srt], [t_ysb])
                    act(junk[:nt, :], ysb[:nt, :], AF.Square, [t_ysb], [t_junk])
                    P.op("dve", lambda e, nt=nt: e.reduce_sum(out=ssq[:nt, 0:2], in_=junk[:nt, :].rearrange("p (g v) -> p g v", g=2), axis=AX.X),
                         r=[t_junk], w=[t_ssq])
                    act(ssq[:nt, 0:2], ssq[:nt, 0:2], AF.Ln, [t_ssq], [t_ssq], bias=1e-6, scale=1.0 / 256.0)
                    act(ssq[:nt, 4:6], ssq[:nt, 0:2], AF.Exp, [t_ssq], [t_ssq], scale=-0.5)
                    tt("dve", ybf[:nt, :].rearrange("p (g v) -> p g v", g=2), ysb[:nt, :].rearrange("p (g v) -> p g v", g=2),
                       bc(ssq[:nt, 4:6].unsqueeze(2), [nt, 2, 256]), ALU.mult, [t_ysb, t_ssq], [t_ybf])
                    pt, tp = PB()
                    for c in range(4):
                        tr(pt[:, c * 128:c * 128 + nt], ybf[:nt, c * 128:(c + 1) * 128], [t_ybf], [tp])
                    pv = pt[:].rearrange("p (c t) -> p c t", c=8)[:, 0:4, 0:nt]
                    tt("dve", yT[:, 4:8, c0:c0 + nt], pv, bc(small_t[:, l, 64:68].unsqueeze(2), [128, 4, nt]), ALU.mult,
                       [tp, tc_], [t_yT[1]])
                    if kind != "S":
                        ps_, tps = PF()
                        for gg in range(2):
                            mm(ps_[:64, gg * 256:(gg + 1) * 256], btok[:nt, gg * 64:(gg + 1) * 64], xw[:nt, gg * 256:(gg + 1) * 256],
                               True, True, [t_xstok, t_xw], [tps])
                        tt("dve", Ss[:, l], Ss[:, l], bc(eGe_all[0:64, bi, :].unsqueeze(2), [64, 8, 64]), ALU.mult,
                           [t_Ss[l], t_eGe], [t_Ss[l]])
                        tt("dve", Ss[:, l], Ss[:, l], ps_[:64, :].rearrange("p (h v) -> p h v", h=8), ALU.add,
                           [t_Ss[l], tps], [t_Ss[l]])
                        cp("act", Ssb[:, l], Ss[:, l], [t_Ss[l]], [t_Ss[l]])
                        if last_group and bi == nblk - 1:
                            P.dma("sp", ssp[l], Ss[:, l], r=[t_Ss[l]], key="pfinal", final=True)
                    else:
                        pass
                chk("ssd")
                ph("ret_proj")
                W, tw = wload(l, "rqk")
                W2_, tw2 = wload(l, "rsw", live=1)
                rt = lnrt[0:64, :].rearrange("p (a h c) -> p a h c", a=2, h=4)
                for which, dstb in ((0, qTb), (1, kTb)):
                    P.dma("sp", lnrt[0:64, :], cst["rtab"][gi, which].rearrange("p a h c -> p (a h c)"), w=[t_rtab], key="lnbc")
                    for h in range(4):
                        pt, tp = proj_feat(W, tw, which * 256 + h * 64, 64, xT, t_xT, 8, ncols)
                        tt("dve", qTf[:, h, :ncols], pt[:64, :ncols], rt[:, 0, h, :ncols], ALU.mult, [tp, t_rtab], [t_qk, t_lhd])
                        pt, tp = proj_feat(W2_, tw2, which * 256 + h * 64, 64, xT, t_xT, 8, ncols)
                        tt("dve", kTf[:, h, :ncols], pt[:64, :ncols], rt[:, 1, h, :ncols], ALU.mult, [tp, t_rtab], [t_qk, t_dec])
                        tt("dve", dstb[:, h, :ncols], qTf[:, h, :ncols], kTf[:, h, :ncols], ALU.add, [t_qk], [t_qkb])
                ph("ret_blk")
                Wv, twv = wload(l, "rv")
                Wg_, twg = wload(l, "rg", live=1)
                def ret_pre(bi):
                    kind, nt, c0, pos0 = g[bi]
                    i2 = bi % 2
                    pt, tp = proj_tok(Wg_, twg, 0, 512, c0, nt)
                    pv_, tpv_ = proj_tok(Wv, twv, 0, 512, c0, nt)
                    lp = lin_pre(kind, nt, c0, i2)
                    next(lp)
                    yield
                    act(junkB[:nt, :], pt[:nt, :], AF.Exp, [tp], [t_junkB], scale=-1.0)
                    act(junkB[:nt, :], junkB[:nt, :], AF.Ln, [t_junkB], [t_junkB], bias=1.0)
                    act(junkB[:nt, :], junkB[:nt, :], AF.Exp, [t_junkB], [t_junkB], scale=-1.0)
                    cp("act", vtok2[:nt, i2, :], pv_[:nt, :], [tpv_], [t_vtok2[i2]])
                    tt("dve", srt2[:nt, i2, :], pt[:nt, :], junkB[:nt, :], ALU.mult, [tp, t_junkB], [t_srt2[i2]])
                    for _ in lp:
                        pass
                for _ in ret_pre(0):
                    pass
                for bi, (kind, nt, c0, pos0) in enumerate(g):
                    i2 = bi % 2
                    egp = regend[:, KIND[kind], :]
                    egs = bc(regend[:, 2, :].unsqueeze(1), [64, 16, 4])
                    lastp = last_group and bi == nblk - 1
                    hook = (lambda bi=bi: ret_pre(bi + 1)) if bi + 1 < nblk else None
                    lin_block(l, kind, nt, c0, Sr, Srb, t_Sr[l], egp, egs, 8, None, sr_in, srp, srs, lastp,
                              vtok2[:, i2, :], t_vtok2[i2], srt2[:, i2, :], t_srt2[i2], i2, hook)
                chk("ret")
                ph("merge")
                macc = big[:, 0:8192].bitcast(F32).rearrange("p (c t) -> p c t", c=8)
                for q in range(2):
                    for br in range(3):
                        Wg2, twg2 = wload(l, "mg%d%d" % (q, br))
                        Wo2, two2 = wload(l, "mo%d%d" % (q, br), live=1)
                        for cc in range(4):
                            dc = q * 4 + cc
                            pg_, tpg_ = proj_feat(Wg2, twg2, cc * 128, 128, xT, t_xT, 8, ncols)
                            jk, tjk = (junkA, t_junkA) if cc % 2 == 0 else (junkB, t_junkB)
                            act(jk[:, :ncols], pg_[:, :ncols], AF.Sigmoid, [tpg_], [tjk])
                            po_, tpo_ = proj_feat(Wo2, two2, cc * 128, 128, yT, t_yT[br], 4, ncols, koff=br * 4)
                            if br == 0:
                                tt("dve", macc[:, dc, :ncols], jk[:, :ncols], po_[:, :ncols], ALU.mult, [tjk, tpo_], [t_xraw])
                            else:
                                tca = t_cacc2[cc % 2]
                                tt("dve", cacc[:, cc % 2, :ncols], jk[:, :ncols], po_[:, :ncols], ALU.mult, [tjk, tpo_], [tca])
                                if br == 1:
                                    tt("dve", macc[:, dc, :ncols], macc[:, dc, :ncols], cacc[:, cc % 2, :ncols], ALU.add,
                                       [t_xraw, tca], [t_xraw])
                                else:
                                    tt("dve", mT[:, dc, :ncols], macc[:, dc, :ncols], cacc[:, cc % 2, :ncols], ALU.add,
                                       [t_xraw, tca], [t_mT])
                chk("merge")
                ph("wo_ln1")
                Wo0 = wload(l, "wo0")
                Wo1 = wload(l, "wo1", live=1)
                for bi, (kind, nt, c0, pos0) in enumerate(g):
                    for hf, (Wq, twq) in enumerate((Wo0, Wo1)):
                        pt, tp = PF()
                        for k in range(8):
                            mm(pt[:nt, :], mT[:, k, c0:c0 + nt], Wq[:, k, :], k == 0, k == 7, [t_mT, twq], [tp])
                        stt(xtok[:nt, bi, hf * 512:(hf + 1) * 512], xtok[:nt, bi, hf * 512:(hf + 1) * 512], ALPHA, pt[:nt, :],
                            ALU.mult, ALU.add, [t_xtok[bi], tp], [t_xtok[bi]])
                layernorm(g, 1 + 2 * l)
                chk("ln1")
                ph("ff1")
                for i in range(8):
                    W1, tw1 = wload(l, "f1%d" % i)
                    for cc in range(4):
                        f = i * 4 + cc
                        pt, tp = proj_feat(W1, tw1, cc * 128, 128, xT, t_xT, 8, ncols)
                        jk, tjk = (junkA, t_junkA) if f % 2 == 0 else (junkB, t_junkB)
                        ts("dve", jk[:, :ncols], pt[:, :ncols], b1col_t[:, l, f:f + 1], 0.0, ALU.add, ALU.max, [tp, tc_], [tjk])
                        act(hid[:, f, :ncols], jk[:, :ncols], AF.Square, [tjk], [t_hid[i]])
                ph("ff2_ln2")
                for half in range(2):
                    accs = [PF() for _ in range(nblk)]
                    for fg in range(4):
                        W2f, tw2f = wload(l, "f2%d%d" % (half, fg))
                        for bi, (kind, nt, c0, pos0) in enumerate(g):
                            pt, tp = accs[bi]
                            for k in range(8):
                                f = fg * 8 + k
                                mm(pt[:nt, :], hid[:, f, c0:c0 + nt], W2f[:, k, :], fg == 0 and k == 0, False,
                                   [t_hid[f // 4], tw2f], [tp])
                            if fg == 3:
                                o_ = 256 + half * 512
                                mm(pt[:nt, :], ones_b[0:33, :nt], rowhl[0:33, l, o_:o_ + 512], False, True, [tc_], [tp])
                    for bi, (kind, nt, c0, pos0) in enumerate(g):
                        pt, tp = accs[bi]
                        stt(xtok[:nt, bi, half * 512:(half + 1) * 512], xtok[:nt, bi, half * 512:(half + 1) * 512], ALPHA,
                            pt[:nt, :], ALU.mult, ALU.add, [t_xtok[bi], tp], [t_xtok[bi]])
                layernorm(g, 2 + 2 * l)
            for bi, (kind, nt, c0, pos0) in enumerate(g):
                if kind == "B":
                    P.dma("sp", yp[pos0 - 16:pos0 - 16 + nt, :], xtok[:nt, bi, :], r=[t_xtok[bi]], key=("yout", bi), final=True)
                elif kind == "S":
                    P.dma("sp", ys, xtok[:nt, bi, :], r=[t_xtok[bi]], key=("yout", bi), final=True)


    try:
        main()
    except _Stop:
        pass
    P.finish()
    for cm in reversed(cms):
        cm.__exit__(None, None, None)
    global LAST_PHASES
    LAST_PHASES = P.phases
    return nc


LAST_PHASES = None
_CACHE = {}


def kernel(x_prompt, x_sample, state_gla, state_ssm, state_conv, state_ret, meta_tokens,
           ln_in_w, ln_in_b, w_in, w_gla_a2, b_gla_a, w_gla_norm, conv_w, conv_b, dt_bias,
           a_log, d_skip, w_ssm_norm, w_gla_out, w_ssm_out, w_ret_out, w_o, ln1_w, ln1_b,
           w_ff1, b_ff1, w_ff2, b_ff2, ln2_w, ln2_b):
    f = lambda a: np.ascontiguousarray(np.asarray(a, dtype=np.float32))
    (x_prompt, x_sample, state_gla, state_ssm, state_conv, state_ret, meta_tokens, ln_in_w, ln_in_b, w_in,
     w_gla_a2, b_gla_a, w_gla_norm, conv_w, conv_b, dt_bias, a_log, d_skip, w_ssm_norm, w_gla_out, w_ssm_out,
     w_ret_out, w_o, ln1_w, ln1_b, w_ff1, b_ff1, w_ff2, b_ff2, ln2_w, ln2_b) = [f(a) for a in (
        x_prompt, x_sample, state_gla, state_ssm, state_conv, state_ret, meta_tokens, ln_in_w, ln_in_b, w_in,
        w_gla_a2, b_gla_a, w_gla_norm, conv_w, conv_b, dt_bias, a_log, d_skip, w_ssm_norm, w_gla_out, w_ssm_out,
        w_ret_out, w_o, ln1_w, ln1_b, w_ff1, b_ff1, w_ff2, b_ff2, ln2_w, ln2_b)]
    if "nc" not in _CACHE:
        _CACHE["nc"] = build_program()
        _CACHE["consts"] = build_consts()
    nc = _CACHE["nc"]
    consts = _CACHE["consts"]
    wstream = build_wstream(w_in, w_gla_out, w_ssm_out, w_ret_out, w_o, w_ff1, w_ff2)
    lnw = [ln_in_w, ln1_w[0], ln2_w[0], ln1_w[1], ln2_w[1]]
    lnb = [ln_in_b, ln1_b[0], ln2_b[0], ln1_b[1], ln2_b[1]]
    lnbc = np.empty((5, 2, 128, D), np.float32)
    lncol = np.empty((128, 5, 2, 8), np.float32)
    for i in range(5):
        lnbc[i, 0] = lnw[i][None, :]
        lnbc[i, 1] = lnb[i][None, :]
        lncol[:, i, 0, :] = lnw[i].reshape(8, 128).T
        lncol[:, i, 1, :] = lnb[i].reshape(8, 128).T
    small = np.zeros((128, L, 80), np.float32)
    rowp = np.zeros((1, L, 1280), np.float32)
    for l in range(L):
        small[:, l, 0:8] = a_log[l][None, :]
        small[:, l, 8:16] = dt_bias[l][None, :]
        small[:, l, 16:24] = d_skip[l][None, :]
        for c in range(8):
            if c < 4:
                ch = np.arange(c * 128, (c + 1) * 128)
            else:
                ch = np.arange(512 + (c - 4) * 64, 512 + (c - 3) * 64)
            small[:len(ch), l, 24 + c] = conv_b[l][ch]
            for i in range(4):
                small[:len(ch), l, 32 + c * 4 + i] = conv_w[l][i, ch]
        small[:, l, 64:68] = w_ssm_norm[l].reshape(4, 128).T
        small[:, l, 68] = w_gla_norm[l]
        rowp[0, l, 0:256] = b_gla_a[l]
        rowp[0, l, 256:1280] = b_ff2[l]
    wa2 = np.ascontiguousarray(w_gla_a2.transpose(1, 0, 2))
    b1col = np.ascontiguousarray(b_ff1.reshape(L, 32, 128).transpose(2, 0, 1))
    in_maps = []
    for c in range(NCORES):
        bs = slice(c * NSEQ, (c + 1) * NSEQ)
        def lin_state(s):
            s = s[:, bs].reshape(L, NSEQ, 2, 2, 64, 128)
            return np.ascontiguousarray(s.transpose(0, 2, 4, 1, 3, 5))
        s = state_ssm[:, bs].reshape(L, NSEQ, 2, 4, 64, 64)
        ss_in = np.ascontiguousarray(s.transpose(0, 2, 4, 1, 3, 5))
        sc_in = np.zeros((L, 128, 8, NSEQ, 3), np.float32)
        sc = state_conv[:, bs]
        for cc in range(8):
            if cc < 4:
                ch = np.arange(cc * 128, (cc + 1) * 128)
            else:
                ch = np.arange(512 + (cc - 4) * 64, 512 + (cc - 3) * 64)
            sc_in[:, :len(ch), cc] = sc[:, :, :, ch].transpose(0, 3, 1, 2)
        m = {"xp": x_prompt[c], "xs": x_sample[bs].reshape(NSEQ * DSEQ, D), "meta": meta_tokens,
             "wst": wstream, "lnbc": lnbc, "lncol": lncol, "small": small, "rowp": rowp.reshape(1, L * 1280), "wa2": wa2, "b1col": b1col,
             "sg_in": lin_state(state_gla), "sr_in": lin_state(state_ret), "ss_in": ss_in, "sc_in": sc_in}
        for n in CONST_NAMES:
            m["c_" + n] = consts[n]
        in_maps.append(m)
    if DBG_CORES:
        res = run_bass_kernel_spmd(nc, in_maps[:DBG_CORES], core_ids=list(range(DBG_CORES)))
        R = [res.results[c % DBG_CORES] for c in range(NCORES)]
    else:
        res = run_bass_kernel_spmd(nc, in_maps, core_ids=list(range(NCORES)))
        R = res.results
    y_prompt = np.stack([R[c]["yp"] for c in range(NCORES)])
    y_sample = np.concatenate([R[c]["ys"].reshape(NSEQ, DSEQ, D) for c in range(NCORES)], axis=0)

    def lin_p(name):
        o = np.stack([R[c][name] for c in range(NCORES)], axis=1)
        return np.ascontiguousarray(o.transpose(0, 1, 3, 2, 4))

    def lin_s(name):
        o = np.concatenate([R[c][name].transpose(0, 3, 1, 4, 2, 5).reshape(L, NSEQ, 4, 64, 128)
                            for c in range(NCORES)], axis=1)
        return np.ascontiguousarray(o)
    ssm_p = np.stack([R[c]["ssp"] for c in range(NCORES)], axis=1)
    ssm_p = np.ascontiguousarray(ssm_p.transpose(0, 1, 3, 2, 4))
    ssm_s = np.concatenate([R[c]["sss"].transpose(0, 3, 1, 4, 2, 5).reshape(L, NSEQ, 8, 64, 64)
                            for c in range(NCORES)], axis=1)
    conv_p = np.stack([R[c]["scp"] for c in range(NCORES)], axis=1)
    conv_s = np.concatenate([R[c]["scs"].transpose(0, 2, 1, 3) for c in range(NCORES)], axis=1)
    return (y_prompt.astype(np.float32), y_sample.astype(np.float32),
            lin_p("sgp"), lin_s("sgs"), ssm_p, np.ascontiguousarray(ssm_s),
            np.ascontiguousarray(conv_p), np.ascontiguousarray(conv_s), lin_p("srp"), lin_s("srs"))
```
